# Optimizing a Trainium2 kernel written in Bass

```python
import jax, jax.numpy as jnp
from jax import lax
import numpy as np

D_MODEL = 2048
BATCH = 4
SEQ = 4096
DEPTH = 1

PLE_DIM = 256
ATT_HEADS = 16
ATT_HD = 64
ATT_W = ATT_HEADS * ATT_HD
ROT_DIM = ATT_HD // 4
ROPE_THETA = 500000.0
DILATION_CFG = ((128, 1), (512, 4), (2048, 16))
HG_HEADS = 8
HG_DK = 128
HG_DV = 128
HG_W = HG_HEADS * HG_DV
HG_F = HG_HEADS * HG_DK
HG_CHUNK = 64
MIX_W = ATT_W + HG_W
SPLITS = (ATT_W, ATT_W, ATT_W, ATT_W,
          HG_F, HG_F, HG_F, HG_W, HG_W)
IN_W = sum(SPLITS)
SPLIT_IDX = [int(i) for i in np.cumsum(SPLITS)[:-1]]
EPS = 1e-6
NEG = -1e30

kernel_name = 'hybrid_dilated_attn_hgrn2_parallel_block'


def rmsnorm(x, g):
    xf = x.astype(jnp.float32)
    y = xf * lax.rsqrt(jnp.mean(xf * xf, axis=-1, keepdims=True) + EPS)
    return (y * g.astype(jnp.float32)).astype(x.dtype)


def rope_partial(x, pos):
    inv = jnp.power(ROPE_THETA, -jnp.arange(0, ROT_DIM, 2, dtype=jnp.float32) / ROT_DIM)
    ang = pos.astype(jnp.float32)[..., None] * inv
    c = jnp.cos(ang)[:, :, None, :]
    s = jnp.sin(ang)[:, :, None, :]
    x1 = x[..., :ROT_DIM // 2]
    x2 = x[..., ROT_DIM // 2:ROT_DIM]
    return jnp.concatenate([x1 * c - x2 * s, x2 * c + x1 * s, x[..., ROT_DIM:]], axis=-1)


def dilated_band_attention(q, k, v, window, dilation):
    B, S, H, E = q.shape
    half = (window // 2) // dilation
    blk = half
    L = S // dilation
    nb = -(-L // blk)
    Lp = nb * blk

    def strided(t):
        t = t.reshape(B, L, dilation, H, E).transpose(0, 2, 3, 1, 4)
        return jnp.pad(t, ((0, 0), (0, 0), (0, 0), (0, Lp - L), (0, 0)))

    def neighbours(t):
        tp = jnp.pad(t, ((0, 0), (0, 0), (0, 0), (blk, blk), (0, 0)))
        parts = [tp[:, :, :, j * blk:j * blk + Lp].reshape(B, dilation, H, nb, blk, E) for j in range(3)]
        return jnp.concatenate(parts, axis=4)

    qb = strided(q).reshape(B, dilation, H, nb, blk, E)
    kb = neighbours(strided(k))
    vb = neighbours(strided(v))
    qa = jnp.arange(nb)[:, None] * blk + jnp.arange(blk)[None, :]
    ka = jnp.arange(nb)[:, None] * blk - blk + jnp.arange(3 * blk)[None, :]
    rel = ka[:, None, :] - qa[:, :, None]
    valid = (jnp.abs(rel) <= half) & (ka[:, None, :] >= 0) & (ka[:, None, :] < L)
    s = jnp.einsum('bdhnqe,bdhnke->bdhnqk', qb, kb)
    s = jnp.where(valid, s, NEG)
    m = jnp.max(s, axis=-1, keepdims=True)
    pe = jnp.exp(s - m)
    den = jnp.sum(pe, axis=-1)
    o = jnp.einsum('bdhnqk,bdhnke->bdhnqe', pe, vb) / den[..., None]
    lse = m[..., 0] + jnp.log(den)
    o = o.reshape(B, dilation, H, Lp, E)[:, :, :, :L].transpose(0, 3, 1, 2, 4).reshape(B, S, H, E)
    lse = lse.reshape(B, dilation, H, Lp)[:, :, :, :L].transpose(0, 3, 1, 2).reshape(B, S, H)
    return o, lse


def hgrn2_chunk_scan(q, k, v, g):
    B, S, H, K = q.shape
    V = v.shape[-1]
    C = HG_CHUNK
    N = S // C

    def chunks(t):
        return t.reshape(B, N, C, H, t.shape[-1]).transpose(1, 0, 3, 2, 4)

    tri = jnp.tril(jnp.ones((C, C), dtype=bool))

    def step(state, inp):
        qc, kc, vc, gc = inp
        b = jnp.cumsum(gc, axis=2)
        diff = b[:, :, :, None, :] - b[:, :, None, :, :]
        decay = jnp.exp(jnp.where(tri[:, :, None], diff, -jnp.inf))
        att = jnp.einsum('bhtk,bhsk,bhtsk->bhts', qc, kc, decay)
        o = (jnp.einsum('bhts,bhsv->bhtv', att, vc)
             + jnp.einsum('bhtk,bhkv->bhtv', qc * jnp.exp(b), state))
        b_last = b[:, :, -1:, :]
        state = (jnp.exp(b_last[:, :, 0, :])[..., None] * state
                 + jnp.einsum('bhsk,bhsv->bhkv', kc * jnp.exp(b_last - b), vc))
        return state, o

    s0 = jnp.zeros((B, H, K, V), dtype=q.dtype)
    _, o = lax.scan(step, s0, (chunks(q), chunks(k), chunks(v), chunks(g)))
    return o.transpose(1, 0, 3, 2, 4).reshape(B, S, H, V)


def hgrn2_direction(qh, vh, f_logit, lb):
    f = lb + (1.0 - lb) * jax.nn.sigmoid(f_logit)
    return hgrn2_chunk_scan(qh, 1.0 - f, vh, jnp.log(f))


def hybrid_layer(h, p_l, pos, w_in, w_out, g_pre, g_post, g_hg, lb_f, lb_b, w_pg, w_pp, g_ple):
    B, S, _ = h.shape
    f32 = jnp.float32
    u = rmsnorm(h, g_pre)
    z = u @ w_in
    aq, ak, av, ag, hq, hf_f, hf_b, hv, hg = jnp.split(z, SPLIT_IDX, axis=-1)

    def heads(t, n):
        return t.reshape(B, S, n, -1).astype(f32)

    q = rope_partial(heads(aq, ATT_HEADS), pos) * (ATT_HD ** -0.5)
    k = rope_partial(heads(ak, ATT_HEADS), pos)
    v = heads(av, ATT_HEADS)
    res = [dilated_band_attention(q, k, v, w, d) for (w, d) in DILATION_CFG]
    outs = jnp.stack([r[0] for r in res])
    lses = jnp.stack([r[1] for r in res])
    alpha = jax.nn.softmax(lses, axis=0)
    o_att = jnp.einsum('gbsh,gbshe->bshe', alpha, outs).reshape(B, S, ATT_W)
    y_att = o_att * jax.nn.silu(ag.astype(f32))

    qh = jax.nn.silu(heads(hq, HG_HEADS))
    vh = heads(hv, HG_HEADS)
    o_fw = hgrn2_direction(qh, vh, heads(hf_f, HG_HEADS), lb_f)
    o_bw = jnp.flip(hgrn2_direction(jnp.flip(qh, 1), jnp.flip(vh, 1),
                                    jnp.flip(heads(hf_b, HG_HEADS), 1), lb_b), 1)
    o_hg = rmsnorm(o_fw + o_bw, g_hg).reshape(B, S, HG_W)
    y_hg = o_hg * jax.nn.silu(hg.astype(f32))

    y = jnp.concatenate([y_att, y_hg], axis=-1).astype(h.dtype) @ w_out
    h = h + rmsnorm(y, g_post)

    gate = jax.nn.sigmoid((h @ w_pg).astype(f32))
    e = (p_l @ w_pp).astype(f32)
    h = h + rmsnorm((gate * e).astype(h.dtype), g_ple)
    return h


def setup_inputs(seed: int = 0) -> dict:
    key = jax.random.key(seed)
    ks = jax.random.split(key, 16)
    nrm = jax.random.normal
    x = nrm(ks[0], (BATCH, SEQ, D_MODEL), jnp.float32)
    p = nrm(ks[1], (DEPTH, BATCH, SEQ, PLE_DIM), jnp.float32)
    positions = (jnp.arange(SEQ, dtype=jnp.int32)[None, :]
                 + jax.random.randint(ks[2], (BATCH, 1), 0, 1024, dtype=jnp.int32))
    w_in = nrm(ks[3], (DEPTH, D_MODEL, IN_W), jnp.float32) * D_MODEL ** -0.5
    w_out = nrm(ks[4], (DEPTH, MIX_W, D_MODEL), jnp.float32) * MIX_W ** -0.5
    g_pre = 1.0 + 0.05 * nrm(ks[5], (DEPTH, D_MODEL), jnp.float32)
    g_post = 1.0 + 0.05 * nrm(ks[6], (DEPTH, D_MODEL), jnp.float32)
    g_hg = 1.0 + 0.05 * nrm(ks[7], (DEPTH, HG_DV), jnp.float32)
    lb_fwd = 0.1 * nrm(ks[8], (DEPTH + 1, HG_F), jnp.float32)
    lb_bwd = 0.1 * nrm(ks[9], (DEPTH + 1, HG_F), jnp.float32)
    w_pg = nrm(ks[10], (DEPTH, D_MODEL, D_MODEL), jnp.float32) * D_MODEL ** -0.5
    w_pp = nrm(ks[11], (DEPTH, PLE_DIM, D_MODEL), jnp.float32) * PLE_DIM ** -0.5
    g_ple = 1.0 + 0.05 * nrm(ks[12], (DEPTH, D_MODEL), jnp.float32)
    return {'x': x, 'p': p, 'positions': positions, 'w_in': w_in, 'w_out': w_out,
            'g_pre': g_pre, 'g_post': g_post, 'g_hg': g_hg, 'lb_fwd': lb_fwd, 'lb_bwd': lb_bwd,
            'w_pg': w_pg, 'w_pp': w_pp, 'g_ple': g_ple}


def reference(x, p, positions, w_in, w_out, g_pre, g_post, g_hg, lb_fwd, lb_bwd, w_pg, w_pp, g_ple):
    lb_f_all = jnp.cumsum(jax.nn.softmax(lb_fwd.astype(jnp.float32), axis=0), axis=0)
    lb_b_all = jnp.cumsum(jax.nn.softmax(lb_bwd.astype(jnp.float32), axis=0), axis=0)
    h = x
    for i in range(DEPTH):
        h = hybrid_layer(h, p[i], positions, w_in[i], w_out[i], g_pre[i], g_post[i], g_hg[i],
                         lb_f_all[i].reshape(HG_HEADS, HG_DK), lb_b_all[i].reshape(HG_HEADS, HG_DK),
                         w_pg[i], w_pp[i], g_ple[i])
    return h
```

```python
import contextlib
import math
import numpy as np
import concourse.bass as bass
import concourse.mybir as mybir
from concourse.bass_utils import run_bass_kernel_spmd

F32 = mybir.dt.float32
BF16 = mybir.dt.bfloat16
I32 = mybir.dt.int32
AF = mybir.ActivationFunctionType
OP = mybir.AluOpType

EPS = 1e-6
NB_W = 72
C_ID, C_RT, C_M12, C_HMF, C_HMB, C_RST, C_INV, C_SGN, C_ONE = 0, 128, 256, 768, 896, 1024, 1536, 1537, 1538
NCST = 1666
NEGBIG = -30000.0
B_ID, B_RT, B_M12, B_HMF, B_HMB, B_ONE = 0, 128, 256, 768, 896, 1024
NCB = 1152


class Sched:
    ENG = ("pe", "act", "dve", "pool", "sp")

    def __init__(self, nc, stack):
        self.nc = nc
        self.stack = stack
        self.ops = []
        self.emitted = 0
        self.last_w = {}
        self.readers = {}
        self.esem = {e: stack.enter_context(nc.semaphore("s_" + e)) for e in self.ENG}
        self.ecnt = {e: 0 for e in self.ENG}
        self.dsem = {}
        self.dcnt = {}
        self.waited = {e: {} for e in self.ENG}
        self.enabled = True
        self.nemit = 0
        self.phase_dmas = []
        self.stop_n = None

    def add(self, eng, fn, r=(), w=(), dma=None):
        if not self.enabled:
            return None
        i = len(self.ops)
        deps = set()
        for k in r:
            j = self.last_w.get(k)
            if j is not None:
                deps.add(j)
        for k in w:
            j = self.last_w.get(k)
            if j is not None:
                deps.add(j)
            deps.update(self.readers.get(k, ()))
        for k in r:
            self.readers.setdefault(k, []).append(i)
        for k in w:
            self.last_w[k] = i
            self.readers[k] = []
        if dma is not None:
            if dma not in self.dsem:
                self.dsem[dma] = self.stack.enter_context(self.nc.semaphore("d%d" % len(self.dsem)))
                self.dcnt[dma] = 0
            self.dcnt[dma] += 16
            sig = (self.dsem[dma], self.dcnt[dma], 16, None)
            self.phase_dmas.append(i)
        else:
            self.ecnt[eng] += 1
            sig = (self.esem[eng], self.ecnt[eng], 1, eng)
        self.ops.append((eng, fn, deps, sig))
        return i

    def mark(self, tag):
        if self.stop_n == tag:
            self.enabled = False

    def emit(self):
        nc = self.nc
        if self.phase_dmas and self.enabled:
            self.ecnt["sp"] += 1
            self.ops.append(("sp", lambda e: e.nop(), set(self.phase_dmas), (self.esem["sp"], self.ecnt["sp"], 1, "sp")))
        self.phase_dmas = []
        new = self.ops[self.emitted:]
        self.emitted = len(self.ops)
        self.nemit += 1
        if isinstance(self.stop_n, int) and self.nemit >= self.stop_n:
            self.enabled = False
        if not new:
            return
        per = {e: [] for e in self.ENG}
        for op in new:
            per[op[0]].append(op)
        ops = self.ops
        waited = self.waited
        hw = {"pe": "tensor", "act": "scalar", "dve": "vector", "pool": "gpsimd", "sp": "sync"}
        with nc.Block() as blk:
            for e in self.ENG:
                lst = per[e]
                if not lst:
                    continue

                def body(engine, lst=lst, e=e):
                    wd = waited[e]
                    for (_, fn, deps, sig) in lst:
                        need = {}
                        for j in deps:
                            s, v, _, pe_ = ops[j][3]
                            if pe_ == "pe" and e == "pe":
                                continue
                            key = id(s)
                            if need.get(key, (None, 0))[1] < v:
                                need[key] = (s, v)
                        for key, (s, v) in need.items():
                            if wd.get(key, 0) < v:
                                engine.wait_ge(s, v)
                                wd[key] = v
                        ins = fn(engine)
                        ins.then_inc(sig[0], sig[2])

                getattr(blk, hw[e])(body)


def build(debug=False, stop=None):
    nc = bass.Bass("TRN2", target_bir_lowering=False)
    dt_in = lambda name, shape, dt=F32: nc.dram_tensor(name, shape, dt, kind="ExternalInput").ap()
    x_d = dt_in("x", [4096, 2048])
    p_d = dt_in("p", [2048, 256])
    pos_d = dt_in("pos", [128, 3072], I32)
    win_d = dt_in("w_in", [NB_W, 128, 2048])
    wout_d = dt_in("w_out", [2048, 2048])
    wpg_d = dt_in("w_pg", [2048, 2048])
    wpp_d = dt_in("w_pp", [256, 2048])
    gpre_d = dt_in("g_pre", [128, 2048])
    gpost_d = dt_in("g_post", [128, 2048])
    gple_d = dt_in("g_ple", [128, 2048])
    ghg_d = dt_in("g_hg", [128, 1])
    lb_d = dt_in("lb", [128, 32])
    cst_d = dt_in("cst", [128, NCST])
    out_d = nc.dram_tensor("out", [2048, 2048], F32, kind="ExternalOutput").ap()
    if debug:
        dbg_d = nc.dram_tensor("dbg", [128, 16 * 2048], BF16, kind="ExternalOutput").ap()

    _cnt = [0]

    def uniq(name):
        _cnt[0] += 1
        return "%s_%d" % (name, _cnt[0])

    st = contextlib.ExitStack()
    with st:
        S = Sched(nc, st)
        S.stop_n = stop
        sb = lambda name, shape, dt: st.enter_context(nc.sbuf_tensor(name, shape, dt))
        bufA = sb("bufA", [128, 16 * 2048], BF16)
        bufBC = sb("bufBC", [128, 16 * 2048], BF16)
        cstf = sb("cstf", [128, 514], F32)
        cstb = sb("cstb", [128, NCB], BF16)
        lbv = sb("lbv", [128, 32], F32)
        ghg = sb("ghg", [128, 1], F32)
        RST = cstf[:, 0:512]
        INV = cstf[:, 512:513]
        SGN = cstf[:, 513:514]
        IDB = cstb[:, B_ID:B_ID + 128]
        RTB = cstb[:, B_RT:B_RT + 128]
        M12 = cstb[:, B_M12:B_M12 + 512]
        HMF = cstb[:, B_HMF:B_HMF + 128]
        HMB = cstb[:, B_HMB:B_HMB + 128]
        ONEB = cstb[:, B_ONE:B_ONE + 128]

        A3 = bufA[:].rearrange("p (k t) -> p k t", k=16)
        BC_lo = bufBC[:, 0:16384].rearrange("p (k t) -> p k t", k=16)
        BC_hi = bufBC[:, 16384:32768].rearrange("p (k t) -> p k t", k=16)
        Y3 = bufBC[:].rearrange("p (k t) -> p k t", k=16)

        def uT3(t0):
            if t0 >= 2048:
                return A3, t0 - 2048
            if t0 >= 1024:
                return BC_hi, t0 - 1024
            return BC_lo, t0

        def uT(kc, t0, n):
            v, o = uT3(t0)
            return v[:, kc, o:o + n]

        ukeys = lambda t0, n: [("uT", t) for t in range(t0 // 128, (t0 + n) // 128)]

        def act(out, in_, func, r, w, scale=1.0, bias=0.0):
            S.add("act", lambda e: e.activation(out, in_, func, bias=bias, scale=scale), r, w)

        def tt(eng, out, a, b, op, r, w):
            S.add(eng, lambda e: e.tensor_tensor(out, a, b, op), r, w)

        def ts(eng, out, a, s1, s2, op0, op1, r, w):
            if s2 is None:
                S.add(eng, lambda e: e.tensor_scalar(out, a, s1, None, op0), r, w)
            else:
                S.add(eng, lambda e: e.tensor_scalar(out, a, s1, s2, op0, op1), r, w)

        def stt(out, a, sc, b, op0, op1, r, w, accum=None):
            if accum is None:
                S.add("dve", lambda e: e.scalar_tensor_tensor(out, a, sc, b, op0, op1), r, w)
            else:
                S.add("dve", lambda e: e.scalar_tensor_tensor(out, a, sc, b, op0, op1, accum_out=accum), r, w)

        def cp(eng, out, in_, r, w):
            if eng == "act":
                S.add("act", lambda e: e.activation(out, in_, AF.Copy), r, w)
            else:
                S.add(eng, lambda e: e.tensor_copy(out, in_), r, w)

        def recip(out, in_, r, w):
            S.add("dve", lambda e: e.reciprocal(out, in_), r, w)

        def dma(out, in_, r, w, key):
            S.add("sp", lambda e: e.dma_start(out=out, in_=in_), r, w, dma=key)

        def mm(out, lhsT, rhs, start, stop):
            return lambda e: e.matmul(out, lhsT, rhs, start=start, stop=stop)

        def mmgroup(fns, r, w):
            def f(e):
                ins = None
                for g in fns:
                    ins = g(e)
                return ins
            S.add("pe", f, r, w)

        def sigmoid_chain(dst, src, r_src, tmp, tk):
            act(tmp, src, AF.Exp, r_src, [tk], scale=-1.0)
            ts("dve", tmp, tmp, 1.0, None, OP.add, None, [tk], [tk])
            recip(dst, tmp, [tk], [tk] if dst is tmp else [tk + ("o",)])

        with nc.sbuf_tensor("cst0", [128, NCST], F32) as cst0, nc.sbuf_tensor("lb0", [128, 32], F32) as lb0:
            dma(cst0[:], cst_d[:, :], [], ["cst0"], "cst0")
            dma(lb0[:], lb_d[:, :], [], ["lb0"], "lb0")
            dma(ghg[:], ghg_d[:, :], [], ["ghg"], "ghg")
            cp("dve", cstf[:, 0:512], cst0[:, C_RST:C_RST + 512], ["cst0"], ["cstf"])
            cp("dve", cstf[:, 512:514], cst0[:, C_INV:C_INV + 2], ["cst0"], ["cstf2"])
            for (bo, co, n) in ((B_ID, C_ID, 128), (B_RT, C_RT, 128), (B_M12, C_M12, 512),
                                (B_HMF, C_HMF, 128), (B_HMB, C_HMB, 128), (B_ONE, C_ONE, 128)):
                cp("dve", cstb[:, bo:bo + n], cst0[:, co:co + n], ["cst0"], [("cstb", bo)])
            for d in range(2):
                o = d * 16
                tt("dve", lb0[:, o:o + 8], lb0[:, o:o + 8], lb0[:, o + 8:o + 16], OP.subtract, ["lb0"], ["lb0"])
                act(lb0[:, o + 8:o + 16], lb0[:, o:o + 8], AF.Exp, ["lb0"], ["lb0"], scale=-1.0)
                ts("dve", lb0[:, o + 8:o + 16], lb0[:, o + 8:o + 16], 1.0, None, OP.add, None, ["lb0"], ["lb0"])
                recip(lbv[:, o:o + 8], lb0[:, o + 8:o + 16], ["lb0"], ["lbv"])
                ts("dve", lbv[:, o + 8:o + 16], lbv[:, o:o + 8], -1.0, 1.0, OP.mult, OP.add, ["lbv"], ["lbv"])
            S.emit()

        LBI = lambda h: lbv[:, h:h + 1]
        OMI = lambda h: lbv[:, 8 + h:9 + h]
        LBO = lambda h: lbv[:, 16 + h:17 + h]
        OMO = lambda h: lbv[:, 24 + h:25 + h]

        with contextlib.ExitStack() as s1:
            sb1 = lambda name, shape, dt: s1.enter_context(nc.sbuf_tensor(name, shape, dt))
            T32 = sb1("T32", [128, 8, 128], F32)
            dlast = sb1("dlast", [128, 8], F32)
            stage = sb1("stage", [128, 1024], F32)
            wbf = [sb1("wbf%d" % i, [128, 16, 128], BF16) for i in range(2)]

            wseq = []
            for h in range(8):
                wseq += [40 + h, 56 + h]
            for hp in range(8):
                wseq += [hp, 8 + hp, 16 + hp, 24 + hp]
            for h in range(8):
                wseq += [32 + h, 40 + h, 56 + h, 48 + h, 64 + h]
            wstate = {"loaded": 0, "cast": 0, "use": 0}

            def w_load():
                n = wstate["loaded"]
                if n >= 2 * len(wseq):
                    return
                hb = n % 2
                dma(stage[:], win_d[wseq[n // 2]][:, hb * 1024:(hb + 1) * 1024], [], ["stage"], "stage")
                wstate["loaded"] += 1

            def w_cast():
                n = wstate["cast"]
                if n >= 2 * len(wseq):
                    return
                blk, hb = n // 2, n % 2
                cp("pool", wbf[blk % 2][:, hb * 8:(hb + 1) * 8, :].rearrange("p k c -> p (k c)"), stage[:], ["stage"],
                   [("wbf", blk % 2)])
                wstate["cast"] += 1

            def w_upto(nh):
                while wstate["cast"] < nh and wstate["cast"] < 2 * len(wseq):
                    if wstate["loaded"] <= wstate["cast"]:
                        w_load()
                    w_cast()
                if wstate["loaded"] <= wstate["cast"]:
                    w_load()

            def w_next(expect):
                n = wstate["use"]
                assert wseq[n] == expect, (n, wseq[n], expect)
                w_upto(2 * n + 2)
                wstate["use"] += 1
                return wbf[n % 2], ("wbf", n % 2)

            def w_prefetch_cast():
                w_upto(2 * wstate["use"] + 2)

            def proj_fm(wt, wk, t0, ps, pk):
                fns = [mm(ps, wt[:, kc, :], uT(kc, t0, 512), kc == 0, kc == 15) for kc in range(16)]
                mmgroup(fns, [wk] + ukeys(t0, 512), [pk])

            def norm_phase(tiles, gpre, xt, xb, ss, pst, junk):
                for n_, ti in enumerate(tiles):
                    sl = n_ % 2
                    t0 = ti * 128
                    dma(xt[sl][:], x_d[t0:t0 + 128, :], [], [("xt", sl)], ("xt", sl))
                    stt(junk[:], xt[sl][:], 1.0, xt[sl][:], OP.mult, OP.mult, [("xt", sl)], ["junk", ("ss", sl)],
                        accum=ss[:, 4 * sl:4 * sl + 1])
                    ts("dve", ss[:, 4 * sl + 1:4 * sl + 2], ss[:, 4 * sl:4 * sl + 1], 1.0 / 2048, EPS, OP.mult, OP.add,
                       [("ss", sl)], [("ss1", sl)])
                    act(ss[:, 4 * sl + 2:4 * sl + 3], ss[:, 4 * sl + 1:4 * sl + 2], AF.Ln, [("ss1", sl)], [("ss2", sl)])
                    act(ss[:, 4 * sl + 3:4 * sl + 4], ss[:, 4 * sl + 2:4 * sl + 3], AF.Exp, [("ss2", sl)], [("ss3", sl)],
                        scale=-0.5)
                    stt(xb[sl][:], xt[sl][:], ss[:, 4 * sl + 3:4 * sl + 4], gpre[:], OP.mult, OP.mult,
                        [("xt", sl), ("ss3", sl), "gpre"], [("xb", sl)])
                    v3, o = uT3(t0)
                    for hf in range(2):
                        fns = [(lambda e, k=k, hf=hf, sl=sl: e.transpose(pst[hf][:, k, :], xb[sl][:, (hf * 8 + k) * 128:(hf * 8 + k + 1) * 128], IDB))
                               for k in range(8)]
                        mmgroup(fns, [("xb", sl), ("cstb", B_ID)], [("pst", hf)])
                        cp("act" if hf == 0 else "pool" if False else "act", v3[:, hf * 8:hf * 8 + 8, o:o + 128], pst[hf][:],
                           [("pst", hf)], [("uT", ti)])

            def gate_chain(psf, pkf, lbc, omc, tmp, mode, sq, outs, par=0):
                t0_, t1_, t2_, t3_ = tmp
                act(t0_, psf, AF.Exp, [pkf], [("g0", par)], scale=-1.0)
                act(t0_, t0_, AF.Ln, [("g0", par)], [("g0", par)], bias=1.0)
                act(t0_, t0_, AF.Exp, [("g0", par)], [("g0", par)], scale=-1.0)
                ts("dve", t1_, t0_, omc, lbc, OP.mult, OP.add, [("g0", par), "lbv"], [("g1", par)])
                act(t2_, t1_, AF.Ln, [("g1", par)], [("g2", par)])
                ts("pool", t1_, t1_, -1.0, 1.0, OP.mult, OP.add, [("g1", par)], [("g1", par)])
                S.add("dve", lambda e: e.tensor_tensor_scan(t3_, RST, t2_, 0.0, OP.mult, OP.add),
                      [("g2", par), "cstf"], [("g3", par)])
                act(outs["dec"], t3_[:, 63:512:64], AF.Exp, [("g3", par)], [outs["deck"]])
                if mode == "out":
                    tt("pool", t3_, t3_, t2_, OP.subtract, [("g3", par), ("g2", par)], [("g3", par)])
                if mode in ("halo", "in"):
                    act(t0_, t3_, AF.Exp, [("g3", par)], [("g0", par)], scale=-1.0)
                    tt("pool", outs["KT"], t1_, t0_, OP.mult, [("g1", par), ("g0", par)], [outs["KTk"]])
                    if mode == "in":
                        act(t2_, t3_, AF.Exp, [("g3", par)], [("g2", par)])
                        tt("dve", outs["QT"], sq, t2_, OP.mult, [("g2", par), "sq"], [outs["QTk"]])
                else:
                    act(t0_, t3_, AF.Exp, [("g3", par)], [("g0", par)])
                    tt("pool", outs["KT"], t1_, t0_, OP.mult, [("g1", par), ("g0", par)], [outs["KTk"]])
                    act(t2_, t3_, AF.Exp, [("g3", par)], [("g2", par)], scale=-1.0)
                    tt("dve", outs["QT"], sq, t2_, OP.mult, [("g2", par), "sq"], [outs["QTk"]])

            with contextlib.ExitStack() as s2:
                sb2 = lambda name, shape, dt: s2.enter_context(nc.sbuf_tensor(uniq(name), shape, dt))
                ps2 = lambda name, shape, dt: s2.enter_context(nc.psum_tensor(uniq(name), shape, dt))
                gpre = sb2("gpre", [128, 2048], F32)
                xt = [sb2("xt%d" % i, [128, 2048], F32) for i in range(2)]
                xb = [sb2("xb%d" % i, [128, 2048], BF16) for i in range(2)]
                junk = sb2("junk", [128, 2048], BF16)
                ss = sb2("ss", [128, 8], F32)
                KT = [sb2("hKT%d" % i, [128, 512], BF16) for i in range(2)]
                HK = sb2("HK", [128, 16, 128], BF16)
                dech = sb2("dech", [128, 8, 32], F32)
                vtok = [sb2("hvtok%d" % i, [128, 4, 128], BF16) for i in range(2)]
                gt = [sb2("hgt%d" % i, [128, 512], F32) for i in range(4)]
                pst = [ps2("pst%d" % i, [128, 8, 128], BF16) for i in range(2)]
                psf = [ps2("psf%d" % i, [128, 512], F32) for i in range(2)]
                psv = ps2("psv", [128, 4, 128], F32)
                pskt = ps2("pskt", [128, 8, 128], BF16)
                psa = [ps2("psa%d" % i, [128, 512], F32) for i in range(2)]

                dma(gpre[:], gpre_d[:, :], [], ["gpre"], "gpre")
                w_load()
                norm_phase(list(range(16)), gpre, xt, xb, ss, pst, junk)
                nblk = 0
                for h in range(8):
                    wf, wfk = w_next(40 + h)
                    wv, wvk = w_next(56 + h)
                    for tb in range(4):
                        sl = nblk % 2
                        nblk += 1
                        proj_fm(wf, wfk, tb * 512, psf[sl][:], ("psf", sl))
                        gate_chain(psf[sl][:], ("psf", sl), LBI(h), OMI(h), [g[:] for g in gt], "halo", None,
                                   dict(KT=KT[sl][:], KTk=("hKT", sl), dec=dech[:, h, tb * 8:tb * 8 + 8],
                                        deck=("dech", h, tb)))
                        fns = []
                        for j in range(4):
                            t0 = (tb * 4 + j) * 128
                            fns += [mm(psv[:, j, :], uT(kc, t0, 128), wv[:, kc, :], kc == 0, kc == 15) for kc in range(16)]
                        mmgroup(fns, [wvk] + ukeys(tb * 512, 512), ["psv"])
                        vs = tb % 2
                        cp("act", vtok[vs][:], psv[:], ["psv"], [("hvtok", vs)])
                        if tb == 3:
                            w_prefetch_cast()
                        fns = [(lambda e, j=j, sl=sl: e.transpose(pskt[:, j, :], KT[sl][:, j * 128:(j + 1) * 128], IDB))
                               for j in range(4)]
                        mmgroup(fns, [("hKT", sl), ("cstb", B_ID)], ["pskt"])
                        cp("act", HK[:, tb * 4:tb * 4 + 4, :], pskt[:, 0:4, :], ["pskt"], [("HK", tb)])
                        for c in range(8):
                            j, pb = c // 2, 64 * (c % 2)
                            cg = tb * 8 + c
                            pa = psa[cg % 2]
                            mmgroup([mm(pa[:, 0:128], HK[pb:pb + 64, tb * 4 + j, :], vtok[vs][pb:pb + 64, j, :], True, True)],
                                    [("HK", tb), ("hvtok", vs)], [("psa", cg % 2)])
                            if cg == 0:
                                cp("dve", T32[:, h, :], pa[:, 0:128], [("psa", cg % 2)], [("T32", h)])
                            else:
                                stt(T32[:, h, :], T32[:, h, :], dech[:, h, cg - 1:cg], pa[:, 0:128], OP.mult, OP.add,
                                    [("T32", h), ("psa", cg % 2), ("dech", h, (cg - 1) // 8)], [("T32", h)])
                    cp("dve", dlast[:, h:h + 1], dech[:, h, 31:32], [("dech", h, 3)], [("dlast", h)])
                S.emit()

            with contextlib.ExitStack() as s2:
                sb2 = lambda name, shape, dt: s2.enter_context(nc.sbuf_tensor(uniq(name), shape, dt))
                ps2 = lambda name, shape, dt: s2.enter_context(nc.psum_tensor(uniq(name), shape, dt))
                gpre = sb2("gpre", [128, 2048], F32)
                xt = [sb2("xt%d" % i, [128, 2048], F32) for i in range(2)]
                xb = [sb2("xb%d" % i, [128, 2048], BF16) for i in range(2)]
                junk = sb2("junk", [128, 2048], BF16)
                ss = sb2("ss", [128, 8], F32)
                pst = [ps2("pst%d" % i, [128, 8, 128], BF16) for i in range(2)]
                dma(gpre[:], gpre_d[:, :], [], ["gpre"], "gpre")
                norm_phase(list(range(16, 32)), gpre, xt, xb, ss, pst, junk)
                S.emit()

            with contextlib.ExitStack() as s2:
                sb2 = lambda name, shape, dt: s2.enter_context(nc.sbuf_tensor(uniq(name), shape, dt))
                ps2 = lambda name, shape, dt: s2.enter_context(nc.psum_tensor(uniq(name), shape, dt))
                qP = sb2("qP", [128, 2, 2048], BF16)
                kT = sb2("kT", [128, 3072], BF16)
                vT = sb2("vT", [128, 3072], BF16)
                vt = sb2("vt", [128, 9, 128], BF16)
                acc = sb2("acc", [128, 2, 2048], F32)
                pe_ = [sb2("pe%d" % i, [128, 512], BF16) for i in range(3)]
                rt = [sb2("rt%d" % i, [128, 512], F32) for i in range(2)]
                Ctab = sb2("Ctab", [128, 3072], BF16)
                Stab = sb2("Stab", [128, 3072], BF16)
                psp = [ps2("psp%d" % i, [128, 512], F32) for i in range(2)]
                psr = psp[0]
                pvt = psp[1][:].bitcast(BF16).rearrange("p (s c) -> p s c", s=8)
                pss = [ps2("pss%d" % i, [128, 2, 2, 128], F32) for i in range(4)]
                pso_full = [ps2("pso%d" % i, [128, 4, 128], F32) for i in range(2)]
                pso = [t_[:, 0:2, :] for t_ in pso_full]

                rt_tmp = [acc[:, 0, 0:512], acc[:, 0, 512:1024], acc[:, 0, 1024:1536]]
                posi_t = sb2("posi", [128, 512], I32)

                def build_tables():
                    rt = rt_tmp
                    posi = posi_t[:]
                    for ch in range(6):
                        cs = slice(ch * 512, (ch + 1) * 512)
                        S.add("act", lambda e, cs=cs: e.dma_start(out=posi, in_=pos_d[:, cs]), [], ["posi"], dma="posi")
                        cp("dve", rt[0][:], posi[:], ["posi"], ["rt0"])
                        ts("dve", rt[0][:], rt[0][:], INV, 0.5, OP.mult, OP.add, ["rt0", "cstf2"], ["rt0"])
                        for off, dst in ((0.0, Stab), (0.25, Ctab)):
                            ts("dve", rt[1][:], rt[0][:], off, None, OP.add, None, ["rt0"], ["rt1"])
                            cp("dve", posi[:], rt[1][:], ["rt1"], ["posi"])
                            cp("dve", rt[2][:], posi[:], ["posi"], ["rt2"])
                            tt("dve", rt[1][:], rt[1][:], rt[2][:], OP.subtract, ["rt1", "rt2"], ["rt1"])
                            ts("dve", rt[2][:], rt[1][:], 0.0, None, OP.is_lt, None, ["rt1"], ["rt2"])
                            tt("dve", rt[1][:], rt[1][:], rt[2][:], OP.add, ["rt1", "rt2"], ["rt1"])
                            ts("dve", rt[1][:], rt[1][:], 2.0 * math.pi, -math.pi, OP.mult, OP.add, ["rt1"], ["rt1"])
                            ts("dve", rt[1][:], rt[1][:], -math.pi, math.pi, OP.max, OP.min, ["rt1"], ["rt1"])
                            act(rt[1][:], rt[1][:], AF.Sin, ["rt1"], ["rt1"])
                            if dst is Stab:
                                ts("dve", dst[:, cs], rt[1][:], SGN, None, OP.mult, None, ["rt1", "cstf2"], [("tab", ch)])
                            else:
                                cp("dve", dst[:, cs], rt[1][:], ["rt1"], [("tabc", ch)])

                S.mark('t1')

                def rope(buf, bkey, nb, tabofs):
                    for b_ in range(nb):
                        cs = slice(b_ * 512, (b_ + 1) * 512)
                        tsl = slice(tabofs + b_ * 512, tabofs + (b_ + 1) * 512)
                        tch = (tabofs + b_ * 512) // 512
                        mmgroup([mm(psr[:], RTB, buf[:, cs], True, True)], [(bkey, b_), ("cstb", B_RT)], [("psp", 0)])
                        S.mark('r1')
                        tt("dve", rt[0][:], psr[:], Stab[:, tsl], OP.mult, [("psp", 0), ("tab", tch)], ["rt0"])
                        S.mark('r2')
                        tt("dve", rt[1][:], buf[:, cs], Ctab[:, tsl], OP.mult, [(bkey, b_), ("tabc", tch)], ["rt1"])
                        S.mark('r3')
                        tt("dve", buf[:, cs], rt[0][:], rt[1][:], OP.add, ["rt0", "rt1"], [(bkey, b_)])

                nunit = 0
                S.add("pool", lambda e: e.memset(qP[:], 0.0), [], [("qP0", b_) for b_ in range(4)] + [("qP1", b_) for b_ in range(4)])
                for hp in range(8):
                    npj = 0
                    for (blk, buf, bkey, nb, t00) in ((hp, None, "qP", 4, 2048), (8 + hp, kT, "kT", 6, 1024),
                                                      (16 + hp, vT, "vT", 6, 1024)):
                        wt, wk = w_next(blk)
                        for b_ in range(nb):
                            sl = npj % 2
                            npj += 1
                            cs = slice(b_ * 512, (b_ + 1) * 512)
                            proj_fm(wt, wk, t00 + b_ * 512, psp[sl][:], ("psp", sl))
                            if b_ == 0:
                                w_prefetch_cast()
                            if buf is None:
                                cp("act", qP[0:64, 0, cs], psp[sl][0:64, :], [("psp", sl)], [("qP0", b_)])
                                cp("act", qP[64:128, 1, cs], psp[sl][64:128, :], [("psp", sl)], [("qP1", b_)])
                            else:
                                cp("act", buf[:, cs], psp[sl][:], [("psp", sl)], [(bkey, b_)])
                    S.mark('t1b')
                    if hp == 0:
                        build_tables()
                    rope(qP[:, 0, :], "qP0", 4, 1024)
                    rope(qP[:, 1, :], "qP1", 4, 1024)
                    rope(kT, "kT", 6, 0)
                    S.mark('t2')
                    units = []
                    for d, r0, r1, i0, i1 in ((1, 0, 1, 0, 8), (1, 0, 1, 8, 16), (4, 0, 1, 0, 4), (4, 1, 2, 0, 4), (4, 2, 3, 0, 4),
                                              (4, 3, 4, 0, 4), (16, 0, 4, 0, 1), (16, 4, 8, 0, 1), (16, 8, 12, 0, 1), (16, 12, 16, 0, 1)):
                        L = 4096 // d
                        nq = (L // 2) // 128
                        ntc = i1 - i0 + 1
                        tiles = []
                        for r_ in range(r0, r1):
                            for j in range(i0, i1 + 1):
                                a_s = L // 2 - 64 + 128 * j
                                n = 128 if j < nq else 64
                                tiles.append((r_ + d * a_s - 1024, n))
                        first = True
                        for r_ in range(r0, r1):
                            for i in range(i0, i1):
                                a0 = L // 2 + 128 * i
                                qs = r_ + d * a0 - 2048
                                kts = []
                                for kt_ in range(2):
                                    ti = (r_ - r0) * ntc + (i - i0) + kt_
                                    idx, n = tiles[ti]
                                    kts.append((ti, idx, n))
                                units.append(dict(d=d, qs=qs, kts=kts, tiles=tiles if first else None))
                                first = False

                    def build_vtiles(d, tiles):
                        for g0 in range(0, len(tiles), 8):
                            grp = tiles[g0:g0 + 8]
                            fns = []
                            rk = set()
                            for s_, (idx, n) in enumerate(grp):
                                fns.append((lambda e, s_=s_, idx=idx, n=n, d=d: e.transpose(
                                    pvt[0:n, s_, :], vT[:, idx:idx + d * (n - 1) + 1:d], IDB)))
                                rk.update(("vT", b_) for b_ in range(idx // 512, (idx + d * (n - 1)) // 512 + 1))
                            mmgroup(fns, list(rk) + [("cstb", B_ID)], [("psp", 1)])
                            cp("dve" if (g0 // 8) % 2 else "act", vt[:, g0:g0 + len(grp), :], pvt[:, 0:len(grp), :],
                               [("psp", 1)], [("vt", g0 // 8)])

                    def unit_front(u, un):
                        sp, se = un % 4, un % 3
                        d, qs, kts = u["d"], u["qs"], u["kts"]
                        qsl = slice(qs, qs + d * 127 + 1, d)
                        qk = [(nm, b_) for nm in ("qP0", "qP1") for b_ in range(qs // 512, (qs + d * 127) // 512 + 1)]
                        fns = []
                        kk_ = set()
                        for hh in range(2):
                            for kt_, (ti, idx, n) in enumerate(kts):
                                fns.append(mm(pss[sp][0:n, hh, kt_, :], kT[:, idx:idx + d * (n - 1) + 1:d],
                                              qP[:, hh, qsl], True, True))
                                kk_.update(("kT", b_) for b_ in range(idx // 512, (idx + d * (n - 1)) // 512 + 1))
                        mmgroup(fns, qk + list(kk_), [("pss", sp)])
                        n2 = kts[1][2]
                        e4 = pe_[se][:].rearrange("p (h k q) -> p h k q", h=2, k=2)
                        if n2 == 128:
                            act(e4, pss[sp][:], AF.Exp, [("pss", sp)], [("pe", se)], scale=0.125)
                        else:
                            act(e4[:, :, 0, :], pss[sp][:, :, 0, :], AF.Exp, [("pss", sp)], [("pe", se)], scale=0.125)
                            act(e4[0:n2, :, 1, :], pss[sp][0:n2, :, 1, :], AF.Exp, [("pss", sp)], [("pe", se)], scale=0.125)
                        tt("dve", pe_[se][:], pe_[se][:], M12, OP.mult, [("pe", se), ("cstb", B_M12)], [("pe", se)])

                    def unit_back(u, un):
                        sl, se = un % 2, un % 3
                        d, qs, kts = u["d"], u["qs"], u["kts"]
                        qsl = slice(qs, qs + d * 127 + 1, d)
                        if u["tiles"] is not None:
                            build_vtiles(d, u["tiles"])
                        e4 = pe_[se][:].rearrange("p (h k q) -> p h k q", h=2, k=2)
                        pnum = pso_full[sl][:, 0, :]
                        pden = pso_full[sl][:, 1:3, :]
                        fns = []
                        for hh in range(2):
                            for kt_, (ti, idx, n) in enumerate(kts):
                                fns.append(mm(pnum[hh * 64:(hh + 1) * 64, :], vt[0:n, ti, hh * 64:(hh + 1) * 64],
                                              e4[0:n, hh, kt_, :], kt_ == 0, kt_ == 1))
                        for kt_, (ti, idx, n) in enumerate(kts):
                            fns.append(mm(pden, ONEB[0:n, :], e4[0:n, :, kt_, :], kt_ == 0, kt_ == 1))
                        vk = list({("vt", ti // 8) for (ti, _, _) in kts})
                        mmgroup(fns, [("pe", se), ("cstb", B_ONE)] + vk, [("pso", sl)])
                        ak = [("acc", b_) for b_ in range(qs // 512, (qs + d * 127) // 512 + 1)]
                        if d == 1:
                            cp("dve", acc[:, 0, qsl], pnum, [("pso", sl)], ak)
                            cp("act", acc[0:64, 1, qsl], pden[0:64, 0, :], [("pso", sl)], ak)
                            cp("act", acc[64:128, 1, qsl], pden[64:128, 1, :], [("pso", sl)], ak)
                        else:
                            tt("dve", acc[:, 0, qsl], acc[:, 0, qsl], pnum, OP.add, [("pso", sl)] + ak, ak)
                            tt("dve", acc[0:64, 1, qsl], acc[0:64, 1, qsl], pden[0:64, 0, :], OP.add, [("pso", sl)] + ak, ak)
                            tt("dve", acc[64:128, 1, qsl], acc[64:128, 1, qsl], pden[64:128, 1, :], OP.add, [("pso", sl)] + ak, ak)

                    unit_front(units[0], nunit)
                    unit_front(units[1], nunit + 1)
                    for ui, u in enumerate(units):
                        if ui + 2 < len(units):
                            unit_front(units[ui + 2], nunit + ui + 2)
                        unit_back(u, nunit + ui)
                    nunit += len(units)
                    S.mark('t6')
                    wt, wk = w_next(24 + hp)
                    for b_ in range(4):
                        sl = b_ % 2
                        cs = slice(b_ * 512, (b_ + 1) * 512)
                        proj_fm(wt, wk, 2048 + b_ * 512, psp[sl][:], ("psp", sl))
                        if b_ == 0:
                            w_prefetch_cast()
                        act(rt[0][:], psp[sl][:], AF.Exp, [("psp", sl)], ["rt0"], scale=-1.0)
                        act(rt[0][:], rt[0][:], AF.Ln, ["rt0"], ["rt0"], bias=1.0)
                        act(rt[0][:], rt[0][:], AF.Exp, ["rt0"], ["rt0"], scale=-1.0)
                        tt("dve", rt[0][:], psp[sl][:], rt[0][:], OP.mult, [("psp", sl), "rt0"], ["rt0"])
                        act(rt[1][:], acc[:, 1, cs], AF.Ln, [("acc", b_)], ["rt1"])
                        act(rt[1][:], rt[1][:], AF.Exp, ["rt1"], ["rt1"], scale=-1.0)
                        tt("pool", rt[1][:], rt[1][:], acc[:, 0, cs], OP.mult, ["rt1", ("acc", b_)], ["rt1"])
                        tt("pool", Y3[:, hp, cs], rt[0][:], rt[1][:], OP.mult, ["rt0", "rt1"], [("yT", hp, b_)])
                S.emit()

            with contextlib.ExitStack() as s2:
                sb2 = lambda name, shape, dt: s2.enter_context(nc.sbuf_tensor(uniq(name), shape, dt))
                ps2 = lambda name, shape, dt: s2.enter_context(nc.psum_tensor(uniq(name), shape, dt))
                sq = sb2("sq", [128, 2048], BF16)
                QT = [sb2("QT%d" % i, [128, 2048], BF16) for i in range(2)]
                KTo = [sb2("KTo%d" % i, [128, 2048], BF16) for i in range(2)]
                Ktk = [sb2("Ktk%d" % i, [128, 16, 128], BF16) for i in range(2)]
                vtk = sb2("vtk", [128, 16, 128], BF16)
                oacc = sb2("oacc", [128, 2048], F32)
                dec = [sb2("dec%d" % i, [128, 32], F32) for i in range(2)]
                gt = [sb2("ogt%d" % i, [128, 512], F32) for i in range(8)]
                sqb = gt[7][:].bitcast(BF16)[:, 0:512]
                attm = [sb2("attm%d" % i, [128, 128], BF16) for i in range(2)]
                Sb = [[sb2("Sb%d%d" % (di, i), [128, 128], BF16) for i in range(3)] for di in range(2)]
                To = sb2("To", [128, 128], F32)
                psf = [ps2("psf%d" % i, [128, 512], F32) for i in range(2)]
                psv = ps2("psv", [128, 4, 128], F32)
                pskt = ps2("pskt", [128, 8, 128], BF16)
                psmd = [ps2("psm%d" % i, [128, 4, 128], F32) for i in range(2)]
                psat = [psmd[0][:, 0, :], psmd[1][:, 0, :]]
                psa_ = [psmd[0][:, 1, :], psmd[1][:, 1, :]]
                pso_ = [ps2("psoo%d" % i, [128, 512], F32) for i in range(2)]

                for h in range(8):
                    npj = 0
                    wt, wk = w_next(32 + h)
                    for b_ in range(4):
                        sl = npj % 2
                        npj += 1
                        cs = slice(b_ * 512, (b_ + 1) * 512)
                        proj_fm(wt, wk, 2048 + b_ * 512, psf[sl][:], ("psf", sl))
                        if b_ == 0:
                            w_prefetch_cast()
                        act(gt[0][:], psf[sl][:], AF.Exp, [("psf", sl)], [("g0", 0)], scale=-1.0)
                        act(gt[0][:], gt[0][:], AF.Ln, [("g0", 0)], [("g0", 0)], bias=1.0)
                        act(gt[0][:], gt[0][:], AF.Exp, [("g0", 0)], [("g0", 0)], scale=-1.0)
                        tt("dve", sq[:, cs], psf[sl][:], gt[0][:], OP.mult, [("psf", sl), ("g0", 0)], ["sq"])
                    for di, (wb, lbc, omc, mode) in enumerate(((40 + h, LBI(h), OMI(h), "in"),
                                                                (48 + h, LBO(h), OMO(h), "out"))):
                        wt, wk = w_next(wb)
                        if di == 0:
                            wv, wvk = w_next(56 + h)
                        for b_ in range(4):
                            sl = npj % 2
                            npj += 1
                            cs = slice(b_ * 512, (b_ + 1) * 512)
                            proj_fm(wt, wk, 2048 + b_ * 512, psf[sl][:], ("psf", sl))
                            if b_ == 0 and di == 1:
                                w_prefetch_cast()
                            gate_chain(psf[sl][:], ("psf", sl), lbc, omc, [g[:] for g in gt[4 * (b_ % 2):4 * (b_ % 2) + 4]], mode,
                                       sq[:, cs],
                                       dict(KT=KTo[di][:, cs], KTk=("KTo", di, b_), QT=QT[di][:, cs], QTk=("QT", di, b_),
                                            dec=dec[di][:, b_ * 8:b_ * 8 + 8], deck=("dec", di, b_)), par=b_ % 2)
                            if di == 0:
                                fns = []
                                for j in range(4):
                                    t0 = 2048 + (b_ * 4 + j) * 128
                                    fns += [mm(psv[:, j, :], uT(kc, t0, 128), wv[:, kc, :], kc == 0, kc == 15) for kc in range(16)]
                                mmgroup(fns, [wvk] + ukeys(2048 + b_ * 512, 512), ["psv"])
                                cp("act", vtk[:, b_ * 4:b_ * 4 + 4, :], psv[:], ["psv"], [("vtk", b_)])
                                if b_ == 3:
                                    w_prefetch_cast()
                        for g0 in range(0, 16, 8):
                            fns = [(lambda e, j=j, di=di, g0=g0: e.transpose(pskt[:, j - g0, :], KTo[di][:, j * 128:(j + 1) * 128], IDB))
                                   for j in range(g0, g0 + 8)]
                            mmgroup(fns, [("KTo", di, g0 // 4), ("KTo", di, g0 // 4 + 1), ("cstb", B_ID)], ["pskt"])
                            cp("act", Ktk[di][:, g0:g0 + 8, :], pskt[:], ["pskt"], [("Ktk", di, g0 // 8)])
                    Abuf = [[psf[0][:, 0:128], psv[:].rearrange("p a b -> p (a b)")[:, 0:128]],
                            [psf[1][:, 0:128], pskt[:].rearrange("p a b -> p (a b)").bitcast(F32)[:, 0:128]]]
                    Akey = [[("psf", 0), "psv"], [("psf", 1), "pskt"]]
                    chunk = lambda di, k: k if di == 0 else 31 - k
                    Tst = [T32[:, h, :], To[:]]
                    Tk = [("T32", h), "To"]

                    def emit_state(di, k):
                        c = chunk(di, k)
                        j, pb = c // 2, 64 * (c % 2)
                        ab, akey = Abuf[di][k % 2], Akey[di][k % 2]
                        mmgroup([mm(ab, Ktk[di][pb:pb + 64, j, :], vtk[pb:pb + 64, j, :], True, True)],
                                [("Ktk", di, j // 8), ("vtk", j // 4)], [akey])
                        if di == 0:
                            if k == 0:
                                act(Sb[0][0][:], T32[:, h, :], AF.Copy, [("T32", h), ("dlast", h)], [("Sb", 0, 0)],
                                    scale=dlast[:, h:h + 1])
                            dprev = dlast[:, h:h + 1] if k == 0 else dec[0][:, c - 1:c]
                            stt(T32[:, h, :], T32[:, h, :], dprev, ab, OP.mult, OP.add,
                                [("T32", h), akey, ("dec", 0, max(c - 1, 0) // 8), ("dlast", h)], [("T32", h)])
                            if k < 31:
                                act(Sb[0][(k + 1) % 3][:], T32[:, h, :], AF.Copy, [("T32", h), ("dec", 0, c // 8)],
                                    [("Sb", 0, (k + 1) % 3)], scale=dec[0][:, c:c + 1])
                        else:
                            if k == 0:
                                cp("dve", To[:], ab, [akey], ["To"])
                            else:
                                stt(To[:], To[:], dec[1][:, c:c + 1], ab, OP.mult, OP.add,
                                    ["To", akey, ("dec", 1, c // 8)], ["To"])
                            if k < 31:
                                cn = c - 1
                                act(Sb[1][(k + 1) % 3][:], To[:], AF.Copy, ["To", ("dec", 1, cn // 8)],
                                    [("Sb", 1, (k + 1) % 3)], scale=dec[1][:, cn:cn + 1])

                    for di in range(2):
                        emit_state(di, 0)
                    for di in range(2):
                        emit_state(di, 1)
                    for k in range(32):
                        for di in range(2):
                            c = chunk(di, k)
                            j, pb = c // 2, 64 * (c % 2)
                            b_ = c // 8
                            first_of_tile = (c % 2 == 0) if di == 0 else (c % 2 == 1)
                            cs64 = slice(c * 64, (c + 1) * 64)
                            if first_of_tile:
                                tsl = slice(j * 128, (j + 1) * 128)
                                mmgroup([mm(psat[di], KTo[di][:, tsl], QT[di][:, tsl], True, True)],
                                        [("KTo", di, b_), ("QT", di, b_)], [("psm", di)])
                                tt("dve", attm[di][:], psat[di], HMF if di == 0 else HMB, OP.mult,
                                   [("psm", di), ("cstb", B_HMF), ("cstb", B_HMB)], [("attm", di)])
                            has_state = (di == 0) or (k > 0)
                            oc = slice((c % 8) * 64, (c % 8 + 1) * 64)
                            fns = [mm(pso_[di][:, oc], vtk[pb:pb + 64, j, :], attm[di][pb:pb + 64, pb:pb + 64], True, not has_state)]
                            rr = [("vtk", j // 4), ("attm", di)]
                            if has_state:
                                fns.append(mm(pso_[di][:, oc], Sb[di][k % 3][:], QT[di][:, cs64], False, True))
                                rr += [("Sb", di, k % 3), ("QT", di, b_)]
                            mmgroup(fns, rr, [("psoo", di)])
                            if k + 2 < 32:
                                emit_state(di, k + 2)
                            done = (c % 8 == 7) if di == 0 else (c % 8 == 0)
                            if done:
                                cs = slice(b_ * 512, (b_ + 1) * 512)
                                firstw = (b_ < 2) if di == 0 else (b_ >= 2)
                                if firstw:
                                    cp("act", oacc[:, cs], pso_[di][:], [("psoo", di)], [("oacc", b_)])
                                else:
                                    tt("dve", oacc[:, cs], oacc[:, cs], pso_[di][:], OP.add,
                                       [("psoo", di), ("oacc", b_)], [("oacc", b_)])
                    wt, wk = w_next(64 + h)
                    for b_ in range(4):
                        sl = b_ % 2
                        cs = slice(b_ * 512, (b_ + 1) * 512)
                        proj_fm(wt, wk, 2048 + b_ * 512, psf[sl][:], ("psf", sl))
                        if b_ == 0:
                            w_prefetch_cast()
                        tt("pool", sqb, oacc[:, cs], oacc[:, cs], OP.mult, [("oacc", b_)], [("g3", 1)])
                        mmgroup([mm(pso_[0][:], ONEB, sqb, True, True)], [("g3", 1), ("cstb", B_ONE)], [("psoo", 0)])
                        act(gt[1][:], pso_[0][:], AF.Ln, [("psoo", 0)], [("g1", 0)], scale=1.0 / 128, bias=EPS)
                        act(gt[1][:], gt[1][:], AF.Exp, [("g1", 0)], [("g1", 0)], scale=-0.5)
                        tt("dve", gt[1][:], gt[1][:], oacc[:, cs], OP.mult, [("g1", 0), ("oacc", b_)], [("g1", 0)])
                        act(gt[0][:], psf[sl][:], AF.Exp, [("psf", sl)], [("g0", 0)], scale=-1.0)
                        act(gt[0][:], gt[0][:], AF.Ln, [("g0", 0)], [("g0", 0)], bias=1.0)
                        act(gt[0][:], gt[0][:], AF.Exp, [("g0", 0)], [("g0", 0)], scale=-1.0)
                        tt("dve", gt[0][:], psf[sl][:], gt[0][:], OP.mult, [("psf", sl), ("g0", 0)], [("g0", 0)])
                        stt(Y3[:, 8 + h, cs], gt[1][:], ghg[:, 0:1], gt[0][:], OP.mult, OP.mult,
                            [("g1", 0), ("g0", 0), "ghg"], [("yT", 8 + h, b_)])
                S.emit()

        if debug:
            S.add("sp", lambda e: e.dma_start(out=dbg_d[:, :], in_=bufBC[:]), [("yT", k, b_) for k in range(16) for b_ in range(4)],
                  ["dbgout"], dma="dbgout")

        def load_w_res(wd, nk, wdst, stg, key):
            wv_ = wd.rearrange("(k p) n -> p k n", p=128)
            engs = ("dve", "pool", "act")
            for kc in range(nk):
                sg = kc % len(stg)
                dma(stg[sg][:], wv_[:, kc, :], [], [("wstg", sg)], ("wstg", sg))
                cp(engs[kc % 3], wdst[:, kc, :], stg[sg][:], [("wstg", sg)], [(key, kc)])

        def row_rstd(srcs, skeys, ss, sl, junk):
            o = 8 * sl
            for cg in range(4):
                S.add("act", lambda e, cg=cg, o=o: e.activation(junk[:], srcs[cg], AF.Square, accum_out=ss[:, o + cg:o + cg + 1]),
                      [skeys[cg]], ["junk", ("ss4", sl, cg)])
            S.add("dve", lambda e, o=o: e.tensor_reduce(ss[:, o + 4:o + 5], ss[:, o:o + 4], mybir.AxisListType.X, OP.add),
                  [("ss4", sl, cg) for cg in range(4)], [("ssr", sl)])
            ts("dve", ss[:, o + 5:o + 6], ss[:, o + 4:o + 5], 1.0 / 2048, EPS, OP.mult, OP.add, [("ssr", sl)], [("ss1", sl)])
            act(ss[:, o + 6:o + 7], ss[:, o + 5:o + 6], AF.Ln, [("ss1", sl)], [("ss2", sl)])
            act(ss[:, o + 7:o + 8], ss[:, o + 6:o + 7], AF.Exp, [("ss2", sl)], [("ss3", sl)], scale=-0.5)
            return ss[:, o + 7:o + 8], ("ss3", sl)

        with contextlib.ExitStack() as s2:
            sb2 = lambda name, shape, dt: s2.enter_context(nc.sbuf_tensor(uniq(name), shape, dt))
            ps2 = lambda name, shape, dt: s2.enter_context(nc.psum_tensor(uniq(name), shape, dt))
            wres = A3
            xt = [sb2("xt%d" % i, [128, 2048], F32) for i in range(2)]
            stg = [sb2("stg%d" % i, [128, 2048], F32) for i in range(3)]
            gvec = sb2("gvec", [128, 2048], F32)
            ytmp = [sb2("ytmp%d" % i, [128, 512], F32) for i in range(2)]
            ss = sb2("ss", [128, 16], F32)
            junk = sb2("junk", [128, 512], BF16)
            pyo = [[ps2("pyo%d%d" % (j, i), [128, 512], F32) for i in range(4)] for j in range(2)]

            load_w_res(wout_d, 16, wres, stg, "wres")
            dma(gvec[:], gpost_d[:, :], [], ["gvec"], "gvec")
            for tt_ in range(16):
                sl = tt_ % 2
                tsl = slice(tt_ * 128, (tt_ + 1) * 128)
                dma(xt[sl][:], x_d[2048 + tt_ * 128:2048 + (tt_ + 1) * 128, :], [], [("xt", sl)], ("xt", sl))
                for cg in range(4):
                    fns = [mm(pyo[sl][cg][:], Y3[:, kc, tsl], wres[:, kc, cg * 512:(cg + 1) * 512], kc == 0, kc == 15)
                           for kc in range(16)]
                    mmgroup(fns, [("yT", kc, tt_ // 4) for kc in range(16)] + [("wres", kc) for kc in range(16)],
                            [("pyo", sl, cg)])
                rs, rsk = row_rstd([pyo[sl][cg][:] for cg in range(4)], [("pyo", sl, cg) for cg in range(4)], ss, sl, junk)
                for cg in range(4):
                    cs = slice(cg * 512, (cg + 1) * 512)
                    stt(ytmp[cg % 2][:], pyo[sl][cg][:], rs, gvec[:, cs], OP.mult, OP.mult,
                        [("pyo", sl, cg), rsk, "gvec"], [("ytmp", cg % 2)])
                    tt("pool", xt[sl][:, cs], xt[sl][:, cs], ytmp[cg % 2][:], OP.add, [("xt", sl), ("ytmp", cg % 2)], [("xt", sl)])
                S.add("pool", lambda e, tsl=tsl, sl=sl: e.dma_start(out=out_d[tsl, :], in_=xt[sl][:]),
                      [("xt", sl)], [("out", tt_)], dma=("outw", sl))
            S.emit()

        with contextlib.ExitStack() as s2:
            sb2 = lambda name, shape, dt: s2.enter_context(nc.sbuf_tensor(uniq(name), shape, dt))
            ps2 = lambda name, shape, dt: s2.enter_context(nc.psum_tensor(uniq(name), shape, dt))
            wres = A3
            wpp = sb2("wpp", [128, 2, 2048], BF16)
            xt = [sb2("xt%d" % i, [128, 2048], F32) for i in range(3)]
            h1b = [sb2("h1b0", [128, 2048], BF16)] * 2
            h1T = [sb2("h1T%d" % i, [128, 16, 128], BF16) for i in range(2)]
            gvec = sb2("gvec", [128, 2048], F32)
            ge = [sb2("ge%d" % i, [128, 2048], F32) for i in range(2)]
            stg = ge
            ss = sb2("ss", [128, 16], F32)
            junk = sb2("junk", [128, 512], BF16)
            ytmp = [sb2("ytmp%d" % i, [128, 512], F32) for i in range(2)]
            pf = sb2("pf", [128, 256], F32)
            pb_ = sb2("pb", [128, 256], BF16)
            pT = Y3
            pg = [ps2("pg%d" % i, [128, 512], F32) for i in range(2)]
            pe2 = [ps2("pe2%d" % i, [128, 512], F32) for i in range(2)]
            ppt = ps2("ppt", [128, 8, 128], BF16)
            pst = [ps2("pst%d" % i, [128, 8, 128], BF16) for i in range(2)]

            load_w_res(wpg_d, 16, wres, stg, "wres")
            load_w_res(wpp_d, 2, wpp, stg, "wpp")
            dma(gvec[:], gple_d[:, :], [], ["gvec"], "gvec")
            for tt_ in range(16):
                tsl = slice(tt_ * 128, (tt_ + 1) * 128)
                dma(pf[:], p_d[tsl, :], [], ["pf"], "pf")
                cp("dve", pb_[:], pf[:], ["pf"], ["pb"])
                fns = [(lambda e, k=k: e.transpose(ppt[:, k, :], pb_[:, k * 128:(k + 1) * 128], IDB)) for k in range(2)]
                mmgroup(fns, ["pb", ("cstb", B_ID)], ["ppt"])
                cp("act", pT[:, 0:2, tsl], ppt[:, 0:2, :], ["ppt", "dbgout"], [("pT", tt_)])

            def o5_front(tt_):
                sl = tt_ % 2
                x3 = tt_ % 3
                tsl = slice(tt_ * 128, (tt_ + 1) * 128)
                dma(xt[x3][:], out_d[tsl, :], [("out", tt_)], [("xt", x3)], ("xt", x3))
                cp("act", h1b[sl][:], xt[x3][:], [("xt", x3)], [("h1b", 0)])
                for hf in range(2):
                    fns = [(lambda e, k=k, hf=hf, sl=sl: e.transpose(pst[hf][:, k, :], h1b[sl][:, (hf * 8 + k) * 128:(hf * 8 + k + 1) * 128], IDB))
                           for k in range(8)]
                    mmgroup(fns, [("h1b", 0), ("cstb", B_ID)], [("pst", hf)])
                    cp("dve" if hf else "act", h1T[sl][:, hf * 8:hf * 8 + 8, :], pst[hf][:], [("pst", hf)], [("h1T", sl, hf)])

            o5_front(0)
            for tt_ in range(16):
                sl = tt_ % 2
                tsl = slice(tt_ * 128, (tt_ + 1) * 128)
                if tt_ + 1 < 16:
                    o5_front(tt_ + 1)
                for cg in range(4):
                    cs = slice(cg * 512, (cg + 1) * 512)
                    s_ = cg % 2
                    fns = [mm(pg[s_][:], h1T[sl][:, kc, :], wres[:, kc, cs], kc == 0, kc == 15) for kc in range(16)]
                    mmgroup(fns, [("h1T", sl, 0), ("h1T", sl, 1)] + [("wres", kc) for kc in range(16)], [("pg", s_)])
                    fns = [mm(pe2[s_][:], pT[:, kc, tsl], wpp[:, kc, cs], kc == 0, kc == 1) for kc in range(2)]
                    mmgroup(fns, [("pT", tt_), ("wpp", 0), ("wpp", 1)], [("pe2", s_)])
                    act(ytmp[s_][:], pg[s_][:], AF.Exp, [("pg", s_)], [("ytmp", s_)], scale=-1.0)
                    act(ytmp[s_][:], ytmp[s_][:], AF.Ln, [("ytmp", s_)], [("ytmp", s_)], bias=1.0)
                    act(ytmp[s_][:], ytmp[s_][:], AF.Exp, [("ytmp", s_)], [("ytmp", s_)], scale=-1.0)
                    tt("dve", ge[sl][:, cs], ytmp[s_][:], pe2[s_][:], OP.mult, [("ytmp", s_), ("pe2", s_)], [("ge", sl, cg)])
                rs, rsk = row_rstd([ge[sl][:, cg * 512:(cg + 1) * 512] for cg in range(4)], [("ge", sl, cg) for cg in range(4)],
                                   ss, sl, junk)
                stt(ge[sl][:], ge[sl][:], rs, gvec[:], OP.mult, OP.mult, [("ge", sl, cg) for cg in range(4)] + [rsk, "gvec"],
                    [("ge", sl, cg) for cg in range(4)])
                tt("pool", ge[sl][:], ge[sl][:], xt[tt_ % 3][:], OP.add, [("xt", tt_ % 3)] + [("ge", sl, cg) for cg in range(4)],
                   [("ge", sl, cg) for cg in range(4)])
                S.add("pool", lambda e, tsl=tsl, sl=sl: e.dma_start(out=out_d[tsl, :], in_=ge[sl][:]),
                      [("ge", sl, cg) for cg in range(4)], [("out", tt_)], dma=("outw", sl))
            S.add("sp", lambda e: e.nop(), [("out", t_) for t_ in range(16)], ["fin"])
            S.emit()
    return nc


def _consts():
    c = np.zeros((128, NCST), np.float32)
    p = np.arange(128)
    c[:, C_ID:C_ID + 128] = np.eye(128)
    for m in range(128):
        e = m % 64
        if e < 8:
            c[m + 8, C_RT + m] = 1.0
        elif e < 16:
            c[m - 8, C_RT + m] = 1.0
    i = p[:, None]
    j = np.arange(128)[None, :]
    m1 = (i >= j).astype(np.float32)
    m2 = (i <= j).astype(np.float32)
    c[:, C_M12:C_M12 + 512] = np.concatenate([m1, m2, m1, m2], axis=1)
    same = (i // 64) == (j // 64)
    c[:, C_HMF:C_HMF + 128] = (same & (i <= j)).astype(np.float32)
    c[:, C_HMB:C_HMB + 128] = (same & (i >= j)).astype(np.float32)
    t = np.arange(512)
    c[:, C_RST:C_RST + 512] = (t % 64 != 0).astype(np.float32)[None, :]
    inv = np.power(np.float32(500000.0), -np.arange(0, 16, 2, dtype=np.float32) / np.float32(16)).astype(np.float32)
    e = p % 64
    c[:, C_INV] = np.where(e < 16, inv[e % 8] / np.float32(2 * np.pi), 0.0)
    c[:, C_SGN] = np.where(e < 8, -1.0, np.where(e < 16, 1.0, 0.0))
    c[:, C_ONE:C_ONE + 128] = 1.0
    return c


_NC_CACHE = {}


def kernel(x, p, positions, w_in, w_out, g_pre, g_post, g_hg, lb_fwd, lb_bwd, w_pg, w_pp, g_ple, _debug=False):
    x = np.asarray(x, np.float32)
    p = np.asarray(p, np.float32)
    positions = np.asarray(positions, np.int32)
    w_in0 = np.asarray(w_in, np.float32)[0]
    aq, ak, av, ag, hq, hff, hfb, hv, hg = [w_in0[:, s:s + 1024] for s in range(0, 9216, 1024)]

    def tile_w(cols):
        w = np.concatenate(cols, axis=1)
        return np.ascontiguousarray(w.reshape(16, 128, NB_W, 128).transpose(2, 1, 0, 3)).reshape(NB_W, 128, 2048)

    w_t = {1: tile_w([aq, ak, av, ag, hq, hff, hfb, hv, hg]), 0: tile_w([aq, ak, av, ag, hq, hfb, hff, hv, hg])}

    def lbt(a, b):
        f = lambda r: np.asarray(r, np.float32).reshape(2, 8, 128).transpose(2, 0, 1).reshape(128, 16)
        return np.ascontiguousarray(np.concatenate([f(a), f(b)], axis=1))

    lb_t = {1: lbt(lb_fwd, lb_bwd), 0: lbt(lb_bwd, lb_fwd)}
    rep = lambda v: np.ascontiguousarray(np.broadcast_to(np.asarray(v, np.float32).reshape(1, -1), (128, 2048)))
    common = {
        "w_out": np.ascontiguousarray(np.asarray(w_out, np.float32)[0]),
        "w_pg": np.ascontiguousarray(np.asarray(w_pg, np.float32)[0]),
        "w_pp": np.ascontiguousarray(np.asarray(w_pp, np.float32)[0]),
        "g_pre": rep(g_pre), "g_post": rep(g_post), "g_ple": rep(g_ple),
        "g_hg": np.ascontiguousarray(np.asarray(g_hg, np.float32).reshape(128, 1)),
        "cst": _consts(),
    }
    in_maps = []
    for c in range(8):
        b, hf = c // 2, c % 2
        if hf == 1:
            xl, pl, posl = x[b], p[0, b, 2048:], positions[b]
        else:
            xl, pl, posl = x[b, ::-1], p[0, b, 2047::-1], positions[b, ::-1]
        m = dict(common)
        m["x"] = np.ascontiguousarray(xl)
        m["p"] = np.ascontiguousarray(pl)
        m["pos"] = np.ascontiguousarray(np.broadcast_to(posl[1024:].reshape(1, 3072), (128, 3072))).astype(np.int32)
        m["w_in"] = w_t[hf]
        m["lb"] = lb_t[hf]
        in_maps.append(m)
    key = bool(_debug)
    if key not in _NC_CACHE:
        _NC_CACHE[key] = build(debug=key)
    nc = _NC_CACHE[key]
    res = run_bass_kernel_spmd(nc, in_maps, core_ids=list(range(8)))
    out = np.empty((4, 4096, 2048), np.float32)
    for c in range(8):
        b, hf = c // 2, c % 2
        o = np.asarray(res.results[c]["out"], np.float32)
        if hf == 1:
            out[b, 2048:] = o
        else:
            out[b, 2047::-1] = o
    if _debug:
        return out, [res.results[c]["dbg"] for c in range(8)]
    return out
```

```python
import contextlib
import math
import numpy as np
import concourse.bass as bass
import concourse.mybir as mybir
from concourse.bass_utils import run_bass_kernel_spmd

F32 = mybir.dt.float32
BF16 = mybir.dt.bfloat16
I32 = mybir.dt.int32
AF = mybir.ActivationFunctionType
OP = mybir.AluOpType

EPS = 1e-6
NB_W = 72
C_ID, C_RT, C_M12, C_HMF, C_HMB, C_RST, C_INV, C_SGN, C_ONE = 0, 128, 256, 768, 896, 1024, 1536, 1537, 1538
NCST = 1666
NEGBIG = -30000.0
B_ID, B_RT, B_M12, B_HMF, B_HMB, B_ONE = 0, 128, 256, 768, 896, 1024
NCB = 1152


class Sched:
    ENG = ("pe", "act", "dve", "pool", "sp")

    def __init__(self, nc, stack):
        self.nc = nc
        self.stack = stack
        self.ops = []
        self.emitted = 0
        self.last_w = {}
        self.readers = {}
        self.esem = {e: stack.enter_context(nc.semaphore("s_" + e)) for e in self.ENG}
        self.ecnt = {e: 0 for e in self.ENG}
        self.dsem = {}
        self.dcnt = {}
        self.waited = {e: {} for e in self.ENG}
        self.enabled = True
        self.nemit = 0
        self.phase_dmas = []
        self.stop_n = None

    def add(self, eng, fn, r=(), w=(), dma=None):
        if not self.enabled:
            return None
        i = len(self.ops)
        deps = set()
        for k in r:
            j = self.last_w.get(k)
            if j is not None:
                deps.add(j)
        for k in w:
            j = self.last_w.get(k)
            if j is not None:
                deps.add(j)
            deps.update(self.readers.get(k, ()))
        for k in r:
            self.readers.setdefault(k, []).append(i)
        for k in w:
            self.last_w[k] = i
            self.readers[k] = []
        if dma is not None:
            if dma not in self.dsem:
                self.dsem[dma] = self.stack.enter_context(self.nc.semaphore("d%d" % len(self.dsem)))
                self.dcnt[dma] = 0
            self.dcnt[dma] += 16
            sig = (self.dsem[dma], self.dcnt[dma], 16, None)
            self.phase_dmas.append(i)
        else:
            self.ecnt[eng] += 1
            sig = (self.esem[eng], self.ecnt[eng], 1, eng)
        self.ops.append((eng, fn, deps, sig))
        return i

    def mark(self, tag):
        if self.stop_n == tag:
            self.enabled = False

    def emit(self):
        nc = self.nc
        if self.phase_dmas and self.enabled:
            self.ecnt["sp"] += 1
            self.ops.append(("sp", lambda e: e.nop(), set(self.phase_dmas), (self.esem["sp"], self.ecnt["sp"], 1, "sp")))
        self.phase_dmas = []
        new = self.ops[self.emitted:]
        self.emitted = len(self.ops)
        self.nemit += 1
        if isinstance(self.stop_n, int) and self.nemit >= self.stop_n:
            self.enabled = False
        if not new:
            return
        per = {e: [] for e in self.ENG}
        for op in new:
            per[op[0]].append(op)
        ops = self.ops
        waited = self.waited
        hw = {"pe": "tensor", "act": "scalar", "dve": "vector", "pool": "gpsimd", "sp": "sync"}
        with nc.Block() as blk:
            for e in self.ENG:
                lst = per[e]
                if not lst:
                    continue

                def body(engine, lst=lst, e=e):
                    wd = waited[e]
                    for (_, fn, deps, sig) in lst:
                        need = {}
                        for j in deps:
                            s, v, _, pe_ = ops[j][3]
                            if pe_ == "pe" and e == "pe":
                                continue
                            key = id(s)
                            if need.get(key, (None, 0))[1] < v:
                                need[key] = (s, v)
                        for key, (s, v) in need.items():
                            if wd.get(key, 0) < v:
                                engine.wait_ge(s, v)
                                wd[key] = v
                        ins = fn(engine)
                        ins.then_inc(sig[0], sig[2])

                getattr(blk, hw[e])(body)


def build(debug=False, stop=None):
    nc = bass.Bass("TRN2", target_bir_lowering=False)
    dt_in = lambda name, shape, dt=F32: nc.dram_tensor(name, shape, dt, kind="ExternalInput").ap()
    x_d = dt_in("x", [4096, 2048])
    p_d = dt_in("p", [2048, 256])
    pos_d = dt_in("pos", [128, 3072], I32)
    win_d = dt_in("w_in", [NB_W, 128, 2048])
    wout_d = dt_in("w_out", [2048, 2048])
    wpg_d = dt_in("w_pg", [2048, 2048])
    wpp_d = dt_in("w_pp", [256, 2048])
    gpre_d = dt_in("g_pre", [128, 2048])
    gpost_d = dt_in("g_post", [128, 2048])
    gple_d = dt_in("g_ple", [128, 2048])
    ghg_d = dt_in("g_hg", [128, 1])
    lb_d = dt_in("lb", [128, 32])
    cst_d = dt_in("cst", [128, NCST])
    out_d = nc.dram_tensor("out", [2048, 2048], F32, kind="ExternalOutput").ap()
    if debug:
        dbg_d = nc.dram_tensor("dbg", [128, 16 * 2048], BF16, kind="ExternalOutput").ap()

    _cnt = [0]

    def uniq(name):
        _cnt[0] += 1
        return "%s_%d" % (name, _cnt[0])

    st = contextlib.ExitStack()
    with st:
        S = Sched(nc, st)
        S.stop_n = stop
        sb = lambda name, shape, dt: st.enter_context(nc.sbuf_tensor(name, shape, dt))
        bufA = sb("bufA", [128, 16 * 2048], BF16)
        bufBC = sb("bufBC", [128, 16 * 2048], BF16)
        cstf = sb("cstf", [128, 514], F32)
        cstb = sb("cstb", [128, NCB], BF16)
        lbv = sb("lbv", [128, 32], F32)
        ghg = sb("ghg", [128, 1], F32)
        RST = cstf[:, 0:512]
        INV = cstf[:, 512:513]
        SGN = cstf[:, 513:514]
        IDB = cstb[:, B_ID:B_ID + 128]
        RTB = cstb[:, B_RT:B_RT + 128]
        M12 = cstb[:, B_M12:B_M12 + 512]
        HMF = cstb[:, B_HMF:B_HMF + 128]
        HMB = cstb[:, B_HMB:B_HMB + 128]
        ONEB = cstb[:, B_ONE:B_ONE + 128]

        A3 = bufA[:].rearrange("p (k t) -> p k t", k=16)
        BC_lo = bufBC[:, 0:16384].rearrange("p (k t) -> p k t", k=16)
        BC_hi = bufBC[:, 16384:32768].rearrange("p (k t) -> p k t", k=16)
        Y3 = bufBC[:].rearrange("p (k t) -> p k t", k=16)

        def uT3(t0):
            if t0 >= 2048:
                return A3, t0 - 2048
            if t0 >= 1024:
                return BC_hi, t0 - 1024
            return BC_lo, t0

        def uT(kc, t0, n):
            v, o = uT3(t0)
            return v[:, kc, o:o + n]

        ukeys = lambda t0, n: [("uT", t) for t in range(t0 // 128, (t0 + n) // 128)]

        def act(out, in_, func, r, w, scale=1.0, bias=0.0):
            S.add("act", lambda e: e.activation(out, in_, func, bias=bias, scale=scale), r, w)

        def tt(eng, out, a, b, op, r, w):
            S.add(eng, lambda e: e.tensor_tensor(out, a, b, op), r, w)

        def ts(eng, out, a, s1, s2, op0, op1, r, w):
            if s2 is None:
                S.add(eng, lambda e: e.tensor_scalar(out, a, s1, None, op0), r, w)
            else:
                S.add(eng, lambda e: e.tensor_scalar(out, a, s1, s2, op0, op1), r, w)

        def stt(out, a, sc, b, op0, op1, r, w, accum=None):
            if accum is None:
                S.add("dve", lambda e: e.scalar_tensor_tensor(out, a, sc, b, op0, op1), r, w)
            else:
                S.add("dve", lambda e: e.scalar_tensor_tensor(out, a, sc, b, op0, op1, accum_out=accum), r, w)

        def cp(eng, out, in_, r, w):
            if eng == "act":
                S.add("act", lambda e: e.activation(out, in_, AF.Copy), r, w)
            else:
                S.add(eng, lambda e: e.tensor_copy(out, in_), r, w)

        def recip(out, in_, r, w):
            S.add("dve", lambda e: e.reciprocal(out, in_), r, w)

        def dma(out, in_, r, w, key):
            S.add("sp", lambda e: e.dma_start(out=out, in_=in_), r, w, dma=key)

        def mm(out, lhsT, rhs, start, stop):
            return lambda e: e.matmul(out, lhsT, rhs, start=start, stop=stop)

        def mmgroup(fns, r, w):
            def f(e):
                ins = None
                for g in fns:
                    ins = g(e)
                return ins
            S.add("pe", f, r, w)

        def sigmoid_chain(dst, src, r_src, tmp, tk):
            act(tmp, src, AF.Exp, r_src, [tk], scale=-1.0)
            ts("dve", tmp, tmp, 1.0, None, OP.add, None, [tk], [tk])
            recip(dst, tmp, [tk], [tk] if dst is tmp else [tk + ("o",)])

        with nc.sbuf_tensor("cst0", [128, NCST], F32) as cst0, nc.sbuf_tensor("lb0", [128, 32], F32) as lb0:
            dma(cst0[:], cst_d[:, :], [], ["cst0"], "cst0")
            dma(lb0[:], lb_d[:, :], [], ["lb0"], "lb0")
            dma(ghg[:], ghg_d[:, :], [], ["ghg"], "ghg")
            cp("dve", cstf[:, 0:512], cst0[:, C_RST:C_RST + 512], ["cst0"], ["cstf"])
            cp("dve", cstf[:, 512:514], cst0[:, C_INV:C_INV + 2], ["cst0"], ["cstf2"])
            for (bo, co, n) in ((B_ID, C_ID, 128), (B_RT, C_RT, 128), (B_M12, C_M12, 512),
                                (B_HMF, C_HMF, 128), (B_HMB, C_HMB, 128), (B_ONE, C_ONE, 128)):
                cp("dve", cstb[:, bo:bo + n], cst0[:, co:co + n], ["cst0"], [("cstb", bo)])
            for d in range(2):
                o = d * 16
                tt("dve", lb0[:, o:o + 8], lb0[:, o:o + 8], lb0[:, o + 8:o + 16], OP.subtract, ["lb0"], ["lb0"])
                act(lb0[:, o + 8:o + 16], lb0[:, o:o + 8], AF.Exp, ["lb0"], ["lb0"], scale=-1.0)
                ts("dve", lb0[:, o + 8:o + 16], lb0[:, o + 8:o + 16], 1.0, None, OP.add, None, ["lb0"], ["lb0"])
                recip(lbv[:, o:o + 8], lb0[:, o + 8:o + 16], ["lb0"], ["lbv"])
                ts("dve", lbv[:, o + 8:o + 16], lbv[:, o:o + 8], -1.0, 1.0, OP.mult, OP.add, ["lbv"], ["lbv"])
            S.emit()

        LBI = lambda h: lbv[:, h:h + 1]
        OMI = lambda h: lbv[:, 8 + h:9 + h]
        LBO = lambda h: lbv[:, 16 + h:17 + h]
        OMO = lambda h: lbv[:, 24 + h:25 + h]

        with contextlib.ExitStack() as s1:
            sb1 = lambda name, shape, dt: s1.enter_context(nc.sbuf_tensor(name, shape, dt))
            T32 = sb1("T32", [128, 8, 128], F32)
            dlast = sb1("dlast", [128, 8], F32)
            stage = sb1("stage", [128, 1024], F32)
            wbf = [sb1("wbf%d" % i, [128, 16, 128], BF16) for i in range(2)]

            wseq = []
            for h in range(8):
                wseq += [40 + h, 56 + h]
            for hp in range(8):
                wseq += [hp, 8 + hp, 16 + hp, 24 + hp]
            for h in range(8):
                wseq += [32 + h, 40 + h, 48 + h, 56 + h, 64 + h]
            wstate = {"loaded": 0, "cast": 0, "use": 0}

            def w_load():
                n = wstate["loaded"]
                if n >= 2 * len(wseq):
                    return
                hb = n % 2
                dma(stage[:], win_d[wseq[n // 2]][:, hb * 1024:(hb + 1) * 1024], [], ["stage"], "stage")
                wstate["loaded"] += 1

            def w_cast():
                n = wstate["cast"]
                if n >= 2 * len(wseq):
                    return
                blk, hb = n // 2, n % 2
                cp("pool", wbf[blk % 2][:, hb * 8:(hb + 1) * 8, :].rearrange("p k c -> p (k c)"), stage[:], ["stage"],
                   [("wbf", blk % 2)])
                wstate["cast"] += 1

            def w_upto(nh):
                while wstate["cast"] < nh and wstate["cast"] < 2 * len(wseq):
                    if wstate["loaded"] <= wstate["cast"]:
                        w_load()
                    w_cast()
                if wstate["loaded"] <= wstate["cast"]:
                    w_load()

            def w_next(expect):
                n = wstate["use"]
                assert wseq[n] == expect, (n, wseq[n], expect)
                w_upto(2 * n + 2)
                wstate["use"] += 1
                return wbf[n % 2], ("wbf", n % 2)

            def w_prefetch_cast():
                w_upto(2 * wstate["use"] + 2)

            def proj_fm(wt, wk, t0, ps, pk):
                fns = [mm(ps, wt[:, kc, :], uT(kc, t0, 512), kc == 0, kc == 15) for kc in range(16)]
                mmgroup(fns, [wk] + ukeys(t0, 512), [pk])

            def norm_phase(tiles, gpre, xt, xb, ss, pst, junk):
                for n_, ti in enumerate(tiles):
                    sl = n_ % 2
                    t0 = ti * 128
                    dma(xt[sl][:], x_d[t0:t0 + 128, :], [], [("xt", sl)], ("xt", sl))
                    stt(junk[:], xt[sl][:], 1.0, xt[sl][:], OP.mult, OP.mult, [("xt", sl)], ["junk", ("ss", sl)],
                        accum=ss[:, 4 * sl:4 * sl + 1])
                    ts("dve", ss[:, 4 * sl + 1:4 * sl + 2], ss[:, 4 * sl:4 * sl + 1], 1.0 / 2048, EPS, OP.mult, OP.add,
                       [("ss", sl)], [("ss1", sl)])
                    act(ss[:, 4 * sl + 2:4 * sl + 3], ss[:, 4 * sl + 1:4 * sl + 2], AF.Ln, [("ss1", sl)], [("ss2", sl)])
                    act(ss[:, 4 * sl + 3:4 * sl + 4], ss[:, 4 * sl + 2:4 * sl + 3], AF.Exp, [("ss2", sl)], [("ss3", sl)],
                        scale=-0.5)
                    stt(xb[sl][:], xt[sl][:], ss[:, 4 * sl + 3:4 * sl + 4], gpre[:], OP.mult, OP.mult,
                        [("xt", sl), ("ss3", sl), "gpre"], [("xb", sl)])
                    v3, o = uT3(t0)
                    for hf in range(2):
                        fns = [(lambda e, k=k, hf=hf, sl=sl: e.transpose(pst[hf][:, k, :], xb[sl][:, (hf * 8 + k) * 128:(hf * 8 + k + 1) * 128], IDB))
                               for k in range(8)]
                        mmgroup(fns, [("xb", sl), ("cstb", B_ID)], [("pst", hf)])
                        cp("act" if hf == 0 else "pool" if False else "act", v3[:, hf * 8:hf * 8 + 8, o:o + 128], pst[hf][:],
                           [("pst", hf)], [("uT", ti)])

            def gate_chain(psf, pkf, lbc, omc, tmp, mode, sq, outs, par=0):
                t0_, t1_, t2_, t3_ = tmp
                act(t0_, psf, AF.Exp, [pkf], [("g0", par)], scale=-1.0)
                act(t0_, t0_, AF.Ln, [("g0", par)], [("g0", par)], bias=1.0)
                act(t0_, t0_, AF.Exp, [("g0", par)], [("g0", par)], scale=-1.0)
                ts("dve", t1_, t0_, omc, lbc, OP.mult, OP.add, [("g0", par), "lbv"], [("g1", par)])
                act(t2_, t1_, AF.Ln, [("g1", par)], [("g2", par)])
                ts("pool", t1_, t1_, -1.0, 1.0, OP.mult, OP.add, [("g1", par)], [("g1", par)])
                S.add("dve", lambda e: e.tensor_tensor_scan(t3_, RST, t2_, 0.0, OP.mult, OP.add),
                      [("g2", par), "cstf"], [("g3", par)])
                act(outs["dec"], t3_[:, 63:512:64], AF.Exp, [("g3", par)], [outs["deck"]])
                if mode == "out":
                    tt("pool", t3_, t3_, t2_, OP.subtract, [("g3", par), ("g2", par)], [("g3", par)])
                if mode in ("halo", "in"):
                    act(t0_, t3_, AF.Exp, [("g3", par)], [("g0", par)], scale=-1.0)
                    tt("pool", outs["KT"], t1_, t0_, OP.mult, [("g1", par), ("g0", par)], [outs["KTk"]])
                    if mode == "in":
                        act(t2_, t3_, AF.Exp, [("g3", par)], [("g2", par)])
                        tt("dve", outs["QT"], sq, t2_, OP.mult, [("g2", par), "sq"], [outs["QTk"]])
                else:
                    act(t0_, t3_, AF.Exp, [("g3", par)], [("g0", par)])
                    tt("pool", outs["KT"], t1_, t0_, OP.mult, [("g1", par), ("g0", par)], [outs["KTk"]])
                    act(t2_, t3_, AF.Exp, [("g3", par)], [("g2", par)], scale=-1.0)
                    tt("dve", outs["QT"], sq, t2_, OP.mult, [("g2", par), "sq"], [outs["QTk"]])

            with contextlib.ExitStack() as s2:
                sb2 = lambda name, shape, dt: s2.enter_context(nc.sbuf_tensor(uniq(name), shape, dt))
                ps2 = lambda name, shape, dt: s2.enter_context(nc.psum_tensor(uniq(name), shape, dt))
                gpre = sb2("gpre", [128, 2048], F32)
                xt = [sb2("xt%d" % i, [128, 2048], F32) for i in range(2)]
                xb = [sb2("xb%d" % i, [128, 2048], BF16) for i in range(2)]
                junk = sb2("junk", [128, 2048], BF16)
                ss = sb2("ss", [128, 8], F32)
                KT = [sb2("hKT%d" % i, [128, 512], BF16) for i in range(2)]
                HK = sb2("HK", [128, 16, 128], BF16)
                dech = sb2("dech", [128, 8, 32], F32)
                vtok = [sb2("hvtok%d" % i, [128, 4, 128], BF16) for i in range(2)]
                gt = [sb2("hgt%d" % i, [128, 512], F32) for i in range(4)]
                pst = [ps2("pst%d" % i, [128, 8, 128], BF16) for i in range(2)]
                psf = [ps2("psf%d" % i, [128, 512], F32) for i in range(2)]
                psv = ps2("psv", [128, 4, 128], F32)
                pskt = ps2("pskt", [128, 8, 128], BF16)
                psa = [ps2("psa%d" % i, [128, 512], F32) for i in range(2)]

                dma(gpre[:], gpre_d[:, :], [], ["gpre"], "gpre")
                w_load()
                norm_phase(list(range(16)), gpre, xt, xb, ss, pst, junk)
                nblk = 0
                for h in range(8):
                    wf, wfk = w_next(40 + h)
                    wv, wvk = w_next(56 + h)
                    for tb in range(4):
                        sl = nblk % 2
                        nblk += 1
                        proj_fm(wf, wfk, tb * 512, psf[sl][:], ("psf", sl))
                        gate_chain(psf[sl][:], ("psf", sl), LBI(h), OMI(h), [g[:] for g in gt], "halo", None,
                                   dict(KT=KT[sl][:], KTk=("hKT", sl), dec=dech[:, h, tb * 8:tb * 8 + 8],
                                        deck=("dech", h, tb)))
                        fns = []
                        for j in range(4):
                            t0 = (tb * 4 + j) * 128
                            fns += [mm(psv[:, j, :], uT(kc, t0, 128), wv[:, kc, :], kc == 0, kc == 15) for kc in range(16)]
                        mmgroup(fns, [wvk] + ukeys(tb * 512, 512), ["psv"])
                        vs = tb % 2
                        cp("act", vtok[vs][:], psv[:], ["psv"], [("hvtok", vs)])
                        if tb == 3:
                            w_prefetch_cast()
                        fns = [(lambda e, j=j, sl=sl: e.transpose(pskt[:, j, :], KT[sl][:, j * 128:(j + 1) * 128], IDB))
                               for j in range(4)]
                        mmgroup(fns, [("hKT", sl), ("cstb", B_ID)], ["pskt"])
                        cp("act", HK[:, tb * 4:tb * 4 + 4, :], pskt[:, 0:4, :], ["pskt"], [("HK", tb)])
                        for c in range(8):
                            j, pb = c // 2, 64 * (c % 2)
                            cg = tb * 8 + c
                            pa = psa[cg % 2]
                            mmgroup([mm(pa[:, 0:128], HK[pb:pb + 64, tb * 4 + j, :], vtok[vs][pb:pb + 64, j, :], True, True)],
                                    [("HK", tb), ("hvtok", vs)], [("psa", cg % 2)])
                            if cg == 0:
                                cp("dve", T32[:, h, :], pa[:, 0:128], [("psa", cg % 2)], [("T32", h)])
                            else:
                                stt(T32[:, h, :], T32[:, h, :], dech[:, h, cg - 1:cg], pa[:, 0:128], OP.mult, OP.add,
                                    [("T32", h), ("psa", cg % 2), ("dech", h, (cg - 1) // 8)], [("T32", h)])
                    cp("dve", dlast[:, h:h + 1], dech[:, h, 31:32], [("dech", h, 3)], [("dlast", h)])
                S.emit()

            with contextlib.ExitStack() as s2:
                sb2 = lambda name, shape, dt: s2.enter_context(nc.sbuf_tensor(uniq(name), shape, dt))
                ps2 = lambda name, shape, dt: s2.enter_context(nc.psum_tensor(uniq(name), shape, dt))
                gpre = sb2("gpre", [128, 2048], F32)
                xt = [sb2("xt%d" % i, [128, 2048], F32) for i in range(2)]
                xb = [sb2("xb%d" % i, [128, 2048], BF16) for i in range(2)]
                junk = sb2("junk", [128, 2048], BF16)
                ss = sb2("ss", [128, 8], F32)
                pst = [ps2("pst%d" % i, [128, 8, 128], BF16) for i in range(2)]
                dma(gpre[:], gpre_d[:, :], [], ["gpre"], "gpre")
                norm_phase(list(range(16, 32)), gpre, xt, xb, ss, pst, junk)
                S.emit()

            with contextlib.ExitStack() as s2:
                sb2 = lambda name, shape, dt: s2.enter_context(nc.sbuf_tensor(uniq(name), shape, dt))
                ps2 = lambda name, shape, dt: s2.enter_context(nc.psum_tensor(uniq(name), shape, dt))
                qP = sb2("qP", [128, 2, 2048], BF16)
                kT = sb2("kT", [128, 3072], BF16)
                vT = sb2("vT", [128, 3072], BF16)
                vt = sb2("vt", [128, 9, 128], BF16)
                acc = sb2("acc", [128, 2, 2048], F32)
                pe_ = [sb2("pe%d" % i, [128, 512], BF16) for i in range(3)]
                rt = [sb2("rt%d" % i, [128, 512], F32) for i in range(2)]
                Ctab = sb2("Ctab", [128, 3072], BF16)
                Stab = sb2("Stab", [128, 3072], BF16)
                psp = [ps2("psp%d" % i, [128, 512], F32) for i in range(2)]
                psr = psp[0]
                pvt = psp[1][:].bitcast(BF16).rearrange("p (s c) -> p s c", s=8)
                pss = [ps2("pss%d" % i, [128, 2, 2, 128], F32) for i in range(4)]
                pso_full = [ps2("pso%d" % i, [128, 4, 128], F32) for i in range(2)]
                pso = [t_[:, 0:2, :] for t_ in pso_full]

                rt_tmp = [acc[:, 0, 0:512], acc[:, 0, 512:1024], acc[:, 0, 1024:1536]]
                posi_t = sb2("posi", [128, 512], I32)

                def build_tables():
                    rt = rt_tmp
                    posi = posi_t[:]
                    for ch in range(6):
                        cs = slice(ch * 512, (ch + 1) * 512)
                        S.add("act", lambda e, cs=cs: e.dma_start(out=posi, in_=pos_d[:, cs]), [], ["posi"], dma="posi")
                        cp("dve", rt[0][:], posi[:], ["posi"], ["rt0"])
                        ts("dve", rt[0][:], rt[0][:], INV, 0.5, OP.mult, OP.add, ["rt0", "cstf2"], ["rt0"])
                        for off, dst in ((0.0, Stab), (0.25, Ctab)):
                            ts("dve", rt[1][:], rt[0][:], off, None, OP.add, None, ["rt0"], ["rt1"])
                            cp("dve", posi[:], rt[1][:], ["rt1"], ["posi"])
                            cp("dve", rt[2][:], posi[:], ["posi"], ["rt2"])
                            tt("dve", rt[1][:], rt[1][:], rt[2][:], OP.subtract, ["rt1", "rt2"], ["rt1"])
                            ts("dve", rt[2][:], rt[1][:], 0.0, None, OP.is_lt, None, ["rt1"], ["rt2"])
                            tt("dve", rt[1][:], rt[1][:], rt[2][:], OP.add, ["rt1", "rt2"], ["rt1"])
                            ts("dve", rt[1][:], rt[1][:], 2.0 * math.pi, -math.pi, OP.mult, OP.add, ["rt1"], ["rt1"])
                            ts("dve", rt[1][:], rt[1][:], -math.pi, math.pi, OP.max, OP.min, ["rt1"], ["rt1"])
                            act(rt[1][:], rt[1][:], AF.Sin, ["rt1"], ["rt1"])
                            if dst is Stab:
                                ts("dve", dst[:, cs], rt[1][:], SGN, None, OP.mult, None, ["rt1", "cstf2"], [("tab", ch)])
                            else:
                                cp("dve", dst[:, cs], rt[1][:], ["rt1"], [("tabc", ch)])

                S.mark('t1')

                def rope(buf, bkey, nb, tabofs):
                    for b_ in range(nb):
                        cs = slice(b_ * 512, (b_ + 1) * 512)
                        tsl = slice(tabofs + b_ * 512, tabofs + (b_ + 1) * 512)
                        tch = (tabofs + b_ * 512) // 512
                        mmgroup([mm(psr[:], RTB, buf[:, cs], True, True)], [(bkey, b_), ("cstb", B_RT)], [("psp", 0)])
                        S.mark('r1')
                        tt("dve", rt[0][:], psr[:], Stab[:, tsl], OP.mult, [("psp", 0), ("tab", tch)], ["rt0"])
                        S.mark('r2')
                        tt("dve", rt[1][:], buf[:, cs], Ctab[:, tsl], OP.mult, [(bkey, b_), ("tabc", tch)], ["rt1"])
                        S.mark('r3')
                        tt("dve", buf[:, cs], rt[0][:], rt[1][:], OP.add, ["rt0", "rt1"], [(bkey, b_)])

                nunit = 0
                S.add("pool", lambda e: e.memset(qP[:], 0.0), [], [("qP0", b_) for b_ in range(4)] + [("qP1", b_) for b_ in range(4)])
                for hp in range(8):
                    npj = 0
                    for (blk, buf, bkey, nb, t00) in ((hp, None, "qP", 4, 2048), (8 + hp, kT, "kT", 6, 1024),
                                                      (16 + hp, vT, "vT", 6, 1024)):
                        wt, wk = w_next(blk)
                        for b_ in range(nb):
                            sl = npj % 2
                            npj += 1
                            cs = slice(b_ * 512, (b_ + 1) * 512)
                            proj_fm(wt, wk, t00 + b_ * 512, psp[sl][:], ("psp", sl))
                            if b_ == 0:
                                w_prefetch_cast()
                            if buf is None:
                                cp("act", qP[0:64, 0, cs], psp[sl][0:64, :], [("psp", sl)], [("qP0", b_)])
                                cp("act", qP[64:128, 1, cs], psp[sl][64:128, :], [("psp", sl)], [("qP1", b_)])
                            else:
                                cp("act", buf[:, cs], psp[sl][:], [("psp", sl)], [(bkey, b_)])
                    S.mark('t1b')
                    if hp == 0:
                        build_tables()
                    rope(qP[:, 0, :], "qP0", 4, 1024)
                    rope(qP[:, 1, :], "qP1", 4, 1024)
                    rope(kT, "kT", 6, 0)
                    S.mark('t2')
                    units = []
                    for d, r0, r1, i0, i1 in ((1, 0, 1, 0, 8), (1, 0, 1, 8, 16), (4, 0, 1, 0, 4), (4, 1, 2, 0, 4), (4, 2, 3, 0, 4),
                                              (4, 3, 4, 0, 4), (16, 0, 4, 0, 1), (16, 4, 8, 0, 1), (16, 8, 12, 0, 1), (16, 12, 16, 0, 1)):
                        L = 4096 // d
                        nq = (L // 2) // 128
                        ntc = i1 - i0 + 1
                        tiles = []
                        for r_ in range(r0, r1):
                            for j in range(i0, i1 + 1):
                                a_s = L // 2 - 64 + 128 * j
                                n = 128 if j < nq else 64
                                tiles.append((r_ + d * a_s - 1024, n))
                        first = True
                        for r_ in range(r0, r1):
                            for i in range(i0, i1):
                                a0 = L // 2 + 128 * i
                                qs = r_ + d * a0 - 2048
                                kts = []
                                for kt_ in range(2):
                                    ti = (r_ - r0) * ntc + (i - i0) + kt_
                                    idx, n = tiles[ti]
                                    kts.append((ti, idx, n))
                                units.append(dict(d=d, qs=qs, kts=kts, tiles=tiles if first else None))
                                first = False

                    def build_vtiles(d, tiles):
                        for g0 in range(0, len(tiles), 8):
                            grp = tiles[g0:g0 + 8]
                            fns = []
                            rk = set()
                            for s_, (idx, n) in enumerate(grp):
                                fns.append((lambda e, s_=s_, idx=idx, n=n, d=d: e.transpose(
                                    pvt[0:n, s_, :], vT[:, idx:idx + d * (n - 1) + 1:d], IDB)))
                                rk.update(("vT", b_) for b_ in range(idx // 512, (idx + d * (n - 1)) // 512 + 1))
                            mmgroup(fns, list(rk) + [("cstb", B_ID)], [("psp", 1)])
                            cp("dve" if (g0 // 8) % 2 else "act", vt[:, g0:g0 + len(grp), :], pvt[:, 0:len(grp), :],
                               [("psp", 1)], [("vt", g0 // 8)])

                    def unit_front(u, un):
                        sp, se = un % 4, un % 3
                        d, qs, kts = u["d"], u["qs"], u["kts"]
                        qsl = slice(qs, qs + d * 127 + 1, d)
                        qk = [(nm, b_) for nm in ("qP0", "qP1") for b_ in range(qs // 512, (qs + d * 127) // 512 + 1)]
                        fns = []
                        kk_ = set()
                        for hh in range(2):
                            for kt_, (ti, idx, n) in enumerate(kts):
                                fns.append(mm(pss[sp][0:n, hh, kt_, :], kT[:, idx:idx + d * (n - 1) + 1:d],
                                              qP[:, hh, qsl], True, True))
                                kk_.update(("kT", b_) for b_ in range(idx // 512, (idx + d * (n - 1)) // 512 + 1))
                        mmgroup(fns, qk + list(kk_), [("pss", sp)])
                        n2 = kts[1][2]
                        e4 = pe_[se][:].rearrange("p (h k q) -> p h k q", h=2, k=2)
                        if n2 == 128:
                            act(e4, pss[sp][:], AF.Exp, [("pss", sp)], [("pe", se)], scale=0.125)
                        else:
                            act(e4[:, :, 0, :], pss[sp][:, :, 0, :], AF.Exp, [("pss", sp)], [("pe", se)], scale=0.125)
                            act(e4[0:n2, :, 1, :], pss[sp][0:n2, :, 1, :], AF.Exp, [("pss", sp)], [("pe", se)], scale=0.125)
                        tt("dve", pe_[se][:], pe_[se][:], M12, OP.mult, [("pe", se), ("cstb", B_M12)], [("pe", se)])

                    def unit_back(u, un):
                        sl, se = un % 2, un % 3
                        d, qs, kts = u["d"], u["qs"], u["kts"]
                        qsl = slice(qs, qs + d * 127 + 1, d)
                        if u["tiles"] is not None:
                            build_vtiles(d, u["tiles"])
                        e4 = pe_[se][:].rearrange("p (h k q) -> p h k q", h=2, k=2)
                        pnum = pso_full[sl][:, 0, :]
                        pden = pso_full[sl][:, 1:3, :]
                        fns = []
                        for hh in range(2):
                            for kt_, (ti, idx, n) in enumerate(kts):
                                fns.append(mm(pnum[hh * 64:(hh + 1) * 64, :], vt[0:n, ti, hh * 64:(hh + 1) * 64],
                                              e4[0:n, hh, kt_, :], kt_ == 0, kt_ == 1))
                        for kt_, (ti, idx, n) in enumerate(kts):
                            fns.append(mm(pden, ONEB[0:n, :], e4[0:n, :, kt_, :], kt_ == 0, kt_ == 1))
                        vk = list({("vt", ti // 8) for (ti, _, _) in kts})
                        mmgroup(fns, [("pe", se), ("cstb", B_ONE)] + vk, [("pso", sl)])
                        ak = [("acc", b_) for b_ in range(qs // 512, (qs + d * 127) // 512 + 1)]
                        if d == 1:
                            cp("dve", acc[:, 0, qsl], pnum, [("pso", sl)], ak)
                            cp("act", acc[0:64, 1, qsl], pden[0:64, 0, :], [("pso", sl)], ak)
                            cp("act", acc[64:128, 1, qsl], pden[64:128, 1, :], [("pso", sl)], ak)
                        else:
                            tt("dve", acc[:, 0, qsl], acc[:, 0, qsl], pnum, OP.add, [("pso", sl)] + ak, ak)
                            tt("dve", acc[0:64, 1, qsl], acc[0:64, 1, qsl], pden[0:64, 0, :], OP.add, [("pso", sl)] + ak, ak)
                            tt("dve", acc[64:128, 1, qsl], acc[64:128, 1, qsl], pden[64:128, 1, :], OP.add, [("pso", sl)] + ak, ak)

                    unit_front(units[0], nunit)
                    unit_front(units[1], nunit + 1)
                    for ui, u in enumerate(units):
                        if ui + 2 < len(units):
                            unit_front(units[ui + 2], nunit + ui + 2)
                        unit_back(u, nunit + ui)
                    nunit += len(units)
                    S.mark('t6')
                    wt, wk = w_next(24 + hp)
                    for b_ in range(4):
                        sl = b_ % 2
                        cs = slice(b_ * 512, (b_ + 1) * 512)
                        proj_fm(wt, wk, 2048 + b_ * 512, psp[sl][:], ("psp", sl))
                        if b_ == 0:
                            w_prefetch_cast()
                        act(rt[0][:], psp[sl][:], AF.Exp, [("psp", sl)], ["rt0"], scale=-1.0)
                        act(rt[0][:], rt[0][:], AF.Ln, ["rt0"], ["rt0"], bias=1.0)
                        act(rt[0][:], rt[0][:], AF.Exp, ["rt0"], ["rt0"], scale=-1.0)
                        tt("dve", rt[0][:], psp[sl][:], rt[0][:], OP.mult, [("psp", sl), "rt0"], ["rt0"])
                        act(rt[1][:], acc[:, 1, cs], AF.Ln, [("acc", b_)], ["rt1"])
                        act(rt[1][:], rt[1][:], AF.Exp, ["rt1"], ["rt1"], scale=-1.0)
                        tt("pool", rt[1][:], rt[1][:], acc[:, 0, cs], OP.mult, ["rt1", ("acc", b_)], ["rt1"])
                        tt("pool", Y3[:, hp, cs], rt[0][:], rt[1][:], OP.mult, ["rt0", "rt1"], [("yT", hp, b_)])
                S.emit()

            with contextlib.ExitStack() as s2:
                sb2 = lambda name, shape, dt: s2.enter_context(nc.sbuf_tensor(uniq(name), shape, dt))
                ps2 = lambda name, shape, dt: s2.enter_context(nc.psum_tensor(uniq(name), shape, dt))
                sq = sb2("sq", [128, 2048], BF16)
                QT = [sb2("QT%d" % i, [128, 2048], BF16) for i in range(2)]
                KTo = [sb2("KTo%d" % i, [128, 2048], BF16) for i in range(2)]
                Ktk = [sb2("Ktk%d" % i, [128, 16, 128], BF16) for i in range(2)]
                vtk = sb2("vtk", [128, 16, 128], BF16)
                oacc = sb2("oacc", [128, 2048], F32)
                dec = [sb2("dec%d" % i, [128, 32], F32) for i in range(2)]
                gt = [sb2("ogt%d" % i, [128, 512], F32) for i in range(8)]
                sqb = gt[7][:].bitcast(BF16)[:, 0:512]
                attm = [sb2("attm%d" % i, [128, 128], BF16) for i in range(2)]
                Sb = [[sb2("Sb%d%d" % (di, i), [128, 128], BF16) for i in range(3)] for di in range(2)]
                To = sb2("To", [128, 128], F32)
                psf = [ps2("psf%d" % i, [128, 512], F32) for i in range(2)]
                psv = ps2("psv", [128, 4, 128], F32)
                pskt = ps2("pskt", [128, 8, 128], BF16)
                psmd = [ps2("psm%d" % i, [128, 4, 128], F32) for i in range(2)]
                psat = [psmd[0][:, 0, :], psmd[1][:, 0, :]]
                psa_ = [psmd[0][:, 1, :], psmd[1][:, 1, :]]
                pso_ = [ps2("psoo%d" % i, [128, 512], F32) for i in range(2)]

                for h in range(8):
                    npj = 0
                    wt, wk = w_next(32 + h)
                    for b_ in range(4):
                        sl = npj % 2
                        npj += 1
                        cs = slice(b_ * 512, (b_ + 1) * 512)
                        proj_fm(wt, wk, 2048 + b_ * 512, psf[sl][:], ("psf", sl))
                        if b_ == 0:
                            w_prefetch_cast()
                        act(gt[0][:], psf[sl][:], AF.Exp, [("psf", sl)], [("g0", 0)], scale=-1.0)
                        act(gt[0][:], gt[0][:], AF.Ln, [("g0", 0)], [("g0", 0)], bias=1.0)
                        act(gt[0][:], gt[0][:], AF.Exp, [("g0", 0)], [("g0", 0)], scale=-1.0)
                        tt("dve", sq[:, cs], psf[sl][:], gt[0][:], OP.mult, [("psf", sl), ("g0", 0)], ["sq"])
                    for di, (wb, lbc, omc, mode) in enumerate(((40 + h, LBI(h), OMI(h), "in"),
                                                                (48 + h, LBO(h), OMO(h), "out"))):
                        wt, wk = w_next(wb)
                        for b_ in range(4):
                            sl = npj % 2
                            npj += 1
                            cs = slice(b_ * 512, (b_ + 1) * 512)
                            proj_fm(wt, wk, 2048 + b_ * 512, psf[sl][:], ("psf", sl))
                            if b_ == 0:
                                w_prefetch_cast()
                            gate_chain(psf[sl][:], ("psf", sl), lbc, omc, [g[:] for g in gt[4 * (b_ % 2):4 * (b_ % 2) + 4]], mode,
                                       sq[:, cs],
                                       dict(KT=KTo[di][:, cs], KTk=("KTo", di, b_), QT=QT[di][:, cs], QTk=("QT", di, b_),
                                            dec=dec[di][:, b_ * 8:b_ * 8 + 8], deck=("dec", di, b_)), par=b_ % 2)
                        for g0 in range(0, 16, 8):
                            fns = [(lambda e, j=j, di=di, g0=g0: e.transpose(pskt[:, j - g0, :], KTo[di][:, j * 128:(j + 1) * 128], IDB))
                                   for j in range(g0, g0 + 8)]
                            mmgroup(fns, [("KTo", di, g0 // 4), ("KTo", di, g0 // 4 + 1), ("cstb", B_ID)], ["pskt"])
                            cp("act", Ktk[di][:, g0:g0 + 8, :], pskt[:], ["pskt"], [("Ktk", di, g0 // 8)])
                    wt, wk = w_next(56 + h)
                    for b_ in range(4):
                        fns = []
                        for j in range(4):
                            t0 = 2048 + (b_ * 4 + j) * 128
                            fns += [mm(psv[:, j, :], uT(kc, t0, 128), wt[:, kc, :], kc == 0, kc == 15) for kc in range(16)]
                        mmgroup(fns, [wk] + ukeys(2048 + b_ * 512, 512), ["psv"])
                        if b_ == 0:
                            w_prefetch_cast()
                        cp("act", vtk[:, b_ * 4:b_ * 4 + 4, :], psv[:], ["psv"], [("vtk", b_)])
                    Abuf = [[psf[0][:, 0:128], psv[:].rearrange("p a b -> p (a b)")[:, 0:128]],
                            [psf[1][:, 0:128], pskt[:].rearrange("p a b -> p (a b)").bitcast(F32)[:, 0:128]]]
                    Akey = [[("psf", 0), "psv"], [("psf", 1), "pskt"]]
                    chunk = lambda di, k: k if di == 0 else 31 - k
                    Tst = [T32[:, h, :], To[:]]
                    Tk = [("T32", h), "To"]

                    def emit_state(di, k):
                        c = chunk(di, k)
                        j, pb = c // 2, 64 * (c % 2)
                        ab, akey = Abuf[di][k % 2], Akey[di][k % 2]
                        mmgroup([mm(ab, Ktk[di][pb:pb + 64, j, :], vtk[pb:pb + 64, j, :], True, True)],
                                [("Ktk", di, j // 8), ("vtk", j // 4)], [akey])
                        if di == 0:
                            if k == 0:
                                act(Sb[0][0][:], T32[:, h, :], AF.Copy, [("T32", h), ("dlast", h)], [("Sb", 0, 0)],
                                    scale=dlast[:, h:h + 1])
                            dprev = dlast[:, h:h + 1] if k == 0 else dec[0][:, c - 1:c]
                            stt(T32[:, h, :], T32[:, h, :], dprev, ab, OP.mult, OP.add,
                                [("T32", h), akey, ("dec", 0, max(c - 1, 0) // 8), ("dlast", h)], [("T32", h)])
                            if k < 31:
                                act(Sb[0][(k + 1) % 3][:], T32[:, h, :], AF.Copy, [("T32", h), ("dec", 0, c // 8)],
                                    [("Sb", 0, (k + 1) % 3)], scale=dec[0][:, c:c + 1])
                        else:
                            if k == 0:
                                cp("dve", To[:], ab, [akey], ["To"])
                            else:
                                stt(To[:], To[:], dec[1][:, c:c + 1], ab, OP.mult, OP.add,
                                    ["To", akey, ("dec", 1, c // 8)], ["To"])
                            if k < 31:
                                cn = c - 1
                                act(Sb[1][(k + 1) % 3][:], To[:], AF.Copy, ["To", ("dec", 1, cn // 8)],
                                    [("Sb", 1, (k + 1) % 3)], scale=dec[1][:, cn:cn + 1])

                    for di in range(2):
                        emit_state(di, 0)
                    for di in range(2):
                        emit_state(di, 1)
                    for k in range(32):
                        for di in range(2):
                            c = chunk(di, k)
                            j, pb = c // 2, 64 * (c % 2)
                            b_ = c // 8
                            first_of_tile = (c % 2 == 0) if di == 0 else (c % 2 == 1)
                            cs64 = slice(c * 64, (c + 1) * 64)
                            if first_of_tile:
                                tsl = slice(j * 128, (j + 1) * 128)
                                mmgroup([mm(psat[di], KTo[di][:, tsl], QT[di][:, tsl], True, True)],
                                        [("KTo", di, b_), ("QT", di, b_)], [("psm", di)])
                                tt("dve", attm[di][:], psat[di], HMF if di == 0 else HMB, OP.mult,
                                   [("psm", di), ("cstb", B_HMF), ("cstb", B_HMB)], [("attm", di)])
                            has_state = (di == 0) or (k > 0)
                            oc = slice((c % 8) * 64, (c % 8 + 1) * 64)
                            fns = [mm(pso_[di][:, oc], vtk[pb:pb + 64, j, :], attm[di][pb:pb + 64, pb:pb + 64], True, not has_state)]
                            rr = [("vtk", j // 4), ("attm", di)]
                            if has_state:
                                fns.append(mm(pso_[di][:, oc], Sb[di][k % 3][:], QT[di][:, cs64], False, True))
                                rr += [("Sb", di, k % 3), ("QT", di, b_)]
                            mmgroup(fns, rr, [("psoo", di)])
                            if k + 2 < 32:
                                emit_state(di, k + 2)
                            done = (c % 8 == 7) if di == 0 else (c % 8 == 0)
                            if done:
                                cs = slice(b_ * 512, (b_ + 1) * 512)
                                firstw = (b_ < 2) if di == 0 else (b_ >= 2)
                                if firstw:
                                    cp("act", oacc[:, cs], pso_[di][:], [("psoo", di)], [("oacc", b_)])
                                else:
                                    tt("dve", oacc[:, cs], oacc[:, cs], pso_[di][:], OP.add,
                                       [("psoo", di), ("oacc", b_)], [("oacc", b_)])
                    wt, wk = w_next(64 + h)
                    for b_ in range(4):
                        sl = b_ % 2
                        cs = slice(b_ * 512, (b_ + 1) * 512)
                        proj_fm(wt, wk, 2048 + b_ * 512, psf[sl][:], ("psf", sl))
                        if b_ == 0:
                            w_prefetch_cast()
                        tt("pool", sqb, oacc[:, cs], oacc[:, cs], OP.mult, [("oacc", b_)], [("g3", 1)])
                        mmgroup([mm(pso_[0][:], ONEB, sqb, True, True)], [("g3", 1), ("cstb", B_ONE)], [("psoo", 0)])
                        act(gt[1][:], pso_[0][:], AF.Ln, [("psoo", 0)], [("g1", 0)], scale=1.0 / 128, bias=EPS)
                        act(gt[1][:], gt[1][:], AF.Exp, [("g1", 0)], [("g1", 0)], scale=-0.5)
                        tt("dve", gt[1][:], gt[1][:], oacc[:, cs], OP.mult, [("g1", 0), ("oacc", b_)], [("g1", 0)])
                        act(gt[0][:], psf[sl][:], AF.Exp, [("psf", sl)], [("g0", 0)], scale=-1.0)
                        act(gt[0][:], gt[0][:], AF.Ln, [("g0", 0)], [("g0", 0)], bias=1.0)
                        act(gt[0][:], gt[0][:], AF.Exp, [("g0", 0)], [("g0", 0)], scale=-1.0)
                        tt("dve", gt[0][:], psf[sl][:], gt[0][:], OP.mult, [("psf", sl), ("g0", 0)], [("g0", 0)])
                        stt(Y3[:, 8 + h, cs], gt[1][:], ghg[:, 0:1], gt[0][:], OP.mult, OP.mult,
                            [("g1", 0), ("g0", 0), "ghg"], [("yT", 8 + h, b_)])
                S.emit()

        if debug:
            S.add("sp", lambda e: e.dma_start(out=dbg_d[:, :], in_=bufBC[:]), [("yT", k, b_) for k in range(16) for b_ in range(4)],
                  ["dbgout"], dma="dbgout")

        def load_w_res(wd, nk, wdst, stg, key):
            wv_ = wd.rearrange("(k p) n -> p k n", p=128)
            engs = ("dve", "pool", "act")
            for kc in range(nk):
                sg = kc % len(stg)
                dma(stg[sg][:], wv_[:, kc, :], [], [("wstg", sg)], ("wstg", sg))
                cp(engs[kc % 3], wdst[:, kc, :], stg[sg][:], [("wstg", sg)], [(key, kc)])

        def row_rstd(srcs, skeys, ss, sl, junk):
            o = 8 * sl
            for cg in range(4):
                S.add("act", lambda e, cg=cg, o=o: e.activation(junk[:], srcs[cg], AF.Square, accum_out=ss[:, o + cg:o + cg + 1]),
                      [skeys[cg]], ["junk", ("ss4", sl, cg)])
            S.add("dve", lambda e, o=o: e.tensor_reduce(ss[:, o + 4:o + 5], ss[:, o:o + 4], mybir.AxisListType.X, OP.add),
                  [("ss4", sl, cg) for cg in range(4)], [("ssr", sl)])
            ts("dve", ss[:, o + 5:o + 6], ss[:, o + 4:o + 5], 1.0 / 2048, EPS, OP.mult, OP.add, [("ssr", sl)], [("ss1", sl)])
            act(ss[:, o + 6:o + 7], ss[:, o + 5:o + 6], AF.Ln, [("ss1", sl)], [("ss2", sl)])
            act(ss[:, o + 7:o + 8], ss[:, o + 6:o + 7], AF.Exp, [("ss2", sl)], [("ss3", sl)], scale=-0.5)
            return ss[:, o + 7:o + 8], ("ss3", sl)

        with contextlib.ExitStack() as s2:
            sb2 = lambda name, shape, dt: s2.enter_context(nc.sbuf_tensor(uniq(name), shape, dt))
            ps2 = lambda name, shape, dt: s2.enter_context(nc.psum_tensor(uniq(name), shape, dt))
            wres = A3
            xt = [sb2("xt%d" % i, [128, 2048], F32) for i in range(2)]
            stg = [sb2("stg%d" % i, [128, 2048], F32) for i in range(3)]
            gvec = sb2("gvec", [128, 2048], F32)
            ytmp = [sb2("ytmp%d" % i, [128, 512], F32) for i in range(2)]
            ss = sb2("ss", [128, 16], F32)
            junk = sb2("junk", [128, 512], BF16)
            pyo = [[ps2("pyo%d%d" % (j, i), [128, 512], F32) for i in range(4)] for j in range(2)]

            load_w_res(wout_d, 16, wres, stg, "wres")
            dma(gvec[:], gpost_d[:, :], [], ["gvec"], "gvec")
            for tt_ in range(16):
                sl = tt_ % 2
                tsl = slice(tt_ * 128, (tt_ + 1) * 128)
                dma(xt[sl][:], x_d[2048 + tt_ * 128:2048 + (tt_ + 1) * 128, :], [], [("xt", sl)], ("xt", sl))
                for cg in range(4):
                    fns = [mm(pyo[sl][cg][:], Y3[:, kc, tsl], wres[:, kc, cg * 512:(cg + 1) * 512], kc == 0, kc == 15)
                           for kc in range(16)]
                    mmgroup(fns, [("yT", kc, tt_ // 4) for kc in range(16)] + [("wres", kc) for kc in range(16)],
                            [("pyo", sl, cg)])
                rs, rsk = row_rstd([pyo[sl][cg][:] for cg in range(4)], [("pyo", sl, cg) for cg in range(4)], ss, sl, junk)
                for cg in range(4):
                    cs = slice(cg * 512, (cg + 1) * 512)
                    stt(ytmp[cg % 2][:], pyo[sl][cg][:], rs, gvec[:, cs], OP.mult, OP.mult,
                        [("pyo", sl, cg), rsk, "gvec"], [("ytmp", cg % 2)])
                    tt("pool", xt[sl][:, cs], xt[sl][:, cs], ytmp[cg % 2][:], OP.add, [("xt", sl), ("ytmp", cg % 2)], [("xt", sl)])
                S.add("pool", lambda e, tsl=tsl, sl=sl: e.dma_start(out=out_d[tsl, :], in_=xt[sl][:]),
                      [("xt", sl)], [("out", tt_)], dma=("outw", sl))
            S.emit()

        with contextlib.ExitStack() as s2:
            sb2 = lambda name, shape, dt: s2.enter_context(nc.sbuf_tensor(uniq(name), shape, dt))
            ps2 = lambda name, shape, dt: s2.enter_context(nc.psum_tensor(uniq(name), shape, dt))
            wres = A3
            wpp = sb2("wpp", [128, 2, 2048], BF16)
            xt = [sb2("xt%d" % i, [128, 2048], F32) for i in range(3)]
            h1b = [sb2("h1b0", [128, 2048], BF16)] * 2
            h1T = [sb2("h1T%d" % i, [128, 16, 128], BF16) for i in range(2)]
            gvec = sb2("gvec", [128, 2048], F32)
            ge = [sb2("ge%d" % i, [128, 2048], F32) for i in range(2)]
            stg = ge
            ss = sb2("ss", [128, 16], F32)
            junk = sb2("junk", [128, 512], BF16)
            ytmp = [sb2("ytmp%d" % i, [128, 512], F32) for i in range(2)]
            pf = sb2("pf", [128, 256], F32)
            pb_ = sb2("pb", [128, 256], BF16)
            pT = Y3
            pg = [ps2("pg%d" % i, [128, 512], F32) for i in range(2)]
            pe2 = [ps2("pe2%d" % i, [128, 512], F32) for i in range(2)]
            ppt = ps2("ppt", [128, 8, 128], BF16)
            pst = [ps2("pst%d" % i, [128, 8, 128], BF16) for i in range(2)]

            load_w_res(wpg_d, 16, wres, stg, "wres")
            load_w_res(wpp_d, 2, wpp, stg, "wpp")
            dma(gvec[:], gple_d[:, :], [], ["gvec"], "gvec")
            for tt_ in range(16):
                tsl = slice(tt_ * 128, (tt_ + 1) * 128)
                dma(pf[:], p_d[tsl, :], [], ["pf"], "pf")
                cp("dve", pb_[:], pf[:], ["pf"], ["pb"])
                fns = [(lambda e, k=k: e.transpose(ppt[:, k, :], pb_[:, k * 128:(k + 1) * 128], IDB)) for k in range(2)]
                mmgroup(fns, ["pb", ("cstb", B_ID)], ["ppt"])
                cp("act", pT[:, 0:2, tsl], ppt[:, 0:2, :], ["ppt", "dbgout"], [("pT", tt_)])

            def o5_front(tt_):
                sl = tt_ % 2
                x3 = tt_ % 3
                tsl = slice(tt_ * 128, (tt_ + 1) * 128)
                dma(xt[x3][:], out_d[tsl, :], [("out", tt_)], [("xt", x3)], ("xt", x3))
                cp("act", h1b[sl][:], xt[x3][:], [("xt", x3)], [("h1b", 0)])
                for hf in range(2):
                    fns = [(lambda e, k=k, hf=hf, sl=sl: e.transpose(pst[hf][:, k, :], h1b[sl][:, (hf * 8 + k) * 128:(hf * 8 + k + 1) * 128], IDB))
                           for k in range(8)]
                    mmgroup(fns, [("h1b", 0), ("cstb", B_ID)], [("pst", hf)])
                    cp("dve" if hf else "act", h1T[sl][:, hf * 8:hf * 8 + 8, :], pst[hf][:], [("pst", hf)], [("h1T", sl, hf)])

            o5_front(0)
            for tt_ in range(16):
                sl = tt_ % 2
                tsl = slice(tt_ * 128, (tt_ + 1) * 128)
                if tt_ + 1 < 16:
                    o5_front(tt_ + 1)
                for cg in range(4):
                    cs = slice(cg * 512, (cg + 1) * 512)
                    s_ = cg % 2
                    fns = [mm(pg[s_][:], h1T[sl][:, kc, :], wres[:, kc, cs], kc == 0, kc == 15) for kc in range(16)]
                    mmgroup(fns, [("h1T", sl, 0), ("h1T", sl, 1)] + [("wres", kc) for kc in range(16)], [("pg", s_)])
                    fns = [mm(pe2[s_][:], pT[:, kc, tsl], wpp[:, kc, cs], kc == 0, kc == 1) for kc in range(2)]
                    mmgroup(fns, [("pT", tt_), ("wpp", 0), ("wpp", 1)], [("pe2", s_)])
                    act(ytmp[s_][:], pg[s_][:], AF.Exp, [("pg", s_)], [("ytmp", s_)], scale=-1.0)
                    act(ytmp[s_][:], ytmp[s_][:], AF.Ln, [("ytmp", s_)], [("ytmp", s_)], bias=1.0)
                    act(ytmp[s_][:], ytmp[s_][:], AF.Exp, [("ytmp", s_)], [("ytmp", s_)], scale=-1.0)
                    tt("dve", ge[sl][:, cs], ytmp[s_][:], pe2[s_][:], OP.mult, [("ytmp", s_), ("pe2", s_)], [("ge", sl, cg)])
                rs, rsk = row_rstd([ge[sl][:, cg * 512:(cg + 1) * 512] for cg in range(4)], [("ge", sl, cg) for cg in range(4)],
                                   ss, sl, junk)
                stt(ge[sl][:], ge[sl][:], rs, gvec[:], OP.mult, OP.mult, [("ge", sl, cg) for cg in range(4)] + [rsk, "gvec"],
                    [("ge", sl, cg) for cg in range(4)])
                tt("pool", ge[sl][:], ge[sl][:], xt[tt_ % 3][:], OP.add, [("xt", tt_ % 3)] + [("ge", sl, cg) for cg in range(4)],
                   [("ge", sl, cg) for cg in range(4)])
                S.add("pool", lambda e, tsl=tsl, sl=sl: e.dma_start(out=out_d[tsl, :], in_=ge[sl][:]),
                      [("ge", sl, cg) for cg in range(4)], [("out", tt_)], dma=("outw", sl))
            S.add("sp", lambda e: e.nop(), [("out", t_) for t_ in range(16)], ["fin"])
            S.emit()
    return nc


def _consts():
    c = np.zeros((128, NCST), np.float32)
    p = np.arange(128)
    c[:, C_ID:C_ID + 128] = np.eye(128)
    for m in range(128):
        e = m % 64
        if e < 8:
            c[m + 8, C_RT + m] = 1.0
        elif e < 16:
            c[m - 8, C_RT + m] = 1.0
    i = p[:, None]
    j = np.arange(128)[None, :]
    m1 = (i >= j).astype(np.float32)
    m2 = (i <= j).astype(np.float32)
    c[:, C_M12:C_M12 + 512] = np.concatenate([m1, m2, m1, m2], axis=1)
    same = (i // 64) == (j // 64)
    c[:, C_HMF:C_HMF + 128] = (same & (i <= j)).astype(np.float32)
    c[:, C_HMB:C_HMB + 128] = (same & (i >= j)).astype(np.float32)
    t = np.arange(512)
    c[:, C_RST:C_RST + 512] = (t % 64 != 0).astype(np.float32)[None, :]
    inv = np.power(np.float32(500000.0), -np.arange(0, 16, 2, dtype=np.float32) / np.float32(16)).astype(np.float32)
    e = p % 64
    c[:, C_INV] = np.where(e < 16, inv[e % 8] / np.float32(2 * np.pi), 0.0)
    c[:, C_SGN] = np.where(e < 8, -1.0, np.where(e < 16, 1.0, 0.0))
    c[:, C_ONE:C_ONE + 128] = 1.0
    return c


_NC_CACHE = {}


def kernel(x, p, positions, w_in, w_out, g_pre, g_post, g_hg, lb_fwd, lb_bwd, w_pg, w_pp, g_ple, _debug=False):
    x = np.asarray(x, np.float32)
    p = np.asarray(p, np.float32)
    positions = np.asarray(positions, np.int32)
    w_in0 = np.asarray(w_in, np.float32)[0]
    aq, ak, av, ag, hq, hff, hfb, hv, hg = [w_in0[:, s:s + 1024] for s in range(0, 9216, 1024)]

    def tile_w(cols):
        w = np.concatenate(cols, axis=1)
        return np.ascontiguousarray(w.reshape(16, 128, NB_W, 128).transpose(2, 1, 0, 3)).reshape(NB_W, 128, 2048)

    w_t = {1: tile_w([aq, ak, av, ag, hq, hff, hfb, hv, hg]), 0: tile_w([aq, ak, av, ag, hq, hfb, hff, hv, hg])}

    def lbt(a, b):
        f = lambda r: np.asarray(r, np.float32).reshape(2, 8, 128).transpose(2, 0, 1).reshape(128, 16)
        return np.ascontiguousarray(np.concatenate([f(a), f(b)], axis=1))

    lb_t = {1: lbt(lb_fwd, lb_bwd), 0: lbt(lb_bwd, lb_fwd)}
    rep = lambda v: np.ascontiguousarray(np.broadcast_to(np.asarray(v, np.float32).reshape(1, -1), (128, 2048)))
    common = {
        "w_out": np.ascontiguousarray(np.asarray(w_out, np.float32)[0]),
        "w_pg": np.ascontiguousarray(np.asarray(w_pg, np.float32)[0]),
        "w_pp": np.ascontiguousarray(np.asarray(w_pp, np.float32)[0]),
        "g_pre": rep(g_pre), "g_post": rep(g_post), "g_ple": rep(g_ple),
        "g_hg": np.ascontiguousarray(np.asarray(g_hg, np.float32).reshape(128, 1)),
        "cst": _consts(),
    }
    in_maps = []
    for c in range(8):
        b, hf = c // 2, c % 2
        if hf == 1:
            xl, pl, posl = x[b], p[0, b, 2048:], positions[b]
        else:
            xl, pl, posl = x[b, ::-1], p[0, b, 2047::-1], positions[b, ::-1]
        m = dict(common)
        m["x"] = np.ascontiguousarray(xl)
        m["p"] = np.ascontiguousarray(pl)
        m["pos"] = np.ascontiguousarray(np.broadcast_to(posl[1024:].reshape(1, 3072), (128, 3072))).astype(np.int32)
        m["w_in"] = w_t[hf]
        m["lb"] = lb_t[hf]
        in_maps.append(m)
    key = bool(_debug)
    if key not in _NC_CACHE:
        _NC_CACHE[key] = build(debug=key)
    nc = _NC_CACHE[key]
    res = run_bass_kernel_spmd(nc, in_maps, core_ids=list(range(8)))
    out = np.empty((4, 4096, 2048), np.float32)
    for c in range(8):
        b, hf = c // 2, c % 2
        o = np.asarray(res.results[c]["out"], np.float32)
        if hf == 1:
            out[b, 2048:] = o
        else:
            out[b, 2047::-1] = o
    if _debug:
        return out, [res.results[c]["dbg"] for c in range(8)]
    return out
```

```python
import contextlib
import math
import numpy as np
import concourse.bass as bass
import concourse.mybir as mybir
from concourse.bass_utils import run_bass_kernel_spmd

F32 = mybir.dt.float32
BF16 = mybir.dt.bfloat16
I32 = mybir.dt.int32
AF = mybir.ActivationFunctionType
OP = mybir.AluOpType

EPS = 1e-6
NB_W = 72
C_ID, C_RT, C_M12, C_HMF, C_HMB, C_RST, C_INV, C_SGN, C_ONE = 0, 128, 256, 768, 896, 1024, 1536, 1537, 1538
NCST = 1666
NEGBIG = -30000.0
B_ID, B_RT, B_M12, B_HMF, B_HMB, B_ONE = 0, 128, 256, 768, 896, 1024
NCB = 1152


class Sched:
    ENG = ("pe", "act", "dve", "pool", "sp")

    def __init__(self, nc, stack):
        self.nc = nc
        self.stack = stack
        self.ops = []
        self.emitted = 0
        self.last_w = {}
        self.readers = {}
        self.esem = {e: stack.enter_context(nc.semaphore("s_" + e)) for e in self.ENG}
        self.ecnt = {e: 0 for e in self.ENG}
        self.dsem = {}
        self.dcnt = {}
        self.waited = {e: {} for e in self.ENG}
        self.enabled = True
        self.nemit = 0
        self.phase_dmas = []
        self.stop_n = None

    def add(self, eng, fn, r=(), w=(), dma=None):
        if not self.enabled:
            return None
        i = len(self.ops)
        deps = set()
        for k in r:
            j = self.last_w.get(k)
            if j is not None:
                deps.add(j)
        for k in w:
            j = self.last_w.get(k)
            if j is not None:
                deps.add(j)
            deps.update(self.readers.get(k, ()))
        for k in r:
            self.readers.setdefault(k, []).append(i)
        for k in w:
            self.last_w[k] = i
            self.readers[k] = []
        if dma is not None:
            if dma not in self.dsem:
                self.dsem[dma] = self.stack.enter_context(self.nc.semaphore("d%d" % len(self.dsem)))
                self.dcnt[dma] = 0
            self.dcnt[dma] += 16
            sig = (self.dsem[dma], self.dcnt[dma], 16, None)
            self.phase_dmas.append(i)
        else:
            self.ecnt[eng] += 1
            sig = (self.esem[eng], self.ecnt[eng], 1, eng)
        self.ops.append((eng, fn, deps, sig))
        return i

    def mark(self, tag):
        if self.stop_n == tag:
            self.enabled = False

    def emit(self):
        nc = self.nc
        if self.phase_dmas and self.enabled:
            self.ecnt["sp"] += 1
            self.ops.append(("sp", lambda e: e.nop(), set(self.phase_dmas), (self.esem["sp"], self.ecnt["sp"], 1, "sp")))
        self.phase_dmas = []
        new = self.ops[self.emitted:]
        self.emitted = len(self.ops)
        self.nemit += 1
        if isinstance(self.stop_n, int) and self.nemit >= self.stop_n:
            self.enabled = False
        if not new:
            return
        per = {e: [] for e in self.ENG}
        for op in new:
            per[op[0]].append(op)
        ops = self.ops
        waited = self.waited
        hw = {"pe": "tensor", "act": "scalar", "dve": "vector", "pool": "gpsimd", "sp": "sync"}
        with nc.Block() as blk:
            for e in self.ENG:
                lst = per[e]
                if not lst:
                    continue

                def body(engine, lst=lst, e=e):
                    wd = waited[e]
                    for (_, fn, deps, sig) in lst:
                        need = {}
                        for j in deps:
                            s, v, _, pe_ = ops[j][3]
                            if pe_ == "pe" and e == "pe":
                                continue
                            key = id(s)
                            if need.get(key, (None, 0))[1] < v:
                                need[key] = (s, v)
                        for key, (s, v) in need.items():
                            if wd.get(key, 0) < v:
                                engine.wait_ge(s, v)
                                wd[key] = v
                        ins = fn(engine)
                        ins.then_inc(sig[0], sig[2])

                getattr(blk, hw[e])(body)


def build(debug=False, stop=None):
    nc = bass.Bass("TRN2", target_bir_lowering=False)
    dt_in = lambda name, shape, dt=F32: nc.dram_tensor(name, shape, dt, kind="ExternalInput").ap()
    x_d = dt_in("x", [4096, 2048])
    p_d = dt_in("p", [2048, 256])
    pos_d = dt_in("pos", [128, 3072], I32)
    win_d = dt_in("w_in", [NB_W, 128, 2048])
    wout_d = dt_in("w_out", [2048, 2048])
    wpg_d = dt_in("w_pg", [2048, 2048])
    wpp_d = dt_in("w_pp", [256, 2048])
    gpre_d = dt_in("g_pre", [128, 2048])
    gpost_d = dt_in("g_post", [128, 2048])
    gple_d = dt_in("g_ple", [128, 2048])
    ghg_d = dt_in("g_hg", [128, 1])
    lb_d = dt_in("lb", [128, 32])
    cst_d = dt_in("cst", [128, NCST])
    out_d = nc.dram_tensor("out", [2048, 2048], F32, kind="ExternalOutput").ap()
    if debug:
        dbg_d = nc.dram_tensor("dbg", [128, 16 * 2048], BF16, kind="ExternalOutput").ap()

    _cnt = [0]

    def uniq(name):
        _cnt[0] += 1
        return "%s_%d" % (name, _cnt[0])

    st = contextlib.ExitStack()
    with st:
        S = Sched(nc, st)
        S.stop_n = stop
        sb = lambda name, shape, dt: st.enter_context(nc.sbuf_tensor(name, shape, dt))
        bufA = sb("bufA", [128, 16 * 2048], BF16)
        bufBC = sb("bufBC", [128, 16 * 2048], BF16)
        cstf = sb("cstf", [128, 514], F32)
        cstb = sb("cstb", [128, NCB], BF16)
        lbv = sb("lbv", [128, 32], F32)
        ghg = sb("ghg", [128, 1], F32)
        RST = cstf[:, 0:512]
        INV = cstf[:, 512:513]
        SGN = cstf[:, 513:514]
        IDB = cstb[:, B_ID:B_ID + 128]
        RTB = cstb[:, B_RT:B_RT + 128]
        M12 = cstb[:, B_M12:B_M12 + 512]
        HMF = cstb[:, B_HMF:B_HMF + 128]
        HMB = cstb[:, B_HMB:B_HMB + 128]
        ONEB = cstb[:, B_ONE:B_ONE + 128]

        A3 = bufA[:].rearrange("p (k t) -> p k t", k=16)
        BC_lo = bufBC[:, 0:16384].rearrange("p (k t) -> p k t", k=16)
        BC_hi = bufBC[:, 16384:32768].rearrange("p (k t) -> p k t", k=16)
        Y3 = bufBC[:].rearrange("p (k t) -> p k t", k=16)

        def uT3(t0):
            if t0 >= 2048:
                return A3, t0 - 2048
            if t0 >= 1024:
                return BC_hi, t0 - 1024
            return BC_lo, t0

        def uT(kc, t0, n):
            v, o = uT3(t0)
            return v[:, kc, o:o + n]

        ukeys = lambda t0, n: [("uT", t) for t in range(t0 // 128, (t0 + n) // 128)]

        def act(out, in_, func, r, w, scale=1.0, bias=0.0):
            S.add("act", lambda e: e.activation(out, in_, func, bias=bias, scale=scale), r, w)

        def tt(eng, out, a, b, op, r, w):
            S.add(eng, lambda e: e.tensor_tensor(out, a, b, op), r, w)

        def ts(eng, out, a, s1, s2, op0, op1, r, w):
            if s2 is None:
                S.add(eng, lambda e: e.tensor_scalar(out, a, s1, None, op0), r, w)
            else:
                S.add(eng, lambda e: e.tensor_scalar(out, a, s1, s2, op0, op1), r, w)

        def stt(out, a, sc, b, op0, op1, r, w, accum=None):
            if accum is None:
                S.add("dve", lambda e: e.scalar_tensor_tensor(out, a, sc, b, op0, op1), r, w)
            else:
                S.add("dve", lambda e: e.scalar_tensor_tensor(out, a, sc, b, op0, op1, accum_out=accum), r, w)

        def cp(eng, out, in_, r, w):
            if eng == "act":
                S.add("act", lambda e: e.activation(out, in_, AF.Copy), r, w)
            else:
                S.add(eng, lambda e: e.tensor_copy(out, in_), r, w)

        def recip(out, in_, r, w):
            S.add("dve", lambda e: e.reciprocal(out, in_), r, w)

        def dma(out, in_, r, w, key):
            S.add("sp", lambda e: e.dma_start(out=out, in_=in_), r, w, dma=key)

        def mm(out, lhsT, rhs, start, stop):
            return lambda e: e.matmul(out, lhsT, rhs, start=start, stop=stop)

        def mmgroup(fns, r, w):
            def f(e):
                ins = None
                for g in fns:
                    ins = g(e)
                return ins
            S.add("pe", f, r, w)

        def sigmoid_chain(dst, src, r_src, tmp, tk):
            act(tmp, src, AF.Exp, r_src, [tk], scale=-1.0)
            ts("dve", tmp, tmp, 1.0, None, OP.add, None, [tk], [tk])
            recip(dst, tmp, [tk], [tk] if dst is tmp else [tk + ("o",)])

        with nc.sbuf_tensor("cst0", [128, NCST], F32) as cst0, nc.sbuf_tensor("lb0", [128, 32], F32) as lb0:
            dma(cst0[:], cst_d[:, :], [], ["cst0"], "cst0")
            dma(lb0[:], lb_d[:, :], [], ["lb0"], "lb0")
            dma(ghg[:], ghg_d[:, :], [], ["ghg"], "ghg")
            cp("dve", cstf[:, 0:512], cst0[:, C_RST:C_RST + 512], ["cst0"], ["cstf"])
            cp("dve", cstf[:, 512:514], cst0[:, C_INV:C_INV + 2], ["cst0"], ["cstf2"])
            for (bo, co, n) in ((B_ID, C_ID, 128), (B_RT, C_RT, 128), (B_M12, C_M12, 512),
                                (B_HMF, C_HMF, 128), (B_HMB, C_HMB, 128), (B_ONE, C_ONE, 128)):
                cp("dve", cstb[:, bo:bo + n], cst0[:, co:co + n], ["cst0"], [("cstb", bo)])
            for d in range(2):
                o = d * 16
                tt("dve", lb0[:, o:o + 8], lb0[:, o:o + 8], lb0[:, o + 8:o + 16], OP.subtract, ["lb0"], ["lb0"])
                act(lb0[:, o + 8:o + 16], lb0[:, o:o + 8], AF.Exp, ["lb0"], ["lb0"], scale=-1.0)
                ts("dve", lb0[:, o + 8:o + 16], lb0[:, o + 8:o + 16], 1.0, None, OP.add, None, ["lb0"], ["lb0"])
                recip(lbv[:, o:o + 8], lb0[:, o + 8:o + 16], ["lb0"], ["lbv"])
                ts("dve", lbv[:, o + 8:o + 16], lbv[:, o:o + 8], -1.0, 1.0, OP.mult, OP.add, ["lbv"], ["lbv"])
            S.emit()

        LBI = lambda h: lbv[:, h:h + 1]
        OMI = lambda h: lbv[:, 8 + h:9 + h]
        LBO = lambda h: lbv[:, 16 + h:17 + h]
        OMO = lambda h: lbv[:, 24 + h:25 + h]

        with contextlib.ExitStack() as s1:
            sb1 = lambda name, shape, dt: s1.enter_context(nc.sbuf_tensor(name, shape, dt))
            T32 = sb1("T32", [128, 8, 128], F32)
            dlast = sb1("dlast", [128, 8], F32)
            stage = sb1("stage", [128, 1024], F32)
            wbf = [sb1("wbf%d" % i, [128, 16, 128], BF16) for i in range(2)]

            wseq = []
            for h in range(8):
                wseq += [40 + h, 56 + h]
            for hp in range(8):
                wseq += [hp, 8 + hp, 16 + hp, 24 + hp]
            for h in range(8):
                wseq += [32 + h, 40 + h, 48 + h, 56 + h, 64 + h]
            wstate = {"loaded": 0, "cast": 0, "use": 0}

            def w_load():
                n = wstate["loaded"]
                if n >= 2 * len(wseq):
                    return
                hb = n % 2
                dma(stage[:], win_d[wseq[n // 2]][:, hb * 1024:(hb + 1) * 1024], [], ["stage"], "stage")
                wstate["loaded"] += 1

            def w_cast():
                n = wstate["cast"]
                if n >= 2 * len(wseq):
                    return
                blk, hb = n // 2, n % 2
                cp("pool", wbf[blk % 2][:, hb * 8:(hb + 1) * 8, :].rearrange("p k c -> p (k c)"), stage[:], ["stage"],
                   [("wbf", blk % 2)])
                wstate["cast"] += 1

            def w_upto(nh):
                while wstate["cast"] < nh and wstate["cast"] < 2 * len(wseq):
                    if wstate["loaded"] <= wstate["cast"]:
                        w_load()
                    w_cast()
                if wstate["loaded"] <= wstate["cast"]:
                    w_load()

            def w_next(expect):
                n = wstate["use"]
                assert wseq[n] == expect, (n, wseq[n], expect)
                w_upto(2 * n + 2)
                wstate["use"] += 1
                return wbf[n % 2], ("wbf", n % 2)

            def w_prefetch_cast():
                w_upto(2 * wstate["use"] + 2)

            def proj_fm(wt, wk, t0, ps, pk):
                fns = [mm(ps, wt[:, kc, :], uT(kc, t0, 512), kc == 0, kc == 15) for kc in range(16)]
                mmgroup(fns, [wk] + ukeys(t0, 512), [pk])

            def norm_phase(tiles, gpre, xt, xb, ss, pst, junk):
                for n_, ti in enumerate(tiles):
                    sl = n_ % 2
                    t0 = ti * 128
                    dma(xt[sl][:], x_d[t0:t0 + 128, :], [], [("xt", sl)], ("xt", sl))
                    stt(junk[:], xt[sl][:], 1.0, xt[sl][:], OP.mult, OP.mult, [("xt", sl)], ["junk", ("ss", sl)],
                        accum=ss[:, 4 * sl:4 * sl + 1])
                    ts("dve", ss[:, 4 * sl + 1:4 * sl + 2], ss[:, 4 * sl:4 * sl + 1], 1.0 / 2048, EPS, OP.mult, OP.add,
                       [("ss", sl)], [("ss1", sl)])
                    act(ss[:, 4 * sl + 2:4 * sl + 3], ss[:, 4 * sl + 1:4 * sl + 2], AF.Ln, [("ss1", sl)], [("ss2", sl)])
                    act(ss[:, 4 * sl + 3:4 * sl + 4], ss[:, 4 * sl + 2:4 * sl + 3], AF.Exp, [("ss2", sl)], [("ss3", sl)],
                        scale=-0.5)
                    stt(xb[sl][:], xt[sl][:], ss[:, 4 * sl + 3:4 * sl + 4], gpre[:], OP.mult, OP.mult,
                        [("xt", sl), ("ss3", sl), "gpre"], [("xb", sl)])
                    v3, o = uT3(t0)
                    for hf in range(2):
                        fns = [(lambda e, k=k, hf=hf, sl=sl: e.transpose(pst[hf][:, k, :], xb[sl][:, (hf * 8 + k) * 128:(hf * 8 + k + 1) * 128], IDB))
                               for k in range(8)]
                        mmgroup(fns, [("xb", sl), ("cstb", B_ID)], [("pst", hf)])
                        cp("act" if hf == 0 else "pool" if False else "act", v3[:, hf * 8:hf * 8 + 8, o:o + 128], pst[hf][:],
                           [("pst", hf)], [("uT", ti)])

            def gate_chain(psf, pkf, lbc, omc, tmp, mode, sq, outs, par=0):
                t0_, t1_, t2_, t3_ = tmp
                act(t0_, psf, AF.Exp, [pkf], [("g0", par)], scale=-1.0)
                act(t0_, t0_, AF.Ln, [("g0", par)], [("g0", par)], bias=1.0)
                act(t0_, t0_, AF.Exp, [("g0", par)], [("g0", par)], scale=-1.0)
                ts("dve", t1_, t0_, omc, lbc, OP.mult, OP.add, [("g0", par), "lbv"], [("g1", par)])
                act(t2_, t1_, AF.Ln, [("g1", par)], [("g2", par)])
                ts("pool", t1_, t1_, -1.0, 1.0, OP.mult, OP.add, [("g1", par)], [("g1", par)])
                S.add("dve", lambda e: e.tensor_tensor_scan(t3_, RST, t2_, 0.0, OP.mult, OP.add),
                      [("g2", par), "cstf"], [("g3", par)])
                act(outs["dec"], t3_[:, 63:512:64], AF.Exp, [("g3", par)], [outs["deck"]])
                if mode == "out":
                    tt("pool", t3_, t3_, t2_, OP.subtract, [("g3", par), ("g2", par)], [("g3", par)])
                if mode in ("halo", "in"):
                    act(t0_, t3_, AF.Exp, [("g3", par)], [("g0", par)], scale=-1.0)
                    tt("pool", outs["KT"], t1_, t0_, OP.mult, [("g1", par), ("g0", par)], [outs["KTk"]])
                    if mode == "in":
                        act(t2_, t3_, AF.Exp, [("g3", par)], [("g2", par)])
                        tt("dve", outs["QT"], sq, t2_, OP.mult, [("g2", par), "sq"], [outs["QTk"]])
                else:
                    act(t0_, t3_, AF.Exp, [("g3", par)], [("g0", par)])
                    tt("pool", outs["KT"], t1_, t0_, OP.mult, [("g1", par), ("g0", par)], [outs["KTk"]])
                    act(t2_, t3_, AF.Exp, [("g3", par)], [("g2", par)], scale=-1.0)
                    tt("dve", outs["QT"], sq, t2_, OP.mult, [("g2", par), "sq"], [outs["QTk"]])

            with contextlib.ExitStack() as s2:
                sb2 = lambda name, shape, dt: s2.enter_context(nc.sbuf_tensor(uniq(name), shape, dt))
                ps2 = lambda name, shape, dt: s2.enter_context(nc.psum_tensor(uniq(name), shape, dt))
                gpre = sb2("gpre", [128, 2048], F32)
                xt = [sb2("xt%d" % i, [128, 2048], F32) for i in range(2)]
                xb = [sb2("xb%d" % i, [128, 2048], BF16) for i in range(2)]
                junk = sb2("junk", [128, 2048], BF16)
                ss = sb2("ss", [128, 8], F32)
                KT = [sb2("hKT%d" % i, [128, 512], BF16) for i in range(2)]
                HK = sb2("HK", [128, 16, 128], BF16)
                dech = sb2("dech", [128, 8, 32], F32)
                vtok = [sb2("hvtok%d" % i, [128, 4, 128], BF16) for i in range(2)]
                gt = [sb2("hgt%d" % i, [128, 512], F32) for i in range(4)]
                pst = [ps2("pst%d" % i, [128, 8, 128], BF16) for i in range(2)]
                psf = [ps2("psf%d" % i, [128, 512], F32) for i in range(2)]
                psv = ps2("psv", [128, 4, 128], F32)
                pskt = ps2("pskt", [128, 8, 128], BF16)
                psa = [ps2("psa%d" % i, [128, 512], F32) for i in range(2)]

                dma(gpre[:], gpre_d[:, :], [], ["gpre"], "gpre")
                w_load()
                norm_phase(list(range(16)), gpre, xt, xb, ss, pst, junk)
                nblk = 0
                for h in range(8):
                    wf, wfk = w_next(40 + h)
                    wv, wvk = w_next(56 + h)
                    for tb in range(4):
                        sl = nblk % 2
                        nblk += 1
                        proj_fm(wf, wfk, tb * 512, psf[sl][:], ("psf", sl))
                        gate_chain(psf[sl][:], ("psf", sl), LBI(h), OMI(h), [g[:] for g in gt], "halo", None,
                                   dict(KT=KT[sl][:], KTk=("hKT", sl), dec=dech[:, h, tb * 8:tb * 8 + 8],
                                        deck=("dech", h, tb)))
                        fns = []
                        for j in range(4):
                            t0 = (tb * 4 + j) * 128
                            fns += [mm(psv[:, j, :], uT(kc, t0, 128), wv[:, kc, :], kc == 0, kc == 15) for kc in range(16)]
                        mmgroup(fns, [wvk] + ukeys(tb * 512, 512), ["psv"])
                        vs = tb % 2
                        cp("act", vtok[vs][:], psv[:], ["psv"], [("hvtok", vs)])
                        if tb == 3:
                            w_prefetch_cast()
                        fns = [(lambda e, j=j, sl=sl: e.transpose(pskt[:, j, :], KT[sl][:, j * 128:(j + 1) * 128], IDB))
                               for j in range(4)]
                        mmgroup(fns, [("hKT", sl), ("cstb", B_ID)], ["pskt"])
                        cp("act", HK[:, tb * 4:tb * 4 + 4, :], pskt[:, 0:4, :], ["pskt"], [("HK", tb)])
                        for c in range(8):
                            j, pb = c // 2, 64 * (c % 2)
                            cg = tb * 8 + c
                            pa = psa[cg % 2]
                            mmgroup([mm(pa[:, 0:128], HK[pb:pb + 64, tb * 4 + j, :], vtok[vs][pb:pb + 64, j, :], True, True)],
                                    [("HK", tb), ("hvtok", vs)], [("psa", cg % 2)])
                            if cg == 0:
                                cp("dve", T32[:, h, :], pa[:, 0:128], [("psa", cg % 2)], [("T32", h)])
                            else:
                                stt(T32[:, h, :], T32[:, h, :], dech[:, h, cg - 1:cg], pa[:, 0:128], OP.mult, OP.add,
                                    [("T32", h), ("psa", cg % 2), ("dech", h, (cg - 1) // 8)], [("T32", h)])
                    cp("dve", dlast[:, h:h + 1], dech[:, h, 31:32], [("dech", h, 3)], [("dlast", h)])
                S.emit()

            with contextlib.ExitStack() as s2:
                sb2 = lambda name, shape, dt: s2.enter_context(nc.sbuf_tensor(uniq(name), shape, dt))
                ps2 = lambda name, shape, dt: s2.enter_context(nc.psum_tensor(uniq(name), shape, dt))
                gpre = sb2("gpre", [128, 2048], F32)
                xt = [sb2("xt%d" % i, [128, 2048], F32) for i in range(2)]
                xb = [sb2("xb%d" % i, [128, 2048], BF16) for i in range(2)]
                junk = sb2("junk", [128, 2048], BF16)
                ss = sb2("ss", [128, 8], F32)
                pst = [ps2("pst%d" % i, [128, 8, 128], BF16) for i in range(2)]
                dma(gpre[:], gpre_d[:, :], [], ["gpre"], "gpre")
                norm_phase(list(range(16, 32)), gpre, xt, xb, ss, pst, junk)
                S.emit()

            with contextlib.ExitStack() as s2:
                sb2 = lambda name, shape, dt: s2.enter_context(nc.sbuf_tensor(uniq(name), shape, dt))
                ps2 = lambda name, shape, dt: s2.enter_context(nc.psum_tensor(uniq(name), shape, dt))
                qP = sb2("qP", [128, 2, 2048], BF16)
                kT = sb2("kT", [128, 3072], BF16)
                vT = sb2("vT", [128, 3072], BF16)
                vt = sb2("vt", [128, 9, 128], BF16)
                acc = sb2("acc", [128, 2, 2048], F32)
                pe_ = [sb2("pe%d" % i, [128, 512], BF16) for i in range(3)]
                rt = [sb2("rt%d" % i, [128, 512], F32) for i in range(2)]
                Ctab = sb2("Ctab", [128, 3072], BF16)
                Stab = sb2("Stab", [128, 3072], BF16)
                psp = [ps2("psp%d" % i, [128, 512], F32) for i in range(2)]
                pvt = psp[1][:].bitcast(BF16).rearrange("p (s c) -> p s c", s=8)
                pss = [ps2("pss%d" % i, [128, 2, 2, 128], F32) for i in range(3)]
                psr = ps2("psr", [128, 512], F32)
                pso_full = [ps2("pso%d" % i, [128, 4, 128], F32) for i in range(2)]
                pso = [t_[:, 0:2, :] for t_ in pso_full]

                rt_tmp = [acc[:, 0, 0:512], acc[:, 0, 512:1024], acc[:, 0, 1024:1536]]
                posi_t = sb2("posi", [128, 512], I32)

                def build_table_chunk(ch):
                    rt = rt_tmp
                    posi = posi_t[:]
                    if True:
                        cs = slice(ch * 512, (ch + 1) * 512)
                        S.add("act", lambda e, cs=cs: e.dma_start(out=posi, in_=pos_d[:, cs]), [], ["posi"], dma="posi")
                        cp("dve", rt[0][:], posi[:], ["posi"], ["rt0"])
                        ts("dve", rt[0][:], rt[0][:], INV, 0.5, OP.mult, OP.add, ["rt0", "cstf2"], ["rt0"])
                        for off, dst in ((0.0, Stab), (0.25, Ctab)):
                            ts("dve", rt[1][:], rt[0][:], off, None, OP.add, None, ["rt0"], ["rt1"])
                            cp("dve", posi[:], rt[1][:], ["rt1"], ["posi"])
                            cp("dve", rt[2][:], posi[:], ["posi"], ["rt2"])
                            tt("dve", rt[1][:], rt[1][:], rt[2][:], OP.subtract, ["rt1", "rt2"], ["rt1"])
                            ts("dve", rt[2][:], rt[1][:], 0.0, None, OP.is_lt, None, ["rt1"], ["rt2"])
                            tt("dve", rt[1][:], rt[1][:], rt[2][:], OP.add, ["rt1", "rt2"], ["rt1"])
                            ts("dve", rt[1][:], rt[1][:], 2.0 * math.pi, -math.pi, OP.mult, OP.add, ["rt1"], ["rt1"])
                            ts("dve", rt[1][:], rt[1][:], -math.pi, math.pi, OP.max, OP.min, ["rt1"], ["rt1"])
                            act(rt[1][:], rt[1][:], AF.Sin, ["rt1"], ["rt1"])
                            if dst is Stab:
                                ts("dve", dst[:, cs], rt[1][:], SGN, None, OP.mult, None, ["rt1", "cstf2"], [("tab", ch)])
                            else:
                                cp("dve", dst[:, cs], rt[1][:], ["rt1"], [("tabc", ch)])

                S.mark('t1')

                def rope(buf, bkey, nb, tabofs):
                    for b_ in range(nb):
                        cs = slice(b_ * 512, (b_ + 1) * 512)
                        tsl = slice(tabofs + b_ * 512, tabofs + (b_ + 1) * 512)
                        tch = (tabofs + b_ * 512) // 512
                        mmgroup([mm(psr[:], RTB, buf[:, cs], True, True)], [(bkey, b_), ("cstb", B_RT)], ["psr"])
                        S.mark('r1')
                        tt("dve", rt[0][:], psr[:], Stab[:, tsl], OP.mult, ["psr", ("tab", tch)], ["rt0"])
                        S.mark('r2')
                        tt("dve", rt[1][:], buf[:, cs], Ctab[:, tsl], OP.mult, [(bkey, b_), ("tabc", tch)], ["rt1"])
                        S.mark('r3')
                        tt("dve", buf[:, cs], rt[0][:], rt[1][:], OP.add, ["rt0", "rt1"], [(bkey, b_)])

                nunit = 0
                S.add("pool", lambda e: e.memset(qP[:], 0.0), [], [("qP0", b_) for b_ in range(4)] + [("qP1", b_) for b_ in range(4)])
                nb0 = 0
                for hp in range(8):
                    npj = 0
                    for (blk, buf, bkey, nb, t00) in ((hp, None, "qP", 4, 2048), (8 + hp, kT, "kT", 6, 1024),
                                                      (16 + hp, vT, "vT", 6, 1024)):
                        wt, wk = w_next(blk)
                        for b_ in range(nb):
                            sl = npj % 2
                            npj += 1
                            cs = slice(b_ * 512, (b_ + 1) * 512)
                            proj_fm(wt, wk, t00 + b_ * 512, psp[sl][:], ("psp", sl))
                            if b_ == 0:
                                w_prefetch_cast()
                            if buf is None:
                                cp("act", qP[0:64, 0, cs], psp[sl][0:64, :], [("psp", sl)], [("qP0", b_)])
                                cp("act", qP[64:128, 1, cs], psp[sl][64:128, :], [("psp", sl)], [("qP1", b_)])
                            else:
                                cp("act", buf[:, cs], psp[sl][:], [("psp", sl)], [(bkey, b_)])
                            if hp == 0:
                                nb0 += 1
                                if nb0 in (1, 3, 5, 7, 9, 11):
                                    build_table_chunk(nb0 // 2)
                        if hp > 0:
                            if bkey == "qP":
                                rope(qP[:, 0, :], "qP0", 4, 1024)
                                rope(qP[:, 1, :], "qP1", 4, 1024)
                            elif bkey == "kT":
                                rope(kT, "kT", 6, 0)
                    S.mark('t1b')
                    if hp == 0:
                        rope(qP[:, 0, :], "qP0", 4, 1024)
                        rope(qP[:, 1, :], "qP1", 4, 1024)
                        rope(kT, "kT", 6, 0)
                    S.mark('t2')
                    units = []
                    for d, r0, r1, i0, i1 in ((1, 0, 1, 0, 8), (1, 0, 1, 8, 16), (4, 0, 1, 0, 4), (4, 1, 2, 0, 4), (4, 2, 3, 0, 4),
                                              (4, 3, 4, 0, 4), (16, 0, 4, 0, 1), (16, 4, 8, 0, 1), (16, 8, 12, 0, 1), (16, 12, 16, 0, 1)):
                        L = 4096 // d
                        nq = (L // 2) // 128
                        ntc = i1 - i0 + 1
                        tiles = []
                        for r_ in range(r0, r1):
                            for j in range(i0, i1 + 1):
                                a_s = L // 2 - 64 + 128 * j
                                n = 128 if j < nq else 64
                                tiles.append((r_ + d * a_s - 1024, n))
                        first = True
                        for r_ in range(r0, r1):
                            for i in range(i0, i1):
                                a0 = L // 2 + 128 * i
                                qs = r_ + d * a0 - 2048
                                kts = []
                                for kt_ in range(2):
                                    ti = (r_ - r0) * ntc + (i - i0) + kt_
                                    idx, n = tiles[ti]
                                    kts.append((ti, idx, n))
                                units.append(dict(d=d, qs=qs, kts=kts, tiles=tiles if first else None))
                                first = False

                    def build_vtiles(d, tiles):
                        for g0 in range(0, len(tiles), 8):
                            grp = tiles[g0:g0 + 8]
                            fns = []
                            rk = set()
                            for s_, (idx, n) in enumerate(grp):
                                fns.append((lambda e, s_=s_, idx=idx, n=n, d=d: e.transpose(
                                    pvt[0:n, s_, :], vT[:, idx:idx + d * (n - 1) + 1:d], IDB)))
                                rk.update(("vT", b_) for b_ in range(idx // 512, (idx + d * (n - 1)) // 512 + 1))
                            mmgroup(fns, list(rk) + [("cstb", B_ID)], [("psp", 1)])
                            cp("dve" if (g0 // 8) % 2 else "act", vt[:, g0:g0 + len(grp), :], pvt[:, 0:len(grp), :],
                               [("psp", 1)], [("vt", g0 // 8)])

                    def unit_front(u, un):
                        sp, se = un % 3, un % 3
                        d, qs, kts = u["d"], u["qs"], u["kts"]
                        qsl = slice(qs, qs + d * 127 + 1, d)
                        qk = [(nm, b_) for nm in ("qP0", "qP1") for b_ in range(qs // 512, (qs + d * 127) // 512 + 1)]
                        fns = []
                        kk_ = set()
                        for hh in range(2):
                            for kt_, (ti, idx, n) in enumerate(kts):
                                fns.append(mm(pss[sp][0:n, hh, kt_, :], kT[:, idx:idx + d * (n - 1) + 1:d],
                                              qP[:, hh, qsl], True, True))
                                kk_.update(("kT", b_) for b_ in range(idx // 512, (idx + d * (n - 1)) // 512 + 1))
                        mmgroup(fns, qk + list(kk_), [("pss", sp)])
                        n2 = kts[1][2]
                        e4 = pe_[se][:].rearrange("p (h k q) -> p h k q", h=2, k=2)
                        if n2 == 128:
                            act(e4, pss[sp][:], AF.Exp, [("pss", sp)], [("pe", se)], scale=0.125)
                        else:
                            act(e4[:, :, 0, :], pss[sp][:, :, 0, :], AF.Exp, [("pss", sp)], [("pe", se)], scale=0.125)
                            act(e4[0:n2, :, 1, :], pss[sp][0:n2, :, 1, :], AF.Exp, [("pss", sp)], [("pe", se)], scale=0.125)
                        tt("dve", pe_[se][:], pe_[se][:], M12, OP.mult, [("pe", se), ("cstb", B_M12)], [("pe", se)])

                    def unit_back(u, un):
                        sl, se = un % 2, un % 3
                        d, qs, kts = u["d"], u["qs"], u["kts"]
                        qsl = slice(qs, qs + d * 127 + 1, d)
                        if u["tiles"] is not None:
                            build_vtiles(d, u["tiles"])
                        e4 = pe_[se][:].rearrange("p (h k q) -> p h k q", h=2, k=2)
                        pnum = pso_full[sl][:, 0, :]
                        pden = pso_full[sl][:, 1:3, :]
                        fns = []
                        for hh in range(2):
                            for kt_, (ti, idx, n) in enumerate(kts):
                                fns.append(mm(pnum[hh * 64:(hh + 1) * 64, :], vt[0:n, ti, hh * 64:(hh + 1) * 64],
                                              e4[0:n, hh, kt_, :], kt_ == 0, kt_ == 1))
                        for kt_, (ti, idx, n) in enumerate(kts):
                            fns.append(mm(pden, ONEB[0:n, :], e4[0:n, :, kt_, :], kt_ == 0, kt_ == 1))
                        vk = list({("vt", ti // 8) for (ti, _, _) in kts})
                        mmgroup(fns, [("pe", se), ("cstb", B_ONE)] + vk, [("pso", sl)])
                        ak = [("acc", b_) for b_ in range(qs // 512, (qs + d * 127) // 512 + 1)]
                        if d == 1:
                            cp("dve", acc[:, 0, qsl], pnum, [("pso", sl)], ak)
                            cp("act", acc[0:64, 1, qsl], pden[0:64, 0, :], [("pso", sl)], ak)
                            cp("act", acc[64:128, 1, qsl], pden[64:128, 1, :], [("pso", sl)], ak)
                        else:
                            tt("dve", acc[:, 0, qsl], acc[:, 0, qsl], pnum, OP.add, [("pso", sl)] + ak, ak)
                            tt("dve", acc[0:64, 1, qsl], acc[0:64, 1, qsl], pden[0:64, 0, :], OP.add, [("pso", sl)] + ak, ak)
                            tt("dve", acc[64:128, 1, qsl], acc[64:128, 1, qsl], pden[64:128, 1, :], OP.add, [("pso", sl)] + ak, ak)

                    unit_front(units[0], nunit)
                    unit_front(units[1], nunit + 1)
                    for ui, u in enumerate(units):
                        if ui + 2 < len(units):
                            unit_front(units[ui + 2], nunit + ui + 2)
                        unit_back(u, nunit + ui)
                    nunit += len(units)
                    S.mark('t6')
                    wt, wk = w_next(24 + hp)
                    for b_ in range(4):
                        sl = b_ % 2
                        cs = slice(b_ * 512, (b_ + 1) * 512)
                        proj_fm(wt, wk, 2048 + b_ * 512, psp[sl][:], ("psp", sl))
                        if b_ == 0:
                            w_prefetch_cast()
                        act(rt[0][:], psp[sl][:], AF.Exp, [("psp", sl)], ["rt0"], scale=-1.0)
                        act(rt[0][:], rt[0][:], AF.Ln, ["rt0"], ["rt0"], bias=1.0)
                        act(rt[0][:], rt[0][:], AF.Exp, ["rt0"], ["rt0"], scale=-1.0)
                        tt("dve", rt[0][:], psp[sl][:], rt[0][:], OP.mult, [("psp", sl), "rt0"], ["rt0"])
                        act(rt[1][:], acc[:, 1, cs], AF.Ln, [("acc", b_)], ["rt1"])
                        act(rt[1][:], rt[1][:], AF.Exp, ["rt1"], ["rt1"], scale=-1.0)
                        tt("pool", rt[1][:], rt[1][:], acc[:, 0, cs], OP.mult, ["rt1", ("acc", b_)], ["rt1"])
                        tt("pool", Y3[:, hp, cs], rt[0][:], rt[1][:], OP.mult, ["rt0", "rt1"], [("yT", hp, b_)])
                S.emit()

            with contextlib.ExitStack() as s2:
                sb2 = lambda name, shape, dt: s2.enter_context(nc.sbuf_tensor(uniq(name), shape, dt))
                ps2 = lambda name, shape, dt: s2.enter_context(nc.psum_tensor(uniq(name), shape, dt))
                sq = sb2("sq", [128, 2048], BF16)
                QT = [sb2("QT%d" % i, [128, 2048], BF16) for i in range(2)]
                KTo = [sb2("KTo%d" % i, [128, 2048], BF16) for i in range(2)]
                Ktk = [sb2("Ktk%d" % i, [128, 16, 128], BF16) for i in range(2)]
                vtk = sb2("vtk", [128, 16, 128], BF16)
                oacc = sb2("oacc", [128, 2048], F32)
                dec = [sb2("dec%d" % i, [128, 32], F32) for i in range(2)]
                gt = [sb2("ogt%d" % i, [128, 512], F32) for i in range(8)]
                sqb = gt[7][:].bitcast(BF16)[:, 0:512]
                attm = [sb2("attm%d" % i, [128, 128], BF16) for i in range(2)]
                Sb = [[sb2("Sb%d%d" % (di, i), [128, 128], BF16) for i in range(3)] for di in range(2)]
                To = sb2("To", [128, 128], F32)
                psf = [ps2("psf%d" % i, [128, 512], F32) for i in range(2)]
                psv = ps2("psv", [128, 4, 128], F32)
                pskt = ps2("pskt", [128, 8, 128], BF16)
                psmd = [ps2("psm%d" % i, [128, 4, 128], F32) for i in range(2)]
                psat = [psmd[0][:, 0, :], psmd[1][:, 0, :]]
                psa_ = [psmd[0][:, 1, :], psmd[1][:, 1, :]]
                pso_ = [ps2("psoo%d" % i, [128, 512], F32) for i in range(2)]

                for h in range(8):
                    npj = 0
                    wt, wk = w_next(32 + h)
                    for b_ in range(4):
                        sl = npj % 2
                        npj += 1
                        cs = slice(b_ * 512, (b_ + 1) * 512)
                        proj_fm(wt, wk, 2048 + b_ * 512, psf[sl][:], ("psf", sl))
                        if b_ == 0:
                            w_prefetch_cast()
                        act(gt[0][:], psf[sl][:], AF.Exp, [("psf", sl)], [("g0", 0)], scale=-1.0)
                        act(gt[0][:], gt[0][:], AF.Ln, [("g0", 0)], [("g0", 0)], bias=1.0)
                        act(gt[0][:], gt[0][:], AF.Exp, [("g0", 0)], [("g0", 0)], scale=-1.0)
                        tt("dve", sq[:, cs], psf[sl][:], gt[0][:], OP.mult, [("psf", sl), ("g0", 0)], ["sq"])
                    for di, (wb, lbc, omc, mode) in enumerate(((40 + h, LBI(h), OMI(h), "in"),
                                                                (48 + h, LBO(h), OMO(h), "out"))):
                        wt, wk = w_next(wb)
                        for b_ in range(4):
                            sl = npj % 2
                            npj += 1
                            cs = slice(b_ * 512, (b_ + 1) * 512)
                            proj_fm(wt, wk, 2048 + b_ * 512, psf[sl][:], ("psf", sl))
                            if b_ == 0:
                                w_prefetch_cast()
                            gate_chain(psf[sl][:], ("psf", sl), lbc, omc, [g[:] for g in gt[4 * (b_ % 2):4 * (b_ % 2) + 4]], mode,
                                       sq[:, cs],
                                       dict(KT=KTo[di][:, cs], KTk=("KTo", di, b_), QT=QT[di][:, cs], QTk=("QT", di, b_),
                                            dec=dec[di][:, b_ * 8:b_ * 8 + 8], deck=("dec", di, b_)), par=b_ % 2)
                        for g0 in range(0, 16, 8):
                            fns = [(lambda e, j=j, di=di, g0=g0: e.transpose(pskt[:, j - g0, :], KTo[di][:, j * 128:(j + 1) * 128], IDB))
                                   for j in range(g0, g0 + 8)]
                            mmgroup(fns, [("KTo", di, g0 // 4), ("KTo", di, g0 // 4 + 1), ("cstb", B_ID)], ["pskt"])
                            cp("act", Ktk[di][:, g0:g0 + 8, :], pskt[:], ["pskt"], [("Ktk", di, g0 // 8)])
                    wt, wk = w_next(56 + h)
                    for b_ in range(4):
                        fns = []
                        for j in range(4):
                            t0 = 2048 + (b_ * 4 + j) * 128
                            fns += [mm(psv[:, j, :], uT(kc, t0, 128), wt[:, kc, :], kc == 0, kc == 15) for kc in range(16)]
                        mmgroup(fns, [wk] + ukeys(2048 + b_ * 512, 512), ["psv"])
                        if b_ == 0:
                            w_prefetch_cast()
                        cp("act", vtk[:, b_ * 4:b_ * 4 + 4, :], psv[:], ["psv"], [("vtk", b_)])
                    Abuf = [[psf[0][:, 0:128], psv[:].rearrange("p a b -> p (a b)")[:, 0:128]],
                            [psf[1][:, 0:128], pskt[:].rearrange("p a b -> p (a b)").bitcast(F32)[:, 0:128]]]
                    Akey = [[("psf", 0), "psv"], [("psf", 1), "pskt"]]
                    chunk = lambda di, k: k if di == 0 else 31 - k
                    Tst = [T32[:, h, :], To[:]]
                    Tk = [("T32", h), "To"]

                    def emit_state(di, k):
                        c = chunk(di, k)
                        j, pb = c // 2, 64 * (c % 2)
                        ab, akey = Abuf[di][k % 2], Akey[di][k % 2]
                        mmgroup([mm(ab, Ktk[di][pb:pb + 64, j, :], vtk[pb:pb + 64, j, :], True, True)],
                                [("Ktk", di, j // 8), ("vtk", j // 4)], [akey])
                        if di == 0:
                            if k == 0:
                                act(Sb[0][0][:], T32[:, h, :], AF.Copy, [("T32", h), ("dlast", h)], [("Sb", 0, 0)],
                                    scale=dlast[:, h:h + 1])
                            dprev = dlast[:, h:h + 1] if k == 0 else dec[0][:, c - 1:c]
                            stt(T32[:, h, :], T32[:, h, :], dprev, ab, OP.mult, OP.add,
                                [("T32", h), akey, ("dec", 0, max(c - 1, 0) // 8), ("dlast", h)], [("T32", h)])
                            if k < 31:
                                act(Sb[0][(k + 1) % 3][:], T32[:, h, :], AF.Copy, [("T32", h), ("dec", 0, c // 8)],
                                    [("Sb", 0, (k + 1) % 3)], scale=dec[0][:, c:c + 1])
                        else:
                            if k == 0:
                                cp("dve", To[:], ab, [akey], ["To"])
                            else:
                                stt(To[:], To[:], dec[1][:, c:c + 1], ab, OP.mult, OP.add,
                                    ["To", akey, ("dec", 1, c // 8)], ["To"])
                            if k < 31:
                                cn = c - 1
                                act(Sb[1][(k + 1) % 3][:], To[:], AF.Copy, ["To", ("dec", 1, cn // 8)],
                                    [("Sb", 1, (k + 1) % 3)], scale=dec[1][:, cn:cn + 1])

                    for di in range(2):
                        emit_state(di, 0)
                    for di in range(2):
                        emit_state(di, 1)
                    for k in range(32):
                        for di in range(2):
                            c = chunk(di, k)
                            j, pb = c // 2, 64 * (c % 2)
                            b_ = c // 8
                            first_of_tile = (c % 2 == 0) if di == 0 else (c % 2 == 1)
                            cs64 = slice(c * 64, (c + 1) * 64)
                            if first_of_tile:
                                tsl = slice(j * 128, (j + 1) * 128)
                                mmgroup([mm(psat[di], KTo[di][:, tsl], QT[di][:, tsl], True, True)],
                                        [("KTo", di, b_), ("QT", di, b_)], [("psm", di)])
                                tt("dve", attm[di][:], psat[di], HMF if di == 0 else HMB, OP.mult,
                                   [("psm", di), ("cstb", B_HMF), ("cstb", B_HMB)], [("attm", di)])
                            has_state = (di == 0) or (k > 0)
                            oc = slice((c % 8) * 64, (c % 8 + 1) * 64)
                            fns = [mm(pso_[di][:, oc], vtk[pb:pb + 64, j, :], attm[di][pb:pb + 64, pb:pb + 64], True, not has_state)]
                            rr = [("vtk", j // 4), ("attm", di)]
                            if has_state:
                                fns.append(mm(pso_[di][:, oc], Sb[di][k % 3][:], QT[di][:, cs64], False, True))
                                rr += [("Sb", di, k % 3), ("QT", di, b_)]
                            mmgroup(fns, rr, [("psoo", di)])
                            if k + 2 < 32:
                                emit_state(di, k + 2)
                            done = (c % 8 == 7) if di == 0 else (c % 8 == 0)
                            if done:
                                cs = slice(b_ * 512, (b_ + 1) * 512)
                                firstw = (b_ < 2) if di == 0 else (b_ >= 2)
                                if firstw:
                                    cp("act", oacc[:, cs], pso_[di][:], [("psoo", di)], [("oacc", b_)])
                                else:
                                    tt("dve", oacc[:, cs], oacc[:, cs], pso_[di][:], OP.add,
                                       [("psoo", di), ("oacc", b_)], [("oacc", b_)])
                    wt, wk = w_next(64 + h)
                    for b_ in range(4):
                        sl = b_ % 2
                        cs = slice(b_ * 512, (b_ + 1) * 512)
                        proj_fm(wt, wk, 2048 + b_ * 512, psf[sl][:], ("psf", sl))
                        if b_ == 0:
                            w_prefetch_cast()
                        tt("pool", sqb, oacc[:, cs], oacc[:, cs], OP.mult, [("oacc", b_)], [("g3", 1)])
                        mmgroup([mm(pso_[0][:], ONEB, sqb, True, True)], [("g3", 1), ("cstb", B_ONE)], [("psoo", 0)])
                        act(gt[1][:], pso_[0][:], AF.Ln, [("psoo", 0)], [("g1", 0)], scale=1.0 / 128, bias=EPS)
                        act(gt[1][:], gt[1][:], AF.Exp, [("g1", 0)], [("g1", 0)], scale=-0.5)
                        tt("dve", gt[1][:], gt[1][:], oacc[:, cs], OP.mult, [("g1", 0), ("oacc", b_)], [("g1", 0)])
                        act(gt[0][:], psf[sl][:], AF.Exp, [("psf", sl)], [("g0", 0)], scale=-1.0)
                        act(gt[0][:], gt[0][:], AF.Ln, [("g0", 0)], [("g0", 0)], bias=1.0)
                        act(gt[0][:], gt[0][:], AF.Exp, [("g0", 0)], [("g0", 0)], scale=-1.0)
                        tt("dve", gt[0][:], psf[sl][:], gt[0][:], OP.mult, [("psf", sl), ("g0", 0)], [("g0", 0)])
                        stt(Y3[:, 8 + h, cs], gt[1][:], ghg[:, 0:1], gt[0][:], OP.mult, OP.mult,
                            [("g1", 0), ("g0", 0), "ghg"], [("yT", 8 + h, b_)])
                S.emit()

        if debug:
            S.add("sp", lambda e: e.dma_start(out=dbg_d[:, :], in_=bufBC[:]), [("yT", k, b_) for k in range(16) for b_ in range(4)],
                  ["dbgout"], dma="dbgout")

        def load_w_res(wd, nk, wdst, stg, key):
            wv_ = wd.rearrange("(k p) n -> p k n", p=128)
            engs = ("dve", "pool", "act")
            for kc in range(nk):
                sg = kc % len(stg)
                dma(stg[sg][:], wv_[:, kc, :], [], [("wstg", sg)], ("wstg", sg))
                cp(engs[kc % 3], wdst[:, kc, :], stg[sg][:], [("wstg", sg)], [(key, kc)])

        def row_rstd(srcs, skeys, ss, sl, junk):
            o = 8 * sl
            for cg in range(4):
                S.add("act", lambda e, cg=cg, o=o: e.activation(junk[:], srcs[cg], AF.Square, accum_out=ss[:, o + cg:o + cg + 1]),
                      [skeys[cg]], ["junk", ("ss4", sl, cg)])
            S.add("dve", lambda e, o=o: e.tensor_reduce(ss[:, o + 4:o + 5], ss[:, o:o + 4], mybir.AxisListType.X, OP.add),
                  [("ss4", sl, cg) for cg in range(4)], [("ssr", sl)])
            ts("dve", ss[:, o + 5:o + 6], ss[:, o + 4:o + 5], 1.0 / 2048, EPS, OP.mult, OP.add, [("ssr", sl)], [("ss1", sl)])
            act(ss[:, o + 6:o + 7], ss[:, o + 5:o + 6], AF.Ln, [("ss1", sl)], [("ss2", sl)])
            act(ss[:, o + 7:o + 8], ss[:, o + 6:o + 7], AF.Exp, [("ss2", sl)], [("ss3", sl)], scale=-0.5)
            return ss[:, o + 7:o + 8], ("ss3", sl)

        with contextlib.ExitStack() as s2:
            sb2 = lambda name, shape, dt: s2.enter_context(nc.sbuf_tensor(uniq(name), shape, dt))
            ps2 = lambda name, shape, dt: s2.enter_context(nc.psum_tensor(uniq(name), shape, dt))
            wres = A3
            xt = [sb2("xt%d" % i, [128, 2048], F32) for i in range(2)]
            stg = [sb2("stg%d" % i, [128, 2048], F32) for i in range(3)]
            gvec = sb2("gvec", [128, 2048], F32)
            ytmp = [sb2("ytmp%d" % i, [128, 512], F32) for i in range(2)]
            ss = sb2("ss", [128, 16], F32)
            junk = sb2("junk", [128, 512], BF16)
            pyo = [[ps2("pyo%d%d" % (j, i), [128, 512], F32) for i in range(4)] for j in range(2)]

            load_w_res(wout_d, 16, wres, stg, "wres")
            dma(gvec[:], gpost_d[:, :], [], ["gvec"], "gvec")
            for tt_ in range(16):
                sl = tt_ % 2
                tsl = slice(tt_ * 128, (tt_ + 1) * 128)
                dma(xt[sl][:], x_d[2048 + tt_ * 128:2048 + (tt_ + 1) * 128, :], [], [("xt", sl)], ("xt", sl))
                for cg in range(4):
                    fns = [mm(pyo[sl][cg][:], Y3[:, kc, tsl], wres[:, kc, cg * 512:(cg + 1) * 512], kc == 0, kc == 15)
                           for kc in range(16)]
                    mmgroup(fns, [("yT", kc, tt_ // 4) for kc in range(16)] + [("wres", kc) for kc in range(16)],
                            [("pyo", sl, cg)])
                rs, rsk = row_rstd([pyo[sl][cg][:] for cg in range(4)], [("pyo", sl, cg) for cg in range(4)], ss, sl, junk)
                for cg in range(4):
                    cs = slice(cg * 512, (cg + 1) * 512)
                    stt(ytmp[cg % 2][:], pyo[sl][cg][:], rs, gvec[:, cs], OP.mult, OP.mult,
                        [("pyo", sl, cg), rsk, "gvec"], [("ytmp", cg % 2)])
                    tt("pool", xt[sl][:, cs], xt[sl][:, cs], ytmp[cg % 2][:], OP.add, [("xt", sl), ("ytmp", cg % 2)], [("xt", sl)])
                S.add("pool", lambda e, tsl=tsl, sl=sl: e.dma_start(out=out_d[tsl, :], in_=xt[sl][:]),
                      [("xt", sl)], [("out", tt_)], dma=("outw", sl))
            S.emit()

        with contextlib.ExitStack() as s2:
            sb2 = lambda name, shape, dt: s2.enter_context(nc.sbuf_tensor(uniq(name), shape, dt))
            ps2 = lambda name, shape, dt: s2.enter_context(nc.psum_tensor(uniq(name), shape, dt))
            wres = A3
            wpp = sb2("wpp", [128, 2, 2048], BF16)
            xt = [sb2("xt%d" % i, [128, 2048], F32) for i in range(3)]
            h1b = [sb2("h1b0", [128, 2048], BF16)] * 2
            h1T = [sb2("h1T%d" % i, [128, 16, 128], BF16) for i in range(2)]
            gvec = sb2("gvec", [128, 2048], F32)
            ge = [sb2("ge%d" % i, [128, 2048], F32) for i in range(2)]
            stg = ge
            ss = sb2("ss", [128, 16], F32)
            junk = sb2("junk", [128, 512], BF16)
            ytmp = [sb2("ytmp%d" % i, [128, 512], F32) for i in range(2)]
            pf = sb2("pf", [128, 256], F32)
            pb_ = sb2("pb", [128, 256], BF16)
            pT = Y3
            pg = [ps2("pg%d" % i, [128, 512], F32) for i in range(2)]
            pe2 = [ps2("pe2%d" % i, [128, 512], F32) for i in range(2)]
            ppt = ps2("ppt", [128, 8, 128], BF16)
            pst = [ps2("pst%d" % i, [128, 8, 128], BF16) for i in range(2)]

            load_w_res(wpg_d, 16, wres, stg, "wres")
            load_w_res(wpp_d, 2, wpp, stg, "wpp")
            dma(gvec[:], gple_d[:, :], [], ["gvec"], "gvec")
            for tt_ in range(16):
                tsl = slice(tt_ * 128, (tt_ + 1) * 128)
                dma(pf[:], p_d[tsl, :], [], ["pf"], "pf")
                cp("dve", pb_[:], pf[:], ["pf"], ["pb"])
                fns = [(lambda e, k=k: e.transpose(ppt[:, k, :], pb_[:, k * 128:(k + 1) * 128], IDB)) for k in range(2)]
                mmgroup(fns, ["pb", ("cstb", B_ID)], ["ppt"])
                cp("act", pT[:, 0:2, tsl], ppt[:, 0:2, :], ["ppt", "dbgout"], [("pT", tt_)])

            def o5_front(tt_):
                sl = tt_ % 2
                x3 = tt_ % 3
                tsl = slice(tt_ * 128, (tt_ + 1) * 128)
                dma(xt[x3][:], out_d[tsl, :], [("out", tt_)], [("xt", x3)], ("xt", x3))
                cp("act", h1b[sl][:], xt[x3][:], [("xt", x3)], [("h1b", 0)])
                for hf in range(2):
                    fns = [(lambda e, k=k, hf=hf, sl=sl: e.transpose(pst[hf][:, k, :], h1b[sl][:, (hf * 8 + k) * 128:(hf * 8 + k + 1) * 128], IDB))
                           for k in range(8)]
                    mmgroup(fns, [("h1b", 0), ("cstb", B_ID)], [("pst", hf)])
                    cp("dve" if hf else "act", h1T[sl][:, hf * 8:hf * 8 + 8, :], pst[hf][:], [("pst", hf)], [("h1T", sl, hf)])

            o5_front(0)
            for tt_ in range(16):
                sl = tt_ % 2
                tsl = slice(tt_ * 128, (tt_ + 1) * 128)
                if tt_ + 1 < 16:
                    o5_front(tt_ + 1)
                for cg in range(4):
                    cs = slice(cg * 512, (cg + 1) * 512)
                    s_ = cg % 2
                    fns = [mm(pg[s_][:], h1T[sl][:, kc, :], wres[:, kc, cs], kc == 0, kc == 15) for kc in range(16)]
                    mmgroup(fns, [("h1T", sl, 0), ("h1T", sl, 1)] + [("wres", kc) for kc in range(16)], [("pg", s_)])
                    fns = [mm(pe2[s_][:], pT[:, kc, tsl], wpp[:, kc, cs], kc == 0, kc == 1) for kc in range(2)]
                    mmgroup(fns, [("pT", tt_), ("wpp", 0), ("wpp", 1)], [("pe2", s_)])
                    act(ytmp[s_][:], pg[s_][:], AF.Exp, [("pg", s_)], [("ytmp", s_)], scale=-1.0)
                    act(ytmp[s_][:], ytmp[s_][:], AF.Ln, [("ytmp", s_)], [("ytmp", s_)], bias=1.0)
                    act(ytmp[s_][:], ytmp[s_][:], AF.Exp, [("ytmp", s_)], [("ytmp", s_)], scale=-1.0)
                    tt("dve", ge[sl][:, cs], ytmp[s_][:], pe2[s_][:], OP.mult, [("ytmp", s_), ("pe2", s_)], [("ge", sl, cg)])
                rs, rsk = row_rstd([ge[sl][:, cg * 512:(cg + 1) * 512] for cg in range(4)], [("ge", sl, cg) for cg in range(4)],
                                   ss, sl, junk)
                stt(ge[sl][:], ge[sl][:], rs, gvec[:], OP.mult, OP.mult, [("ge", sl, cg) for cg in range(4)] + [rsk, "gvec"],
                    [("ge", sl, cg) for cg in range(4)])
                tt("pool", ge[sl][:], ge[sl][:], xt[tt_ % 3][:], OP.add, [("xt", tt_ % 3)] + [("ge", sl, cg) for cg in range(4)],
                   [("ge", sl, cg) for cg in range(4)])
                S.add("pool", lambda e, tsl=tsl, sl=sl: e.dma_start(out=out_d[tsl, :], in_=ge[sl][:]),
                      [("ge", sl, cg) for cg in range(4)], [("out", tt_)], dma=("outw", sl))
            S.add("sp", lambda e: e.nop(), [("out", t_) for t_ in range(16)], ["fin"])
            S.emit()
    return nc


def _consts():
    c = np.zeros((128, NCST), np.float32)
    p = np.arange(128)
    c[:, C_ID:C_ID + 128] = np.eye(128)
    for m in range(128):
        e = m % 64
        if e < 8:
            c[m + 8, C_RT + m] = 1.0
        elif e < 16:
            c[m - 8, C_RT + m] = 1.0
    i = p[:, None]
    j = np.arange(128)[None, :]
    m1 = (i >= j).astype(np.float32)
    m2 = (i <= j).astype(np.float32)
    c[:, C_M12:C_M12 + 512] = np.concatenate([m1, m2, m1, m2], axis=1)
    same = (i // 64) == (j // 64)
    c[:, C_HMF:C_HMF + 128] = (same & (i <= j)).astype(np.float32)
    c[:, C_HMB:C_HMB + 128] = (same & (i >= j)).astype(np.float32)
    t = np.arange(512)
    c[:, C_RST:C_RST + 512] = (t % 64 != 0).astype(np.float32)[None, :]
    inv = np.power(np.float32(500000.0), -np.arange(0, 16, 2, dtype=np.float32) / np.float32(16)).astype(np.float32)
    e = p % 64
    c[:, C_INV] = np.where(e < 16, inv[e % 8] / np.float32(2 * np.pi), 0.0)
    c[:, C_SGN] = np.where(e < 8, -1.0, np.where(e < 16, 1.0, 0.0))
    c[:, C_ONE:C_ONE + 128] = 1.0
    return c


_NC_CACHE = {}


def kernel(x, p, positions, w_in, w_out, g_pre, g_post, g_hg, lb_fwd, lb_bwd, w_pg, w_pp, g_ple, _debug=False):
    x = np.asarray(x, np.float32)
    p = np.asarray(p, np.float32)
    positions = np.asarray(positions, np.int32)
    w_in0 = np.asarray(w_in, np.float32)[0]
    aq, ak, av, ag, hq, hff, hfb, hv, hg = [w_in0[:, s:s + 1024] for s in range(0, 9216, 1024)]

    def tile_w(cols):
        w = np.concatenate(cols, axis=1)
        return np.ascontiguousarray(w.reshape(16, 128, NB_W, 128).transpose(2, 1, 0, 3)).reshape(NB_W, 128, 2048)

    w_t = {1: tile_w([aq, ak, av, ag, hq, hff, hfb, hv, hg]), 0: tile_w([aq, ak, av, ag, hq, hfb, hff, hv, hg])}

    def lbt(a, b):
        f = lambda r: np.asarray(r, np.float32).reshape(2, 8, 128).transpose(2, 0, 1).reshape(128, 16)
        return np.ascontiguousarray(np.concatenate([f(a), f(b)], axis=1))

    lb_t = {1: lbt(lb_fwd, lb_bwd), 0: lbt(lb_bwd, lb_fwd)}
    rep = lambda v: np.ascontiguousarray(np.broadcast_to(np.asarray(v, np.float32).reshape(1, -1), (128, 2048)))
    common = {
        "w_out": np.ascontiguousarray(np.asarray(w_out, np.float32)[0]),
        "w_pg": np.ascontiguousarray(np.asarray(w_pg, np.float32)[0]),
        "w_pp": np.ascontiguousarray(np.asarray(w_pp, np.float32)[0]),
        "g_pre": rep(g_pre), "g_post": rep(g_post), "g_ple": rep(g_ple),
        "g_hg": np.ascontiguousarray(np.asarray(g_hg, np.float32).reshape(128, 1)),
        "cst": _consts(),
    }
    in_maps = []
    for c in range(8):
        b, hf = c // 2, c % 2
        if hf == 1:
            xl, pl, posl = x[b], p[0, b, 2048:], positions[b]
        else:
            xl, pl, posl = x[b, ::-1], p[0, b, 2047::-1], positions[b, ::-1]
        m = dict(common)
        m["x"] = np.ascontiguousarray(xl)
        m["p"] = np.ascontiguousarray(pl)
        m["pos"] = np.ascontiguousarray(np.broadcast_to(posl[1024:].reshape(1, 3072), (128, 3072))).astype(np.int32)
        m["w_in"] = w_t[hf]
        m["lb"] = lb_t[hf]
        in_maps.append(m)
    key = bool(_debug)
    if key not in _NC_CACHE:
        _NC_CACHE[key] = build(debug=key)
    nc = _NC_CACHE[key]
    res = run_bass_kernel_spmd(nc, in_maps, core_ids=list(range(8)))
    out = np.empty((4, 4096, 2048), np.float32)
    for c in range(8):
        b, hf = c // 2, c % 2
        o = np.asarray(res.results[c]["out"], np.float32)
        if hf == 1:
            out[b, 2048:] = o
        else:
            out[b, 2047::-1] = o
    if _debug:
        return out, [res.results[c]["dbg"] for c in range(8)]
    return out
```

```python
import contextlib
import math
import numpy as np
import concourse.bass as bass
import concourse.mybir as mybir
from concourse.bass_utils import run_bass_kernel_spmd

F32 = mybir.dt.float32
BF16 = mybir.dt.bfloat16
I32 = mybir.dt.int32
AF = mybir.ActivationFunctionType
OP = mybir.AluOpType

EPS = 1e-6
NB_W = 72
C_ID, C_RT, C_M12, C_HMF, C_HMB, C_RST, C_INV, C_SGN, C_ONE = 0, 128, 256, 768, 896, 1024, 1536, 1537, 1538
NCST = 1666
NEGBIG = -30000.0
B_ID, B_RT, B_M12, B_HMF, B_HMB, B_ONE = 0, 128, 256, 768, 896, 1024
NCB = 1152


class Sched:
    ENG = ("pe", "act", "dve", "pool", "sp")

    def __init__(self, nc, stack):
        self.nc = nc
        self.stack = stack
        self.ops = []
        self.emitted = 0
        self.last_w = {}
        self.readers = {}
        self.esem = {e: stack.enter_context(nc.semaphore("s_" + e)) for e in self.ENG}
        self.ecnt = {e: 0 for e in self.ENG}
        self.dsem = {}
        self.dcnt = {}
        self.waited = {e: {} for e in self.ENG}
        self.enabled = True
        self.nemit = 0
        self.phase_dmas = []
        self.stop_n = None

    def add(self, eng, fn, r=(), w=(), dma=None):
        if not self.enabled:
            return None
        i = len(self.ops)
        deps = set()
        for k in r:
            j = self.last_w.get(k)
            if j is not None:
                deps.add(j)
        for k in w:
            j = self.last_w.get(k)
            if j is not None:
                deps.add(j)
            deps.update(self.readers.get(k, ()))
        for k in r:
            self.readers.setdefault(k, []).append(i)
        for k in w:
            self.last_w[k] = i
            self.readers[k] = []
        if dma is not None:
            if dma not in self.dsem:
                self.dsem[dma] = self.stack.enter_context(self.nc.semaphore("d%d" % len(self.dsem)))
                self.dcnt[dma] = 0
            self.dcnt[dma] += 16
            sig = (self.dsem[dma], self.dcnt[dma], 16, None)
            self.phase_dmas.append(i)
        else:
            self.ecnt[eng] += 1
            sig = (self.esem[eng], self.ecnt[eng], 1, eng)
        self.ops.append((eng, fn, deps, sig))
        return i

    def mark(self, tag):
        if self.stop_n == tag:
            self.enabled = False

    def emit(self):
        nc = self.nc
        if self.phase_dmas and self.enabled:
            self.ecnt["sp"] += 1
            self.ops.append(("sp", lambda e: e.nop(), set(self.phase_dmas), (self.esem["sp"], self.ecnt["sp"], 1, "sp")))
        self.phase_dmas = []
        new = self.ops[self.emitted:]
        self.emitted = len(self.ops)
        self.nemit += 1
        if isinstance(self.stop_n, int) and self.nemit >= self.stop_n:
            self.enabled = False
        if not new:
            return
        per = {e: [] for e in self.ENG}
        for op in new:
            per[op[0]].append(op)
        ops = self.ops
        waited = self.waited
        hw = {"pe": "tensor", "act": "scalar", "dve": "vector", "pool": "gpsimd", "sp": "sync"}
        with nc.Block() as blk:
            for e in self.ENG:
                lst = per[e]
                if not lst:
                    continue

                def body(engine, lst=lst, e=e):
                    wd = waited[e]
                    for (_, fn, deps, sig) in lst:
                        need = {}
                        for j in deps:
                            s, v, _, pe_ = ops[j][3]
                            if pe_ == "pe" and e == "pe":
                                continue
                            key = id(s)
                            if need.get(key, (None, 0))[1] < v:
                                need[key] = (s, v)
                        for key, (s, v) in need.items():
                            if wd.get(key, 0) < v:
                                engine.wait_ge(s, v)
                                wd[key] = v
                        ins = fn(engine)
                        ins.then_inc(sig[0], sig[2])

                getattr(blk, hw[e])(body)


def build(debug=False, stop=None):
    nc = bass.Bass("TRN2", target_bir_lowering=False)
    dt_in = lambda name, shape, dt=F32: nc.dram_tensor(name, shape, dt, kind="ExternalInput").ap()
    x_d = dt_in("x", [4096, 2048])
    p_d = dt_in("p", [2048, 256])
    pos_d = dt_in("pos", [128, 3072], I32)
    win_d = dt_in("w_in", [NB_W, 128, 2048])
    wout_d = dt_in("w_out", [2048, 2048])
    wpg_d = dt_in("w_pg", [2048, 2048])
    wpp_d = dt_in("w_pp", [256, 2048])
    gpre_d = dt_in("g_pre", [128, 2048])
    gpost_d = dt_in("g_post", [128, 2048])
    gple_d = dt_in("g_ple", [128, 2048])
    ghg_d = dt_in("g_hg", [128, 1])
    lb_d = dt_in("lb", [128, 32])
    cst_d = dt_in("cst", [128, NCST])
    out_d = nc.dram_tensor("out", [2048, 2048], F32, kind="ExternalOutput").ap()
    if debug:
        dbg_d = nc.dram_tensor("dbg", [128, 16 * 2048], BF16, kind="ExternalOutput").ap()

    _cnt = [0]

    def uniq(name):
        _cnt[0] += 1
        return "%s_%d" % (name, _cnt[0])

    st = contextlib.ExitStack()
    with st:
        S = Sched(nc, st)
        S.stop_n = stop
        sb = lambda name, shape, dt: st.enter_context(nc.sbuf_tensor(name, shape, dt))
        bufA = sb("bufA", [128, 16 * 2048], BF16)
        bufBC = sb("bufBC", [128, 16 * 2048], BF16)
        cstf = sb("cstf", [128, 514], F32)
        cstb = sb("cstb", [128, NCB], BF16)
        lbv = sb("lbv", [128, 32], F32)
        ghg = sb("ghg", [128, 1], F32)
        RST = cstf[:, 0:512]
        INV = cstf[:, 512:513]
        SGN = cstf[:, 513:514]
        IDB = cstb[:, B_ID:B_ID + 128]
        RTB = cstb[:, B_RT:B_RT + 128]
        M12 = cstb[:, B_M12:B_M12 + 512]
        HMF = cstb[:, B_HMF:B_HMF + 128]
        HMB = cstb[:, B_HMB:B_HMB + 128]
        ONEB = cstb[:, B_ONE:B_ONE + 128]

        A3 = bufA[:].rearrange("p (k t) -> p k t", k=16)
        BC_lo = bufBC[:, 0:16384].rearrange("p (k t) -> p k t", k=16)
        BC_hi = bufBC[:, 16384:32768].rearrange("p (k t) -> p k t", k=16)
        Y3 = bufBC[:].rearrange("p (k t) -> p k t", k=16)

        def uT3(t0):
            if t0 >= 2048:
                return A3, t0 - 2048
            if t0 >= 1024:
                return BC_hi, t0 - 1024
            return BC_lo, t0

        def uT(kc, t0, n):
            v, o = uT3(t0)
            return v[:, kc, o:o + n]

        ukeys = lambda t0, n: [("uT", t) for t in range(t0 // 128, (t0 + n) // 128)]

        def act(out, in_, func, r, w, scale=1.0, bias=0.0):
            S.add("act", lambda e: e.activation(out, in_, func, bias=bias, scale=scale), r, w)

        def tt(eng, out, a, b, op, r, w):
            S.add(eng, lambda e: e.tensor_tensor(out, a, b, op), r, w)

        def ts(eng, out, a, s1, s2, op0, op1, r, w):
            if s2 is None:
                S.add(eng, lambda e: e.tensor_scalar(out, a, s1, None, op0), r, w)
            else:
                S.add(eng, lambda e: e.tensor_scalar(out, a, s1, s2, op0, op1), r, w)

        def stt(out, a, sc, b, op0, op1, r, w, accum=None):
            if accum is None:
                S.add("dve", lambda e: e.scalar_tensor_tensor(out, a, sc, b, op0, op1), r, w)
            else:
                S.add("dve", lambda e: e.scalar_tensor_tensor(out, a, sc, b, op0, op1, accum_out=accum), r, w)

        def cp(eng, out, in_, r, w):
            if eng == "act":
                S.add("act", lambda e: e.activation(out, in_, AF.Copy), r, w)
            else:
                S.add(eng, lambda e: e.tensor_copy(out, in_), r, w)

        def recip(out, in_, r, w):
            S.add("dve", lambda e: e.reciprocal(out, in_), r, w)

        def dma(out, in_, r, w, key):
            S.add("sp", lambda e: e.dma_start(out=out, in_=in_), r, w, dma=key)

        def mm(out, lhsT, rhs, start, stop):
            return lambda e: e.matmul(out, lhsT, rhs, start=start, stop=stop)

        def mmgroup(fns, r, w):
            def f(e):
                ins = None
                for g in fns:
                    ins = g(e)
                return ins
            S.add("pe", f, r, w)

        def sigmoid_chain(dst, src, r_src, tmp, tk):
            act(tmp, src, AF.Exp, r_src, [tk], scale=-1.0)
            ts("dve", tmp, tmp, 1.0, None, OP.add, None, [tk], [tk])
            recip(dst, tmp, [tk], [tk] if dst is tmp else [tk + ("o",)])

        with nc.sbuf_tensor("cst0", [128, NCST], F32) as cst0, nc.sbuf_tensor("lb0", [128, 32], F32) as lb0:
            dma(cst0[:], cst_d[:, :], [], ["cst0"], "cst0")
            dma(lb0[:], lb_d[:, :], [], ["lb0"], "lb0")
            dma(ghg[:], ghg_d[:, :], [], ["ghg"], "ghg")
            cp("dve", cstf[:, 0:512], cst0[:, C_RST:C_RST + 512], ["cst0"], ["cstf"])
            cp("dve", cstf[:, 512:514], cst0[:, C_INV:C_INV + 2], ["cst0"], ["cstf2"])
            for (bo, co, n) in ((B_ID, C_ID, 128), (B_RT, C_RT, 128), (B_M12, C_M12, 512),
                                (B_HMF, C_HMF, 128), (B_HMB, C_HMB, 128), (B_ONE, C_ONE, 128)):
                cp("dve", cstb[:, bo:bo + n], cst0[:, co:co + n], ["cst0"], [("cstb", bo)])
            for d in range(2):
                o = d * 16
                tt("dve", lb0[:, o:o + 8], lb0[:, o:o + 8], lb0[:, o + 8:o + 16], OP.subtract, ["lb0"], ["lb0"])
                act(lb0[:, o + 8:o + 16], lb0[:, o:o + 8], AF.Exp, ["lb0"], ["lb0"], scale=-1.0)
                ts("dve", lb0[:, o + 8:o + 16], lb0[:, o + 8:o + 16], 1.0, None, OP.add, None, ["lb0"], ["lb0"])
                recip(lbv[:, o:o + 8], lb0[:, o + 8:o + 16], ["lb0"], ["lbv"])
                ts("dve", lbv[:, o + 8:o + 16], lbv[:, o:o + 8], -1.0, 1.0, OP.mult, OP.add, ["lbv"], ["lbv"])
            S.emit()

        LBI = lambda h: lbv[:, h:h + 1]
        OMI = lambda h: lbv[:, 8 + h:9 + h]
        LBO = lambda h: lbv[:, 16 + h:17 + h]
        OMO = lambda h: lbv[:, 24 + h:25 + h]

        with contextlib.ExitStack() as s1:
            sb1 = lambda name, shape, dt: s1.enter_context(nc.sbuf_tensor(name, shape, dt))
            T32 = sb1("T32", [128, 8, 128], F32)
            dlast = sb1("dlast", [128, 8], F32)
            stage = sb1("stage", [128, 1024], F32)
            wbf = [sb1("wbf%d" % i, [128, 16, 128], BF16) for i in range(2)]

            wseq = []
            for h in range(8):
                wseq += [40 + h, 56 + h]
            for hp in range(8):
                wseq += [hp, 8 + hp, 16 + hp, 24 + hp]
            for h in range(8):
                wseq += [32 + h, 40 + h, 48 + h, 56 + h, 64 + h]
            wstate = {"loaded": 0, "cast": 0, "use": 0}

            def w_load():
                n = wstate["loaded"]
                if n >= 2 * len(wseq):
                    return
                hb = n % 2
                dma(stage[:], win_d[wseq[n // 2]][:, hb * 1024:(hb + 1) * 1024], [], ["stage"], "stage")
                wstate["loaded"] += 1

            def w_cast():
                n = wstate["cast"]
                if n >= 2 * len(wseq):
                    return
                blk, hb = n // 2, n % 2
                cp("pool", wbf[blk % 2][:, hb * 8:(hb + 1) * 8, :].rearrange("p k c -> p (k c)"), stage[:], ["stage"],
                   [("wbf", blk % 2)])
                wstate["cast"] += 1

            def w_upto(nh):
                while wstate["cast"] < nh and wstate["cast"] < 2 * len(wseq):
                    if wstate["loaded"] <= wstate["cast"]:
                        w_load()
                    w_cast()
                if wstate["loaded"] <= wstate["cast"]:
                    w_load()

            def w_next(expect):
                n = wstate["use"]
                assert wseq[n] == expect, (n, wseq[n], expect)
                w_upto(2 * n + 2)
                wstate["use"] += 1
                return wbf[n % 2], ("wbf", n % 2)

            def w_prefetch_cast():
                w_upto(2 * wstate["use"] + 2)

            def proj_fm(wt, wk, t0, ps, pk):
                fns = [mm(ps, wt[:, kc, :], uT(kc, t0, 512), kc == 0, kc == 15) for kc in range(16)]
                mmgroup(fns, [wk] + ukeys(t0, 512), [pk])

            def norm_phase(tiles, gpre, xt, xb, ss, pst, junk):
                for n_, ti in enumerate(tiles):
                    sl = n_ % 2
                    t0 = ti * 128
                    dma(xt[sl][:], x_d[t0:t0 + 128, :], [], [("xt", sl)], ("xt", sl))
                    stt(junk[:], xt[sl][:], 1.0, xt[sl][:], OP.mult, OP.mult, [("xt", sl)], ["junk", ("ss", sl)],
                        accum=ss[:, 4 * sl:4 * sl + 1])
                    ts("dve", ss[:, 4 * sl + 1:4 * sl + 2], ss[:, 4 * sl:4 * sl + 1], 1.0 / 2048, EPS, OP.mult, OP.add,
                       [("ss", sl)], [("ss1", sl)])
                    act(ss[:, 4 * sl + 2:4 * sl + 3], ss[:, 4 * sl + 1:4 * sl + 2], AF.Ln, [("ss1", sl)], [("ss2", sl)])
                    act(ss[:, 4 * sl + 3:4 * sl + 4], ss[:, 4 * sl + 2:4 * sl + 3], AF.Exp, [("ss2", sl)], [("ss3", sl)],
                        scale=-0.5)
                    stt(xb[sl][:], xt[sl][:], ss[:, 4 * sl + 3:4 * sl + 4], gpre[:], OP.mult, OP.mult,
                        [("xt", sl), ("ss3", sl), "gpre"], [("xb", sl)])
                    v3, o = uT3(t0)
                    for hf in range(2):
                        fns = [(lambda e, k=k, hf=hf, sl=sl: e.transpose(pst[hf][:, k, :], xb[sl][:, (hf * 8 + k) * 128:(hf * 8 + k + 1) * 128], IDB))
                               for k in range(8)]
                        mmgroup(fns, [("xb", sl), ("cstb", B_ID)], [("pst", hf)])
                        cp("act" if hf == 0 else "pool" if False else "act", v3[:, hf * 8:hf * 8 + 8, o:o + 128], pst[hf][:],
                           [("pst", hf)], [("uT", ti)])

            def gate_chain(psf, pkf, lbc, omc, tmp, mode, sq, outs, par=0):
                t0_, t1_, t2_, t3_ = tmp
                act(t0_, psf, AF.Exp, [pkf], [("g0", par)], scale=-1.0)
                act(t0_, t0_, AF.Ln, [("g0", par)], [("g0", par)], bias=1.0)
                act(t0_, t0_, AF.Exp, [("g0", par)], [("g0", par)], scale=-1.0)
                ts("dve", t1_, t0_, omc, lbc, OP.mult, OP.add, [("g0", par), "lbv"], [("g1", par)])
                act(t2_, t1_, AF.Ln, [("g1", par)], [("g2", par)])
                ts("pool", t1_, t1_, -1.0, 1.0, OP.mult, OP.add, [("g1", par)], [("g1", par)])
                S.add("dve", lambda e: e.tensor_tensor_scan(t3_, RST, t2_, 0.0, OP.mult, OP.add),
                      [("g2", par), "cstf"], [("g3", par)])
                act(outs["dec"], t3_[:, 63:512:64], AF.Exp, [("g3", par)], [outs["deck"]])
                if mode == "out":
                    tt("pool", t3_, t3_, t2_, OP.subtract, [("g3", par), ("g2", par)], [("g3", par)])
                if mode in ("halo", "in"):
                    act(t0_, t3_, AF.Exp, [("g3", par)], [("g0", par)], scale=-1.0)
                    tt("pool", outs["KT"], t1_, t0_, OP.mult, [("g1", par), ("g0", par)], [outs["KTk"]])
                    if mode == "in":
                        act(t2_, t3_, AF.Exp, [("g3", par)], [("g2", par)])
                        tt("dve", outs["QT"], sq, t2_, OP.mult, [("g2", par), "sq"], [outs["QTk"]])
                else:
                    act(t0_, t3_, AF.Exp, [("g3", par)], [("g0", par)])
                    tt("pool", outs["KT"], t1_, t0_, OP.mult, [("g1", par), ("g0", par)], [outs["KTk"]])
                    act(t2_, t3_, AF.Exp, [("g3", par)], [("g2", par)], scale=-1.0)
                    tt("dve", outs["QT"], sq, t2_, OP.mult, [("g2", par), "sq"], [outs["QTk"]])

            with contextlib.ExitStack() as s2:
                sb2 = lambda name, shape, dt: s2.enter_context(nc.sbuf_tensor(uniq(name), shape, dt))
                ps2 = lambda name, shape, dt: s2.enter_context(nc.psum_tensor(uniq(name), shape, dt))
                gpre = sb2("gpre", [128, 2048], F32)
                xt = [sb2("xt%d" % i, [128, 2048], F32) for i in range(2)]
                xb = [sb2("xb%d" % i, [128, 2048], BF16) for i in range(2)]
                junk = sb2("junk", [128, 2048], BF16)
                ss = sb2("ss", [128, 8], F32)
                KT = [sb2("hKT%d" % i, [128, 512], BF16) for i in range(2)]
                HK = sb2("HK", [128, 16, 128], BF16)
                dech = sb2("dech", [128, 8, 32], F32)
                vtok = [sb2("hvtok%d" % i, [128, 4, 128], BF16) for i in range(2)]
                gt = [sb2("hgt%d" % i, [128, 512], F32) for i in range(4)]
                pst = [ps2("pst%d" % i, [128, 8, 128], BF16) for i in range(2)]
                psf = [ps2("psf%d" % i, [128, 512], F32) for i in range(2)]
                psv = ps2("psv", [128, 4, 128], F32)
                pskt = ps2("pskt", [128, 8, 128], BF16)
                psa = [ps2("psa%d" % i, [128, 512], F32) for i in range(2)]

                dma(gpre[:], gpre_d[:, :], [], ["gpre"], "gpre")
                w_load()
                norm_phase(list(range(16)), gpre, xt, xb, ss, pst, junk)
                nblk = 0
                for h in range(8):
                    wf, wfk = w_next(40 + h)
                    wv, wvk = w_next(56 + h)
                    for tb in range(4):
                        sl = nblk % 2
                        nblk += 1
                        proj_fm(wf, wfk, tb * 512, psf[sl][:], ("psf", sl))
                        gate_chain(psf[sl][:], ("psf", sl), LBI(h), OMI(h), [g[:] for g in gt], "halo", None,
                                   dict(KT=KT[sl][:], KTk=("hKT", sl), dec=dech[:, h, tb * 8:tb * 8 + 8],
                                        deck=("dech", h, tb)))
                        fns = []
                        for j in range(4):
                            t0 = (tb * 4 + j) * 128
                            fns += [mm(psv[:, j, :], uT(kc, t0, 128), wv[:, kc, :], kc == 0, kc == 15) for kc in range(16)]
                        mmgroup(fns, [wvk] + ukeys(tb * 512, 512), ["psv"])
                        vs = tb % 2
                        cp("act", vtok[vs][:], psv[:], ["psv"], [("hvtok", vs)])
                        if tb == 3:
                            w_prefetch_cast()
                        fns = [(lambda e, j=j, sl=sl: e.transpose(pskt[:, j, :], KT[sl][:, j * 128:(j + 1) * 128], IDB))
                               for j in range(4)]
                        mmgroup(fns, [("hKT", sl), ("cstb", B_ID)], ["pskt"])
                        cp("act", HK[:, tb * 4:tb * 4 + 4, :], pskt[:, 0:4, :], ["pskt"], [("HK", tb)])
                        for c in range(8):
                            j, pb = c // 2, 64 * (c % 2)
                            cg = tb * 8 + c
                            pa = psa[cg % 2]
                            mmgroup([mm(pa[:, 0:128], HK[pb:pb + 64, tb * 4 + j, :], vtok[vs][pb:pb + 64, j, :], True, True)],
                                    [("HK", tb), ("hvtok", vs)], [("psa", cg % 2)])
                            if cg == 0:
                                cp("dve", T32[:, h, :], pa[:, 0:128], [("psa", cg % 2)], [("T32", h)])
                            else:
                                stt(T32[:, h, :], T32[:, h, :], dech[:, h, cg - 1:cg], pa[:, 0:128], OP.mult, OP.add,
                                    [("T32", h), ("psa", cg % 2), ("dech", h, (cg - 1) // 8)], [("T32", h)])
                    cp("dve", dlast[:, h:h + 1], dech[:, h, 31:32], [("dech", h, 3)], [("dlast", h)])
                S.emit()

            with contextlib.ExitStack() as s2:
                sb2 = lambda name, shape, dt: s2.enter_context(nc.sbuf_tensor(uniq(name), shape, dt))
                ps2 = lambda name, shape, dt: s2.enter_context(nc.psum_tensor(uniq(name), shape, dt))
                gpre = sb2("gpre", [128, 2048], F32)
                xt = [sb2("xt%d" % i, [128, 2048], F32) for i in range(2)]
                xb = [sb2("xb%d" % i, [128, 2048], BF16) for i in range(2)]
                junk = sb2("junk", [128, 2048], BF16)
                ss = sb2("ss", [128, 8], F32)
                pst = [ps2("pst%d" % i, [128, 8, 128], BF16) for i in range(2)]
                dma(gpre[:], gpre_d[:, :], [], ["gpre"], "gpre")
                norm_phase(list(range(16, 32)), gpre, xt, xb, ss, pst, junk)
                S.emit()

            with contextlib.ExitStack() as s2:
                sb2 = lambda name, shape, dt: s2.enter_context(nc.sbuf_tensor(uniq(name), shape, dt))
                ps2 = lambda name, shape, dt: s2.enter_context(nc.psum_tensor(uniq(name), shape, dt))
                qP = sb2("qP", [128, 2, 2048], BF16)
                kT = sb2("kT", [128, 3072], BF16)
                vT = sb2("vT", [128, 3072], BF16)
                vt = sb2("vt", [128, 9, 128], BF16)
                acc = sb2("acc", [128, 2, 2048], F32)
                pe_ = [sb2("pe%d" % i, [128, 512], BF16) for i in range(3)]
                rt = [sb2("rt%d" % i, [128, 512], F32) for i in range(2)]
                Ctab = sb2("Ctab", [128, 3072], BF16)
                Stab = sb2("Stab", [128, 3072], BF16)
                psp = [ps2("psp%d" % i, [128, 512], F32) for i in range(2)]
                psr = psp[0]
                pvt = psp[1][:].bitcast(BF16).rearrange("p (s c) -> p s c", s=8)
                pss = [ps2("pss%d" % i, [128, 2, 2, 128], F32) for i in range(4)]
                pso_full = [ps2("pso%d" % i, [128, 4, 128], F32) for i in range(2)]
                pso = [t_[:, 0:2, :] for t_ in pso_full]

                rt_tmp = [acc[:, 0, 0:512], acc[:, 0, 512:1024], acc[:, 0, 1024:1536]]
                posi_t = sb2("posi", [128, 512], I32)

                def build_tables():
                    rt = rt_tmp
                    posi = posi_t[:]
                    for ch in range(6):
                        cs = slice(ch * 512, (ch + 1) * 512)
                        S.add("act", lambda e, cs=cs: e.dma_start(out=posi, in_=pos_d[:, cs]), [], ["posi"], dma="posi")
                        cp("dve", rt[0][:], posi[:], ["posi"], ["rt0"])
                        ts("dve", rt[0][:], rt[0][:], INV, 0.5, OP.mult, OP.add, ["rt0", "cstf2"], ["rt0"])
                        for off, dst in ((0.0, Stab), (0.25, Ctab)):
                            ts("dve", rt[1][:], rt[0][:], off, None, OP.add, None, ["rt0"], ["rt1"])
                            cp("dve", posi[:], rt[1][:], ["rt1"], ["posi"])
                            cp("dve", rt[2][:], posi[:], ["posi"], ["rt2"])
                            tt("dve", rt[1][:], rt[1][:], rt[2][:], OP.subtract, ["rt1", "rt2"], ["rt1"])
                            ts("dve", rt[2][:], rt[1][:], 0.0, None, OP.is_lt, None, ["rt1"], ["rt2"])
                            tt("dve", rt[1][:], rt[1][:], rt[2][:], OP.add, ["rt1", "rt2"], ["rt1"])
                            ts("dve", rt[1][:], rt[1][:], 2.0 * math.pi, -math.pi, OP.mult, OP.add, ["rt1"], ["rt1"])
                            ts("dve", rt[1][:], rt[1][:], -math.pi, math.pi, OP.max, OP.min, ["rt1"], ["rt1"])
                            act(rt[1][:], rt[1][:], AF.Sin, ["rt1"], ["rt1"])
                            if dst is Stab:
                                ts("dve", dst[:, cs], rt[1][:], SGN, None, OP.mult, None, ["rt1", "cstf2"], [("tab", ch)])
                            else:
                                cp("dve", dst[:, cs], rt[1][:], ["rt1"], [("tabc", ch)])

                S.mark('t1')

                def rope(buf, bkey, nb, tabofs):
                    for b_ in range(nb):
                        cs = slice(b_ * 512, (b_ + 1) * 512)
                        tsl = slice(tabofs + b_ * 512, tabofs + (b_ + 1) * 512)
                        tch = (tabofs + b_ * 512) // 512
                        mmgroup([mm(psr[:], RTB, buf[:, cs], True, True)], [(bkey, b_), ("cstb", B_RT)], [("psp", 0)])
                        S.mark('r1')
                        tt("dve", rt[0][:], psr[:], Stab[:, tsl], OP.mult, [("psp", 0), ("tab", tch)], ["rt0"])
                        S.mark('r2')
                        tt("dve", rt[1][:], buf[:, cs], Ctab[:, tsl], OP.mult, [(bkey, b_), ("tabc", tch)], ["rt1"])
                        S.mark('r3')
                        tt("dve", buf[:, cs], rt[0][:], rt[1][:], OP.add, ["rt0", "rt1"], [(bkey, b_)])

                nunit = 0
                S.add("pool", lambda e: e.memset(qP[:], 0.0), [], [("qP0", b_) for b_ in range(4)] + [("qP1", b_) for b_ in range(4)])
                for hp in range(8):
                    npj = 0
                    for (blk, buf, bkey, nb, t00) in ((hp, None, "qP", 4, 2048), (8 + hp, kT, "kT", 6, 1024),
                                                      (16 + hp, vT, "vT", 6, 1024)):
                        wt, wk = w_next(blk)
                        for b_ in range(nb):
                            sl = npj % 2
                            npj += 1
                            cs = slice(b_ * 512, (b_ + 1) * 512)
                            proj_fm(wt, wk, t00 + b_ * 512, psp[sl][:], ("psp", sl))
                            if b_ == 0:
                                w_prefetch_cast()
                            if buf is None:
                                cp("act", qP[0:64, 0, cs], psp[sl][0:64, :], [("psp", sl)], [("qP0", b_)])
                                cp("act", qP[64:128, 1, cs], psp[sl][64:128, :], [("psp", sl)], [("qP1", b_)])
                            else:
                                cp("act", buf[:, cs], psp[sl][:], [("psp", sl)], [(bkey, b_)])
                    S.mark('t1b')
                    if hp == 0:
                        build_tables()
                    rope(qP[:, 0, :], "qP0", 4, 1024)
                    rope(qP[:, 1, :], "qP1", 4, 1024)
                    rope(kT, "kT", 6, 0)
                    S.mark('t2')
                    units = []
                    for d, r0, r1, i0, i1 in ((1, 0, 1, 0, 8), (1, 0, 1, 8, 16), (4, 0, 1, 0, 4), (4, 1, 2, 0, 4), (4, 2, 3, 0, 4),
                                              (4, 3, 4, 0, 4), (16, 0, 4, 0, 1), (16, 4, 8, 0, 1), (16, 8, 12, 0, 1), (16, 12, 16, 0, 1)):
                        L = 4096 // d
                        nq = (L // 2) // 128
                        ntc = i1 - i0 + 1
                        tiles = []
                        for r_ in range(r0, r1):
                            for j in range(i0, i1 + 1):
                                a_s = L // 2 - 64 + 128 * j
                                n = 128 if j < nq else 64
                                tiles.append((r_ + d * a_s - 1024, n))
                        first = True
                        for r_ in range(r0, r1):
                            for i in range(i0, i1):
                                a0 = L // 2 + 128 * i
                                qs = r_ + d * a0 - 2048
                                kts = []
                                for kt_ in range(2):
                                    ti = (r_ - r0) * ntc + (i - i0) + kt_
                                    idx, n = tiles[ti]
                                    kts.append((ti, idx, n))
                                units.append(dict(d=d, qs=qs, kts=kts, tiles=tiles if first else None))
                                first = False

                    def build_vtiles(d, tiles):
                        for g0 in range(0, len(tiles), 8):
                            grp = tiles[g0:g0 + 8]
                            fns = []
                            rk = set()
                            for s_, (idx, n) in enumerate(grp):
                                fns.append((lambda e, s_=s_, idx=idx, n=n, d=d: e.transpose(
                                    pvt[0:n, s_, :], vT[:, idx:idx + d * (n - 1) + 1:d], IDB)))
                                rk.update(("vT", b_) for b_ in range(idx // 512, (idx + d * (n - 1)) // 512 + 1))
                            mmgroup(fns, list(rk) + [("cstb", B_ID)], [("psp", 1)])
                            cp("dve" if (g0 // 8) % 2 else "act", vt[:, g0:g0 + len(grp), :], pvt[:, 0:len(grp), :],
                               [("psp", 1)], [("vt", g0 // 8)])

                    def unit_front(u, un):
                        sp, se = un % 4, un % 3
                        d, qs, kts = u["d"], u["qs"], u["kts"]
                        qsl = slice(qs, qs + d * 127 + 1, d)
                        qk = [(nm, b_) for nm in ("qP0", "qP1") for b_ in range(qs // 512, (qs + d * 127) // 512 + 1)]
                        fns = []
                        kk_ = set()
                        for hh in range(2):
                            for kt_, (ti, idx, n) in enumerate(kts):
                                fns.append(mm(pss[sp][0:n, hh, kt_, :], kT[:, idx:idx + d * (n - 1) + 1:d],
                                              qP[:, hh, qsl], True, True))
                                kk_.update(("kT", b_) for b_ in range(idx // 512, (idx + d * (n - 1)) // 512 + 1))
                        mmgroup(fns, qk + list(kk_), [("pss", sp)])
                        n2 = kts[1][2]
                        e4 = pe_[se][:].rearrange("p (h k q) -> p h k q", h=2, k=2)
                        if n2 == 128:
                            act(e4, pss[sp][:], AF.Exp, [("pss", sp)], [("pe", se)], scale=0.125)
                        else:
                            act(e4[:, :, 0, :], pss[sp][:, :, 0, :], AF.Exp, [("pss", sp)], [("pe", se)], scale=0.125)
                            act(e4[0:n2, :, 1, :], pss[sp][0:n2, :, 1, :], AF.Exp, [("pss", sp)], [("pe", se)], scale=0.125)
                        tt("dve", pe_[se][:], pe_[se][:], M12, OP.mult, [("pe", se), ("cstb", B_M12)], [("pe", se)])

                    def unit_back(u, un):
                        sl, se = un % 2, un % 3
                        d, qs, kts = u["d"], u["qs"], u["kts"]
                        qsl = slice(qs, qs + d * 127 + 1, d)
                        if u["tiles"] is not None:
                            build_vtiles(d, u["tiles"])
                        e4 = pe_[se][:].rearrange("p (h k q) -> p h k q", h=2, k=2)
                        pnum = pso_full[sl][:, 0, :]
                        pden = pso_full[sl][:, 1:3, :]
                        fns = []
                        for hh in range(2):
                            for kt_, (ti, idx, n) in enumerate(kts):
                                fns.append(mm(pnum[hh * 64:(hh + 1) * 64, :], vt[0:n, ti, hh * 64:(hh + 1) * 64],
                                              e4[0:n, hh, kt_, :], kt_ == 0, kt_ == 1))
                        for kt_, (ti, idx, n) in enumerate(kts):
                            fns.append(mm(pden, ONEB[0:n, :], e4[0:n, :, kt_, :], kt_ == 0, kt_ == 1))
                        vk = list({("vt", ti // 8) for (ti, _, _) in kts})
                        mmgroup(fns, [("pe", se), ("cstb", B_ONE)] + vk, [("pso", sl)])
                        ak = [("acc", b_) for b_ in range(qs // 512, (qs + d * 127) // 512 + 1)]
                        if d == 1:
                            cp("dve", acc[:, 0, qsl], pnum, [("pso", sl)], ak)
                            cp("act", acc[0:64, 1, qsl], pden[0:64, 0, :], [("pso", sl)], ak)
                            cp("act", acc[64:128, 1, qsl], pden[64:128, 1, :], [("pso", sl)], ak)
                        else:
                            tt("dve", acc[:, 0, qsl], acc[:, 0, qsl], pnum, OP.add, [("pso", sl)] + ak, ak)
                            tt("dve", acc[0:64, 1, qsl], acc[0:64, 1, qsl], pden[0:64, 0, :], OP.add, [("pso", sl)] + ak, ak)
                            tt("dve", acc[64:128, 1, qsl], acc[64:128, 1, qsl], pden[64:128, 1, :], OP.add, [("pso", sl)] + ak, ak)

                    unit_front(units[0], nunit)
                    unit_front(units[1], nunit + 1)
                    for ui, u in enumerate(units):
                        if ui + 2 < len(units):
                            unit_front(units[ui + 2], nunit + ui + 2)
                        unit_back(u, nunit + ui)
                    nunit += len(units)
                    S.mark('t6')
                    wt, wk = w_next(24 + hp)
                    for b_ in range(4):
                        sl = b_ % 2
                        cs = slice(b_ * 512, (b_ + 1) * 512)
                        proj_fm(wt, wk, 2048 + b_ * 512, psp[sl][:], ("psp", sl))
                        if b_ == 0:
                            w_prefetch_cast()
                        act(rt[0][:], psp[sl][:], AF.Exp, [("psp", sl)], ["rt0"], scale=-1.0)
                        act(rt[0][:], rt[0][:], AF.Ln, ["rt0"], ["rt0"], bias=1.0)
                        act(rt[0][:], rt[0][:], AF.Exp, ["rt0"], ["rt0"], scale=-1.0)
                        tt("dve", rt[0][:], psp[sl][:], rt[0][:], OP.mult, [("psp", sl), "rt0"], ["rt0"])
                        act(rt[1][:], acc[:, 1, cs], AF.Ln, [("acc", b_)], ["rt1"])
                        act(rt[1][:], rt[1][:], AF.Exp, ["rt1"], ["rt1"], scale=-1.0)
                        tt("pool", rt[1][:], rt[1][:], acc[:, 0, cs], OP.mult, ["rt1", ("acc", b_)], ["rt1"])
                        tt("pool", Y3[:, hp, cs], rt[0][:], rt[1][:], OP.mult, ["rt0", "rt1"], [("yT", hp, b_)])
                S.emit()

            with contextlib.ExitStack() as s2:
                sb2 = lambda name, shape, dt: s2.enter_context(nc.sbuf_tensor(uniq(name), shape, dt))
                ps2 = lambda name, shape, dt: s2.enter_context(nc.psum_tensor(uniq(name), shape, dt))
                sq = sb2("sq", [128, 2048], BF16)
                QT = [sb2("QT%d" % i, [128, 2048], BF16) for i in range(2)]
                KTo = [sb2("KTo%d" % i, [128, 2048], BF16) for i in range(2)]
                Ktk = [sb2("Ktk%d" % i, [128, 16, 128], BF16) for i in range(2)]
                vtk = sb2("vtk", [128, 16, 128], BF16)
                oacc = sb2("oacc", [128, 2048], F32)
                dec = [sb2("dec%d" % i, [128, 32], F32) for i in range(2)]
                gt = [sb2("ogt%d" % i, [128, 512], F32) for i in range(8)]
                sqb = gt[7][:].bitcast(BF16)[:, 0:512]
                attm = [sb2("attm%d" % i, [128, 128], BF16) for i in range(2)]
                Sb = [[sb2("Sb%d%d" % (di, i), [128, 128], BF16) for i in range(3)] for di in range(2)]
                To = sb2("To", [128, 128], F32)
                psf = [ps2("psf%d" % i, [128, 512], F32) for i in range(2)]
                psv = ps2("psv", [128, 4, 128], F32)
                pskt = ps2("pskt", [128, 8, 128], BF16)
                psmd = [ps2("psm%d" % i, [128, 4, 128], F32) for i in range(2)]
                psat = [psmd[0][:, 0, :], psmd[1][:, 0, :]]
                psa_ = [psmd[0][:, 1, :], psmd[1][:, 1, :]]
                pso_ = [ps2("psoo%d" % i, [128, 512], F32) for i in range(2)]

                for h in range(8):
                    npj = 0
                    wt, wk = w_next(32 + h)
                    for b_ in range(4):
                        sl = npj % 2
                        npj += 1
                        cs = slice(b_ * 512, (b_ + 1) * 512)
                        proj_fm(wt, wk, 2048 + b_ * 512, psf[sl][:], ("psf", sl))
                        if b_ == 0:
                            w_prefetch_cast()
                        act(gt[0][:], psf[sl][:], AF.Exp, [("psf", sl)], [("g0", 0)], scale=-1.0)
                        act(gt[0][:], gt[0][:], AF.Ln, [("g0", 0)], [("g0", 0)], bias=1.0)
                        act(gt[0][:], gt[0][:], AF.Exp, [("g0", 0)], [("g0", 0)], scale=-1.0)
                        tt("dve", sq[:, cs], psf[sl][:], gt[0][:], OP.mult, [("psf", sl), ("g0", 0)], ["sq"])
                    for di, (wb, lbc, omc, mode) in enumerate(((40 + h, LBI(h), OMI(h), "in"),
                                                                (48 + h, LBO(h), OMO(h), "out"))):
                        wt, wk = w_next(wb)
                        for b_ in range(4):
                            sl = npj % 2
                            npj += 1
                            cs = slice(b_ * 512, (b_ + 1) * 512)
                            proj_fm(wt, wk, 2048 + b_ * 512, psf[sl][:], ("psf", sl))
                            if b_ == 0:
                                w_prefetch_cast()
                            gate_chain(psf[sl][:], ("psf", sl), lbc, omc, [g[:] for g in gt[4 * (b_ % 2):4 * (b_ % 2) + 4]], mode,
                                       sq[:, cs],
                                       dict(KT=KTo[di][:, cs], KTk=("KTo", di, b_), QT=QT[di][:, cs], QTk=("QT", di, b_),
                                            dec=dec[di][:, b_ * 8:b_ * 8 + 8], deck=("dec", di, b_)), par=b_ % 2)
                    wt, wk = w_next(56 + h)
                    for b_ in range(4):
                        fns = []
                        for j in range(4):
                            t0 = 2048 + (b_ * 4 + j) * 128
                            fns += [mm(psv[:, j, :], uT(kc, t0, 128), wt[:, kc, :], kc == 0, kc == 15) for kc in range(16)]
                        mmgroup(fns, [wk] + ukeys(2048 + b_ * 512, 512), ["psv"])
                        if b_ == 0:
                            w_prefetch_cast()
                        cp("act", vtk[:, b_ * 4:b_ * 4 + 4, :], psv[:], ["psv"], [("vtk", b_)])
                    for di in range(2):
                        for g0 in range(0, 16, 8):
                            fns = [(lambda e, j=j, di=di, g0=g0: e.transpose(pskt[:, j - g0, :], KTo[di][:, j * 128:(j + 1) * 128], IDB))
                                   for j in range(g0, g0 + 8)]
                            mmgroup(fns, [("KTo", di, g0 // 4), ("KTo", di, g0 // 4 + 1), ("cstb", B_ID)], ["pskt"])
                            cp("act", Ktk[di][:, g0:g0 + 8, :], pskt[:], ["pskt"], [("Ktk", di, g0 // 8)])
                    Abuf = [[psf[0][:, 0:128], psv[:].rearrange("p a b -> p (a b)")[:, 0:128]],
                            [psf[1][:, 0:128], pskt[:].rearrange("p a b -> p (a b)").bitcast(F32)[:, 0:128]]]
                    Akey = [[("psf", 0), "psv"], [("psf", 1), "pskt"]]
                    chunk = lambda di, k: k if di == 0 else 31 - k
                    Tst = [T32[:, h, :], To[:]]
                    Tk = [("T32", h), "To"]

                    def emit_state(di, k):
                        c = chunk(di, k)
                        j, pb = c // 2, 64 * (c % 2)
                        ab, akey = Abuf[di][k % 2], Akey[di][k % 2]
                        mmgroup([mm(ab, Ktk[di][pb:pb + 64, j, :], vtk[pb:pb + 64, j, :], True, True)],
                                [("Ktk", di, j // 8), ("vtk", j // 4)], [akey])
                        if di == 0:
                            if k == 0:
                                act(Sb[0][0][:], T32[:, h, :], AF.Copy, [("T32", h), ("dlast", h)], [("Sb", 0, 0)],
                                    scale=dlast[:, h:h + 1])
                            dprev = dlast[:, h:h + 1] if k == 0 else dec[0][:, c - 1:c]
                            stt(T32[:, h, :], T32[:, h, :], dprev, ab, OP.mult, OP.add,
                                [("T32", h), akey, ("dec", 0, max(c - 1, 0) // 8), ("dlast", h)], [("T32", h)])
                            if k < 31:
                                act(Sb[0][(k + 1) % 3][:], T32[:, h, :], AF.Copy, [("T32", h), ("dec", 0, c // 8)],
                                    [("Sb", 0, (k + 1) % 3)], scale=dec[0][:, c:c + 1])
                        else:
                            if k == 0:
                                cp("dve", To[:], ab, [akey], ["To"])
                            else:
                                stt(To[:], To[:], dec[1][:, c:c + 1], ab, OP.mult, OP.add,
                                    ["To", akey, ("dec", 1, c // 8)], ["To"])
                            if k < 31:
                                cn = c - 1
                                act(Sb[1][(k + 1) % 3][:], To[:], AF.Copy, ["To", ("dec", 1, cn // 8)],
                                    [("Sb", 1, (k + 1) % 3)], scale=dec[1][:, cn:cn + 1])

                    for di in range(2):
                        emit_state(di, 0)
                    for di in range(2):
                        emit_state(di, 1)
                    for k in range(32):
                        for di in range(2):
                            c = chunk(di, k)
                            j, pb = c // 2, 64 * (c % 2)
                            b_ = c // 8
                            first_of_tile = (c % 2 == 0) if di == 0 else (c % 2 == 1)
                            cs64 = slice(c * 64, (c + 1) * 64)
                            if first_of_tile:
                                tsl = slice(j * 128, (j + 1) * 128)
                                mmgroup([mm(psat[di], KTo[di][:, tsl], QT[di][:, tsl], True, True)],
                                        [("KTo", di, b_), ("QT", di, b_)], [("psm", di)])
                                tt("dve", attm[di][:], psat[di], HMF if di == 0 else HMB, OP.mult,
                                   [("psm", di), ("cstb", B_HMF), ("cstb", B_HMB)], [("attm", di)])
                            has_state = (di == 0) or (k > 0)
                            oc = slice((c % 8) * 64, (c % 8 + 1) * 64)
                            fns = [mm(pso_[di][:, oc], vtk[pb:pb + 64, j, :], attm[di][pb:pb + 64, pb:pb + 64], True, not has_state)]
                            rr = [("vtk", j // 4), ("attm", di)]
                            if has_state:
                                fns.append(mm(pso_[di][:, oc], Sb[di][k % 3][:], QT[di][:, cs64], False, True))
                                rr += [("Sb", di, k % 3), ("QT", di, b_)]
                            mmgroup(fns, rr, [("psoo", di)])
                            if k + 2 < 32:
                                emit_state(di, k + 2)
                            done = (c % 8 == 7) if di == 0 else (c % 8 == 0)
                            if done:
                                cs = slice(b_ * 512, (b_ + 1) * 512)
                                firstw = (b_ < 2) if di == 0 else (b_ >= 2)
                                if firstw:
                                    cp("act", oacc[:, cs], pso_[di][:], [("psoo", di)], [("oacc", b_)])
                                else:
                                    tt("dve", oacc[:, cs], oacc[:, cs], pso_[di][:], OP.add,
                                       [("psoo", di), ("oacc", b_)], [("oacc", b_)])
                    wt, wk = w_next(64 + h)
                    for b_ in range(4):
                        sl = b_ % 2
                        cs = slice(b_ * 512, (b_ + 1) * 512)
                        proj_fm(wt, wk, 2048 + b_ * 512, psf[sl][:], ("psf", sl))
                        if b_ == 0:
                            w_prefetch_cast()
                        tt("pool", sqb, oacc[:, cs], oacc[:, cs], OP.mult, [("oacc", b_)], [("g3", 1)])
                        mmgroup([mm(pso_[0][:], ONEB, sqb, True, True)], [("g3", 1), ("cstb", B_ONE)], [("psoo", 0)])
                        act(gt[1][:], pso_[0][:], AF.Ln, [("psoo", 0)], [("g1", 0)], scale=1.0 / 128, bias=EPS)
                        act(gt[1][:], gt[1][:], AF.Exp, [("g1", 0)], [("g1", 0)], scale=-0.5)
                        tt("dve", gt[1][:], gt[1][:], oacc[:, cs], OP.mult, [("g1", 0), ("oacc", b_)], [("g1", 0)])
                        act(gt[0][:], psf[sl][:], AF.Exp, [("psf", sl)], [("g0", 0)], scale=-1.0)
                        act(gt[0][:], gt[0][:], AF.Ln, [("g0", 0)], [("g0", 0)], bias=1.0)
                        act(gt[0][:], gt[0][:], AF.Exp, [("g0", 0)], [("g0", 0)], scale=-1.0)
                        tt("dve", gt[0][:], psf[sl][:], gt[0][:], OP.mult, [("psf", sl), ("g0", 0)], [("g0", 0)])
                        stt(Y3[:, 8 + h, cs], gt[1][:], ghg[:, 0:1], gt[0][:], OP.mult, OP.mult,
                            [("g1", 0), ("g0", 0), "ghg"], [("yT", 8 + h, b_)])
                S.emit()

        if debug:
            S.add("sp", lambda e: e.dma_start(out=dbg_d[:, :], in_=bufBC[:]), [("yT", k, b_) for k in range(16) for b_ in range(4)],
                  ["dbgout"], dma="dbgout")

        def load_w_res(wd, nk, wdst, stg, key):
            wv_ = wd.rearrange("(k p) n -> p k n", p=128)
            engs = ("dve", "pool", "act")
            for kc in range(nk):
                sg = kc % len(stg)
                dma(stg[sg][:], wv_[:, kc, :], [], [("wstg", sg)], ("wstg", sg))
                cp(engs[kc % 3], wdst[:, kc, :], stg[sg][:], [("wstg", sg)], [(key, kc)])

        def row_rstd(srcs, skeys, ss, sl, junk):
            o = 8 * sl
            for cg in range(4):
                S.add("act", lambda e, cg=cg, o=o: e.activation(junk[:], srcs[cg], AF.Square, accum_out=ss[:, o + cg:o + cg + 1]),
                      [skeys[cg]], ["junk", ("ss4", sl, cg)])
            S.add("dve", lambda e, o=o: e.tensor_reduce(ss[:, o + 4:o + 5], ss[:, o:o + 4], mybir.AxisListType.X, OP.add),
                  [("ss4", sl, cg) for cg in range(4)], [("ssr", sl)])
            ts("dve", ss[:, o + 5:o + 6], ss[:, o + 4:o + 5], 1.0 / 2048, EPS, OP.mult, OP.add, [("ssr", sl)], [("ss1", sl)])
            act(ss[:, o + 6:o + 7], ss[:, o + 5:o + 6], AF.Ln, [("ss1", sl)], [("ss2", sl)])
            act(ss[:, o + 7:o + 8], ss[:, o + 6:o + 7], AF.Exp, [("ss2", sl)], [("ss3", sl)], scale=-0.5)
            return ss[:, o + 7:o + 8], ("ss3", sl)

        with contextlib.ExitStack() as s2:
            sb2 = lambda name, shape, dt: s2.enter_context(nc.sbuf_tensor(uniq(name), shape, dt))
            ps2 = lambda name, shape, dt: s2.enter_context(nc.psum_tensor(uniq(name), shape, dt))
            wres = A3
            xt = [sb2("xt%d" % i, [128, 2048], F32) for i in range(2)]
            stg = [sb2("stg%d" % i, [128, 2048], F32) for i in range(3)]
            gvec = sb2("gvec", [128, 2048], F32)
            ytmp = [sb2("ytmp%d" % i, [128, 512], F32) for i in range(2)]
            ss = sb2("ss", [128, 16], F32)
            junk = sb2("junk", [128, 512], BF16)
            pyo = [[ps2("pyo%d%d" % (j, i), [128, 512], F32) for i in range(4)] for j in range(2)]

            load_w_res(wout_d, 16, wres, stg, "wres")
            dma(gvec[:], gpost_d[:, :], [], ["gvec"], "gvec")
            for tt_ in range(16):
                sl = tt_ % 2
                tsl = slice(tt_ * 128, (tt_ + 1) * 128)
                dma(xt[sl][:], x_d[2048 + tt_ * 128:2048 + (tt_ + 1) * 128, :], [], [("xt", sl)], ("xt", sl))
                for cg in range(4):
                    fns = [mm(pyo[sl][cg][:], Y3[:, kc, tsl], wres[:, kc, cg * 512:(cg + 1) * 512], kc == 0, kc == 15)
                           for kc in range(16)]
                    mmgroup(fns, [("yT", kc, tt_ // 4) for kc in range(16)] + [("wres", kc) for kc in range(16)],
                            [("pyo", sl, cg)])
                rs, rsk = row_rstd([pyo[sl][cg][:] for cg in range(4)], [("pyo", sl, cg) for cg in range(4)], ss, sl, junk)
                for cg in range(4):
                    cs = slice(cg * 512, (cg + 1) * 512)
                    stt(ytmp[cg % 2][:], pyo[sl][cg][:], rs, gvec[:, cs], OP.mult, OP.mult,
                        [("pyo", sl, cg), rsk, "gvec"], [("ytmp", cg % 2)])
                    tt("pool", xt[sl][:, cs], xt[sl][:, cs], ytmp[cg % 2][:], OP.add, [("xt", sl), ("ytmp", cg % 2)], [("xt", sl)])
                S.add("pool", lambda e, tsl=tsl, sl=sl: e.dma_start(out=out_d[tsl, :], in_=xt[sl][:]),
                      [("xt", sl)], [("out", tt_)], dma=("outw", sl))
            S.emit()

        with contextlib.ExitStack() as s2:
            sb2 = lambda name, shape, dt: s2.enter_context(nc.sbuf_tensor(uniq(name), shape, dt))
            ps2 = lambda name, shape, dt: s2.enter_context(nc.psum_tensor(uniq(name), shape, dt))
            wres = A3
            wpp = sb2("wpp", [128, 2, 2048], BF16)
            xt = [sb2("xt%d" % i, [128, 2048], F32) for i in range(3)]
            h1b = [sb2("h1b0", [128, 2048], BF16)] * 2
            h1T = [sb2("h1T%d" % i, [128, 16, 128], BF16) for i in range(2)]
            gvec = sb2("gvec", [128, 2048], F32)
            ge = [sb2("ge%d" % i, [128, 2048], F32) for i in range(2)]
            stg = ge
            ss = sb2("ss", [128, 16], F32)
            junk = sb2("junk", [128, 512], BF16)
            ytmp = [sb2("ytmp%d" % i, [128, 512], F32) for i in range(2)]
            pf = sb2("pf", [128, 256], F32)
            pb_ = sb2("pb", [128, 256], BF16)
            pT = Y3
            pg = [ps2("pg%d" % i, [128, 512], F32) for i in range(2)]
            pe2 = [ps2("pe2%d" % i, [128, 512], F32) for i in range(2)]
            ppt = ps2("ppt", [128, 8, 128], BF16)
            pst = [ps2("pst%d" % i, [128, 8, 128], BF16) for i in range(2)]

            load_w_res(wpg_d, 16, wres, stg, "wres")
            load_w_res(wpp_d, 2, wpp, stg, "wpp")
            dma(gvec[:], gple_d[:, :], [], ["gvec"], "gvec")
            for tt_ in range(16):
                tsl = slice(tt_ * 128, (tt_ + 1) * 128)
                dma(pf[:], p_d[tsl, :], [], ["pf"], "pf")
                cp("dve", pb_[:], pf[:], ["pf"], ["pb"])
                fns = [(lambda e, k=k: e.transpose(ppt[:, k, :], pb_[:, k * 128:(k + 1) * 128], IDB)) for k in range(2)]
                mmgroup(fns, ["pb", ("cstb", B_ID)], ["ppt"])
                cp("act", pT[:, 0:2, tsl], ppt[:, 0:2, :], ["ppt", "dbgout"], [("pT", tt_)])

            def o5_front(tt_):
                sl = tt_ % 2
                x3 = tt_ % 3
                tsl = slice(tt_ * 128, (tt_ + 1) * 128)
                dma(xt[x3][:], out_d[tsl, :], [("out", tt_)], [("xt", x3)], ("xt", x3))
                cp("act", h1b[sl][:], xt[x3][:], [("xt", x3)], [("h1b", 0)])
                for hf in range(2):
                    fns = [(lambda e, k=k, hf=hf, sl=sl: e.transpose(pst[hf][:, k, :], h1b[sl][:, (hf * 8 + k) * 128:(hf * 8 + k + 1) * 128], IDB))
                           for k in range(8)]
                    mmgroup(fns, [("h1b", 0), ("cstb", B_ID)], [("pst", hf)])
                    cp("dve" if hf else "act", h1T[sl][:, hf * 8:hf * 8 + 8, :], pst[hf][:], [("pst", hf)], [("h1T", sl, hf)])

            o5_front(0)
            for tt_ in range(16):
                sl = tt_ % 2
                tsl = slice(tt_ * 128, (tt_ + 1) * 128)
                if tt_ + 1 < 16:
                    o5_front(tt_ + 1)
                for cg in range(4):
                    cs = slice(cg * 512, (cg + 1) * 512)
                    s_ = cg % 2
                    fns = [mm(pg[s_][:], h1T[sl][:, kc, :], wres[:, kc, cs], kc == 0, kc == 15) for kc in range(16)]
                    mmgroup(fns, [("h1T", sl, 0), ("h1T", sl, 1)] + [("wres", kc) for kc in range(16)], [("pg", s_)])
                    fns = [mm(pe2[s_][:], pT[:, kc, tsl], wpp[:, kc, cs], kc == 0, kc == 1) for kc in range(2)]
                    mmgroup(fns, [("pT", tt_), ("wpp", 0), ("wpp", 1)], [("pe2", s_)])
                    act(ytmp[s_][:], pg[s_][:], AF.Exp, [("pg", s_)], [("ytmp", s_)], scale=-1.0)
                    act(ytmp[s_][:], ytmp[s_][:], AF.Ln, [("ytmp", s_)], [("ytmp", s_)], bias=1.0)
                    act(ytmp[s_][:], ytmp[s_][:], AF.Exp, [("ytmp", s_)], [("ytmp", s_)], scale=-1.0)
                    tt("dve", ge[sl][:, cs], ytmp[s_][:], pe2[s_][:], OP.mult, [("ytmp", s_), ("pe2", s_)], [("ge", sl, cg)])
                rs, rsk = row_rstd([ge[sl][:, cg * 512:(cg + 1) * 512] for cg in range(4)], [("ge", sl, cg) for cg in range(4)],
                                   ss, sl, junk)
                stt(ge[sl][:], ge[sl][:], rs, gvec[:], OP.mult, OP.mult, [("ge", sl, cg) for cg in range(4)] + [rsk, "gvec"],
                    [("ge", sl, cg) for cg in range(4)])
                tt("pool", ge[sl][:], ge[sl][:], xt[tt_ % 3][:], OP.add, [("xt", tt_ % 3)] + [("ge", sl, cg) for cg in range(4)],
                   [("ge", sl, cg) for cg in range(4)])
                S.add("pool", lambda e, tsl=tsl, sl=sl: e.dma_start(out=out_d[tsl, :], in_=ge[sl][:]),
                      [("ge", sl, cg) for cg in range(4)], [("out", tt_)], dma=("outw", sl))
            S.add("sp", lambda e: e.nop(), [("out", t_) for t_ in range(16)], ["fin"])
            S.emit()
    return nc


def _consts():
    c = np.zeros((128, NCST), np.float32)
    p = np.arange(128)
    c[:, C_ID:C_ID + 128] = np.eye(128)
    for m in range(128):
        e = m % 64
        if e < 8:
            c[m + 8, C_RT + m] = 1.0
        elif e < 16:
            c[m - 8, C_RT + m] = 1.0
    i = p[:, None]
    j = np.arange(128)[None, :]
    m1 = (i >= j).astype(np.float32)
    m2 = (i <= j).astype(np.float32)
    c[:, C_M12:C_M12 + 512] = np.concatenate([m1, m2, m1, m2], axis=1)
    same = (i // 64) == (j // 64)
    c[:, C_HMF:C_HMF + 128] = (same & (i <= j)).astype(np.float32)
    c[:, C_HMB:C_HMB + 128] = (same & (i >= j)).astype(np.float32)
    t = np.arange(512)
    c[:, C_RST:C_RST + 512] = (t % 64 != 0).astype(np.float32)[None, :]
    inv = np.power(np.float32(500000.0), -np.arange(0, 16, 2, dtype=np.float32) / np.float32(16)).astype(np.float32)
    e = p % 64
    c[:, C_INV] = np.where(e < 16, inv[e % 8] / np.float32(2 * np.pi), 0.0)
    c[:, C_SGN] = np.where(e < 8, -1.0, np.where(e < 16, 1.0, 0.0))
    c[:, C_ONE:C_ONE + 128] = 1.0
    return c


_NC_CACHE = {}


def kernel(x, p, positions, w_in, w_out, g_pre, g_post, g_hg, lb_fwd, lb_bwd, w_pg, w_pp, g_ple, _debug=False):
    x = np.asarray(x, np.float32)
    p = np.asarray(p, np.float32)
    positions = np.asarray(positions, np.int32)
    w_in0 = np.asarray(w_in, np.float32)[0]
    aq, ak, av, ag, hq, hff, hfb, hv, hg = [w_in0[:, s:s + 1024] for s in range(0, 9216, 1024)]

    def tile_w(cols):
        w = np.concatenate(cols, axis=1)
        return np.ascontiguousarray(w.reshape(16, 128, NB_W, 128).transpose(2, 1, 0, 3)).reshape(NB_W, 128, 2048)

    w_t = {1: tile_w([aq, ak, av, ag, hq, hff, hfb, hv, hg]), 0: tile_w([aq, ak, av, ag, hq, hfb, hff, hv, hg])}

    def lbt(a, b):
        f = lambda r: np.asarray(r, np.float32).reshape(2, 8, 128).transpose(2, 0, 1).reshape(128, 16)
        return np.ascontiguousarray(np.concatenate([f(a), f(b)], axis=1))

    lb_t = {1: lbt(lb_fwd, lb_bwd), 0: lbt(lb_bwd, lb_fwd)}
    rep = lambda v: np.ascontiguousarray(np.broadcast_to(np.asarray(v, np.float32).reshape(1, -1), (128, 2048)))
    common = {
        "w_out": np.ascontiguousarray(np.asarray(w_out, np.float32)[0]),
        "w_pg": np.ascontiguousarray(np.asarray(w_pg, np.float32)[0]),
        "w_pp": np.ascontiguousarray(np.asarray(w_pp, np.float32)[0]),
        "g_pre": rep(g_pre), "g_post": rep(g_post), "g_ple": rep(g_ple),
        "g_hg": np.ascontiguousarray(np.asarray(g_hg, np.float32).reshape(128, 1)),
        "cst": _consts(),
    }
    in_maps = []
    for c in range(8):
        b, hf = c // 2, c % 2
        if hf == 1:
            xl, pl, posl = x[b], p[0, b, 2048:], positions[b]
        else:
            xl, pl, posl = x[b, ::-1], p[0, b, 2047::-1], positions[b, ::-1]
        m = dict(common)
        m["x"] = np.ascontiguousarray(xl)
        m["p"] = np.ascontiguousarray(pl)
        m["pos"] = np.ascontiguousarray(np.broadcast_to(posl[1024:].reshape(1, 3072), (128, 3072))).astype(np.int32)
        m["w_in"] = w_t[hf]
        m["lb"] = lb_t[hf]
        in_maps.append(m)
    key = bool(_debug)
    if key not in _NC_CACHE:
        _NC_CACHE[key] = build(debug=key)
    nc = _NC_CACHE[key]
    res = run_bass_kernel_spmd(nc, in_maps, core_ids=list(range(8)))
    out = np.empty((4, 4096, 2048), np.float32)
    for c in range(8):
        b, hf = c // 2, c % 2
        o = np.asarray(res.results[c]["out"], np.float32)
        if hf == 1:
            out[b, 2048:] = o
        else:
            out[b, 2047::-1] = o
    if _debug:
        return out, [res.results[c]["dbg"] for c in range(8)]
    return out
```
